# Optimizing a Trainium2 kernel written in Bass

```python
import math
import jax, jax.numpy as jnp
from jax import lax
import numpy as np

D_MODEL = 2048
BATCH = 1
SEQ = 8192
DEPTH = 2

D_MIX = D_MODEL
D_FF = 4 * D_MODEL
CONV_W = 4
CHUNK = 64
EPS = 1e-6
ROPE_BASE = 10000.0
RWKV_LN_EPS = 64e-5

GDN_HEADS = 4
GDN_DK = D_MIX // 16
GDN_DV = D_MIX // 16
GDN_QK = GDN_HEADS * GDN_DK
GDN_V = GDN_HEADS * GDN_DV
GDN_PW = 2 * GDN_QK + 2 * GDN_V + 2 * GDN_HEADS
RET_HEADS = 4
RET_DV = D_MIX // 16
RET_DK = RET_DV // 2
RET_QK = RET_HEADS * RET_DK
RET_V = RET_HEADS * RET_DV
RET_PW = 2 * RET_QK + 2 * RET_V
M2_HEADS = 8
M2_HEADDIM = D_MIX // 32
M2_GROUPS = 2
M2_STATE = 128
M2_W = M2_HEADS * M2_HEADDIM
M2_BC = M2_GROUPS * M2_STATE
M2_PW = 2 * M2_W + 2 * M2_BC + M2_HEADS
RW_HEADS = 8
RW_N = D_MIX // 32
RW_W = RW_HEADS * RW_N
RW_W_LORA = 32
RW_A_LORA = 32
RW_G_LORA = 96
RW_PW = 3 * RW_W + RW_W_LORA + RW_A_LORA + RW_G_LORA

P_TOTAL = GDN_PW + RET_PW + M2_PW + RW_PW

kernel_name = 'hybrid_parallel_heads_block'


def _offsets(widths):
    return [int(o) for o in np.cumsum(widths)]


def _rmsnorm(x, w, eps=EPS):
    xf = x.astype(jnp.float32)
    y = xf * lax.rsqrt(jnp.mean(xf * xf, axis=-1, keepdims=True) + eps)
    return (y * w.astype(jnp.float32)).astype(x.dtype)


def _layernorm(x, w, eps):
    mu = jnp.mean(x, axis=-1, keepdims=True)
    xc = x - mu
    return xc * lax.rsqrt(jnp.mean(xc * xc, axis=-1, keepdims=True) + eps) * w


def _l2norm(x):
    return x * lax.rsqrt(jnp.sum(x * x, axis=-1, keepdims=True) + EPS)


def _causal_dwconv(x, w):
    T = x.shape[1]
    xp = jnp.pad(x, ((0, 0), (CONV_W - 1, 0), (0, 0)))
    return sum(xp[:, i:i + T] * w[i] for i in range(CONV_W))


def _to_chunks(t):
    Bsz, T = t.shape[:2]
    t = t.reshape(Bsz, T // CHUNK, CHUNK, *t.shape[2:])
    return jnp.moveaxis(t, 3, 1)


def _from_chunks(t):
    t = jnp.moveaxis(t, 1, 3)
    return t.reshape(t.shape[0], t.shape[1] * t.shape[2], *t.shape[3:])


def _rotary(x):
    T, D = x.shape[1], x.shape[-1]
    theta = 1.0 / (ROPE_BASE ** jnp.linspace(0.0, 1.0, D // 2, dtype=jnp.float32))
    ang = jnp.arange(T, dtype=jnp.float32)[:, None] * theta
    cos = jnp.cos(ang)[None, :, None, :]
    sin = jnp.sin(ang)[None, :, None, :]
    xp = x.reshape(*x.shape[:-1], D // 2, 2)
    x0, x1 = xp[..., 0], xp[..., 1]
    return jnp.stack([x0 * cos - x1 * sin, x1 * cos + x0 * sin], axis=-1).reshape(x.shape)


def _gated_deltanet(q, k, v, beta, log_a):
    Bsz, T, H, DK = q.shape
    DV = v.shape[-1]
    qc, kc, vc = _to_chunks(q), _to_chunks(k), _to_chunks(v)
    bc = _to_chunks(beta)
    g = jnp.cumsum(_to_chunks(log_a), axis=-1)
    idx = jnp.arange(CHUNK)
    causal = idx[:, None] >= idx[None, :]
    strict = idx[:, None] > idx[None, :]
    gamma = jnp.exp(jnp.where(causal, g[..., :, None] - g[..., None, :], -jnp.inf))
    kkt = jnp.einsum('bhncd,bhnmd->bhncm', kc, kc)
    lower = jnp.where(strict, kkt * gamma * bc[..., :, None], 0.0)
    m = lower + jnp.eye(CHUNK, dtype=lower.dtype)
    rhs = jnp.concatenate([vc * bc[..., None], kc * (bc * jnp.exp(g))[..., None]], axis=-1)
    sol = lax.linalg.triangular_solve(m, rhs, left_side=True, lower=True, unit_diagonal=True)
    u, w = sol[..., :DV], sol[..., DV:]
    qk = jnp.einsum('bhncd,bhnmd->bhncm', qc, kc) * gamma
    q_dec = qc * jnp.exp(g)[..., None]
    g_last = g[..., -1]
    k_tail = kc * jnp.exp(g_last[..., None] - g)[..., None]

    def step(S, inp):
        w_i, u_i, qk_i, qd_i, kt_i, gl_i = inp
        v_new = u_i - jnp.einsum('bhcd,bhde->bhce', w_i, S)
        o = jnp.einsum('bhcd,bhde->bhce', qd_i, S) + jnp.einsum('bhcm,bhme->bhce', qk_i, v_new)
        S = S * jnp.exp(gl_i)[..., None, None] + jnp.einsum('bhcd,bhce->bhde', kt_i, v_new)
        return S, o

    xs = tuple(jnp.moveaxis(t, 2, 0) for t in (w, u, qk, q_dec, k_tail, g_last))
    S0 = jnp.zeros((Bsz, H, DK, DV), jnp.float32)
    _, o = lax.scan(step, S0, xs)
    return _from_chunks(jnp.moveaxis(o, 0, 2))


def _retention(q, k, v):
    Bsz, T, H, DK = q.shape
    DV = v.shape[-1]
    lg = jnp.log1p(-jnp.exp2(-5.0 - jnp.arange(H, dtype=jnp.float32)))
    pos = jnp.arange(CHUNK, dtype=jnp.float32)
    causal = pos[:, None] >= pos[None, :]
    dmask = jnp.exp(jnp.where(causal, (pos[:, None] - pos[None, :]) * lg[:, None, None], -jnp.inf))
    qc, kc, vc = _to_chunks(q), _to_chunks(k), _to_chunks(v)
    scores = jnp.einsum('bhncd,bhnmd->bhncm', qc, kc) * dmask[:, None]
    o = jnp.einsum('bhncm,bhnme->bhnce', scores, vc)
    k_dec = jnp.exp((CHUNK - 1 - pos) * lg[:, None])
    states = jnp.einsum('bhncd,bhnce->bhnde', kc * k_dec[:, None, :, None], vc)
    chunk_decay = jnp.exp(CHUNK * lg)

    def step(R, s):
        return R * chunk_decay[:, None, None] + s, R

    _, r_prev = lax.scan(step, jnp.zeros((Bsz, H, DK, DV), jnp.float32), jnp.moveaxis(states, 2, 0))
    r_prev = jnp.moveaxis(r_prev, 0, 2)
    q_dec = jnp.exp((pos + 1.0) * lg[:, None])
    o = o + jnp.einsum('bhncd,bhnde->bhnce', qc * q_dec[:, None, :, None], r_prev)
    return _from_chunks(o)


def _ssd(x, dt, A, Bm, Cm):
    Bsz, T, H, P = x.shape
    G, N = Bm.shape[2], Bm.shape[3]
    Hg = H // G
    nc = T // CHUNK

    def grp(t):
        t = t.reshape(Bsz, nc, CHUNK, G, Hg, *t.shape[3:])
        return jnp.moveaxis(t, (3, 4), (1, 2))

    xc, dtc = grp(x), grp(dt)
    g = jnp.cumsum(grp(dt * A), axis=-1)
    Bc = jnp.moveaxis(Bm.reshape(Bsz, nc, CHUNK, G, N), 3, 1)
    Cc = jnp.moveaxis(Cm.reshape(Bsz, nc, CHUNK, G, N), 3, 1)
    idx = jnp.arange(CHUNK)
    causal = idx[:, None] >= idx[None, :]
    L = jnp.exp(jnp.where(causal, g[..., :, None] - g[..., None, :], -jnp.inf))
    cb = jnp.einsum('bgncs,bgnms->bgncm', Cc, Bc)
    y = jnp.einsum('bghncm,bghnmp->bghncp', cb[:, :, None] * L * dtc[..., None, :], xc)
    g_last = g[..., -1]
    states = jnp.einsum('bgncs,bghncp->bghnsp', Bc, xc * (jnp.exp(g_last[..., None] - g) * dtc)[..., None])

    def step(hs, inp):
        s, gl = inp
        return hs * jnp.exp(gl)[..., None, None] + s, hs

    _, h_prev = lax.scan(step, jnp.zeros((Bsz, G, Hg, N, P), jnp.float32),
                         (jnp.moveaxis(states, 3, 0), jnp.moveaxis(g_last, 3, 0)))
    h_prev = jnp.moveaxis(h_prev, 0, 3)
    y = y + jnp.einsum('bgncs,bghnsp->bghncp', Cc, h_prev) * jnp.exp(g)[..., None]
    return jnp.moveaxis(y, (1, 2), (3, 4)).reshape(Bsz, T, H, P)


def _rwkv7_scan(r, decay, k, v, kk, a):
    Bsz, T, H, N = r.shape

    def step(S, inp):
        r_t, d_t, k_t, v_t, kk_t, a_t = inp
        sa = jnp.einsum('bhvk,bhk->bhv', S, -kk_t)
        S = S * d_t[:, :, None, :] + sa[..., None] * (kk_t * a_t)[:, :, None, :] + v_t[..., None] * k_t[:, :, None, :]
        return S, jnp.einsum('bhvk,bhk->bhv', S, r_t)

    xs = tuple(jnp.moveaxis(t, 1, 0) for t in (r, decay, k, v, kk, a))
    _, y = lax.scan(step, jnp.zeros((Bsz, H, N, N), jnp.float32), xs)
    return jnp.moveaxis(y, 0, 1)


def _token_mix(h, w_in, gdn_conv_w, gdn_a_log, gdn_dt_bias, gdn_norm_w, ret_norm_w,
               m2_conv_w, m2_conv_b, m2_a_log, m2_dt_bias, m2_d, m2_norm_w,
               rw_mu, rw_w0, rw_w_up, rw_a0, rw_a_up, rw_g_up, rw_k_k, rw_k_a, rw_r_k,
               rw_ln_w, rw_ln_b, w_out):
    f32 = jnp.float32
    Bsz, T, _ = h.shape
    proj = jnp.einsum('btd,dp->btp', h, w_in).astype(f32)
    p_gdn, p_ret, p_m2, p_rw = jnp.split(proj, _offsets((GDN_PW, RET_PW, M2_PW)), axis=-1)

    gqkv, gz, gb, ga = jnp.split(p_gdn, _offsets((2 * GDN_QK + GDN_V, GDN_V, GDN_HEADS)), axis=-1)
    gqkv = jax.nn.silu(_causal_dwconv(gqkv, gdn_conv_w.astype(f32)))
    gq, gk, gv = jnp.split(gqkv, _offsets((GDN_QK, GDN_QK)), axis=-1)
    q = _l2norm(gq.reshape(Bsz, T, GDN_HEADS, GDN_DK)) * (GDN_DK ** -0.5)
    k = _l2norm(gk.reshape(Bsz, T, GDN_HEADS, GDN_DK))
    v = gv.reshape(Bsz, T, GDN_HEADS, GDN_DV)
    beta = jax.nn.sigmoid(gb)
    log_a = -jnp.exp(gdn_a_log.astype(f32)) * jax.nn.softplus(ga + gdn_dt_bias.astype(f32))
    o_a = _gated_deltanet(q, k, v, beta, log_a)
    o_a = _rmsnorm(o_a, gdn_norm_w) * jax.nn.silu(gz.reshape(Bsz, T, GDN_HEADS, GDN_DV))
    o_a = o_a.reshape(Bsz, T, GDN_V)

    rq, rk, rv, rg = jnp.split(p_ret, _offsets((RET_QK, RET_QK, RET_V)), axis=-1)
    q = _rotary(rq.reshape(Bsz, T, RET_HEADS, RET_DK))
    k = _rotary(rk.reshape(Bsz, T, RET_HEADS, RET_DK)) * (RET_DK ** -0.5)
    v = rv.reshape(Bsz, T, RET_HEADS, RET_DV)
    o_b = _retention(q, k, v)
    o_b = _layernorm(o_b, ret_norm_w.astype(f32).reshape(RET_HEADS, RET_DV), EPS)
    o_b = o_b.reshape(Bsz, T, RET_V) * jax.nn.silu(rg)

    mz, mxbc, mdt = jnp.split(p_m2, _offsets((M2_W, M2_W + 2 * M2_BC)), axis=-1)
    mxbc = jax.nn.silu(_causal_dwconv(mxbc, m2_conv_w.astype(f32)) + m2_conv_b.astype(f32))
    mx, mB, mC = jnp.split(mxbc, _offsets((M2_W, M2_BC)), axis=-1)
    xh = mx.reshape(Bsz, T, M2_HEADS, M2_HEADDIM)
    dt = jax.nn.softplus(mdt + m2_dt_bias.astype(f32))
    A = -jnp.exp(m2_a_log.astype(f32))
    y = _ssd(xh, dt, A, mB.reshape(Bsz, T, M2_GROUPS, M2_STATE), mC.reshape(Bsz, T, M2_GROUPS, M2_STATE))
    y = y + m2_d.astype(f32)[:, None] * xh
    y = y.reshape(Bsz, T, M2_W) * jax.nn.silu(mz)
    o_c = _rmsnorm(y.reshape(Bsz, T, M2_GROUPS, M2_W // M2_GROUPS),
                   m2_norm_w.reshape(M2_GROUPS, M2_W // M2_GROUPS)).reshape(Bsz, T, M2_W)

    p_prev = jnp.pad(p_rw, ((0, 0), (1, 0), (0, 0)))[:, :-1]
    mixed = p_rw + (p_prev - p_rw) * rw_mu.astype(f32)
    wr, wk, wv, wl, al, gl = jnp.split(mixed, _offsets((RW_W, RW_W, RW_W, RW_W_LORA, RW_A_LORA)), axis=-1)
    w_raw = -jax.nn.softplus(-(rw_w0.astype(f32) + jnp.tanh(wl) @ rw_w_up.astype(f32))) - 0.5
    decay = jnp.exp(-jnp.exp(w_raw))
    a = jax.nn.sigmoid(rw_a0.astype(f32) + al @ rw_a_up.astype(f32))
    g = jax.nn.sigmoid(gl) @ rw_g_up.astype(f32)
    hd = lambda t: t.reshape(Bsz, T, RW_HEADS, RW_N)
    r, k, v, a, decay = hd(wr), hd(wk), hd(wv), hd(a), hd(decay)
    kk = _l2norm(k * rw_k_k.astype(f32).reshape(RW_HEADS, RW_N))
    k = k * (1.0 + (a - 1.0) * rw_k_a.astype(f32).reshape(RW_HEADS, RW_N))
    y = _rwkv7_scan(r, decay, k, v, kk, a)
    y = _layernorm(y, rw_ln_w.astype(f32).reshape(RW_HEADS, RW_N), RWKV_LN_EPS) + rw_ln_b.astype(f32).reshape(RW_HEADS, RW_N)
    y = y + jnp.sum(r * k * rw_r_k.astype(f32), axis=-1, keepdims=True) * v
    o_d = y.reshape(Bsz, T, RW_W) * g

    o = jnp.concatenate([o_a, o_b, o_c, o_d], axis=-1)
    return jnp.einsum('btm,md->btd', o, w_out.astype(f32)).astype(h.dtype)


def _sq_relu_mlp(h, w_up, w_down):
    return jnp.square(jax.nn.relu(h @ w_up)) @ w_down


def setup_inputs(seed: int = 0) -> dict:
    key = jax.random.key(seed)
    ks = iter(jax.random.split(key, 32))
    L = DEPTH

    def nrm(shape, scale):
        return jax.random.normal(next(ks), shape, jnp.float32) * scale

    def unif(shape, lo, hi):
        return jax.random.uniform(next(ks), shape, jnp.float32, lo, hi)

    def dt_bias(shape):
        dt = jnp.exp(unif(shape, math.log(1e-3), math.log(1e-1)))
        return dt + jnp.log(-jnp.expm1(-dt))

    return {
        'x': nrm((BATCH, SEQ, D_MODEL), 1.0),
        'norm1_w': 1.0 + nrm((L, D_MODEL), 0.02),
        'w_in': nrm((L, D_MODEL, P_TOTAL), D_MODEL ** -0.5),
        'gdn_conv_w': nrm((L, CONV_W, 2 * GDN_QK + GDN_V), CONV_W ** -0.5),
        'gdn_a_log': jnp.log(unif((L, GDN_HEADS), 1.0, 16.0)),
        'gdn_dt_bias': dt_bias((L, GDN_HEADS)),
        'gdn_norm_w': 1.0 + nrm((L, GDN_DV), 0.02),
        'ret_norm_w': 1.0 + nrm((L, RET_V), 0.02),
        'm2_conv_w': nrm((L, CONV_W, M2_W + 2 * M2_BC), CONV_W ** -0.5),
        'm2_conv_b': nrm((L, M2_W + 2 * M2_BC), 0.1),
        'm2_a_log': jnp.log(unif((L, M2_HEADS), 1.0, 16.0)),
        'm2_dt_bias': dt_bias((L, M2_HEADS)),
        'm2_d': 1.0 + nrm((L, M2_HEADS), 0.1),
        'm2_norm_w': 1.0 + nrm((L, M2_W), 0.02),
        'rw_mu': unif((L, RW_PW), 0.0, 1.0),
        'rw_w0': unif((L, RW_W), -5.0, 1.0),
        'rw_w_up': nrm((L, RW_W_LORA, RW_W), 0.1),
        'rw_a0': nrm((L, RW_W), 0.1),
        'rw_a_up': nrm((L, RW_A_LORA, RW_W), 0.1),
        'rw_g_up': nrm((L, RW_G_LORA, RW_W), RW_G_LORA ** -0.5),
        'rw_k_k': 0.85 + nrm((L, RW_W), 0.05),
        'rw_k_a': 1.0 + nrm((L, RW_W), 0.05),
        'rw_r_k': nrm((L, RW_HEADS, RW_N), 0.1),
        'rw_ln_w': 1.0 + nrm((L, RW_W), 0.02),
        'rw_ln_b': nrm((L, RW_W), 0.02),
        'w_out': nrm((L, D_MIX, D_MODEL), D_MIX ** -0.5),
        'norm2_w': 1.0 + nrm((L, D_MODEL), 0.02),
        'w_ffn_up': nrm((L, D_MODEL, D_FF), D_MODEL ** -0.5),
        'w_ffn_down': nrm((L, D_FF, D_MODEL), D_FF ** -0.5),
        'final_norm_w': 1.0 + nrm((D_MODEL,), 0.02),
    }


def reference(x, norm1_w, w_in, gdn_conv_w, gdn_a_log, gdn_dt_bias, gdn_norm_w, ret_norm_w,
              m2_conv_w, m2_conv_b, m2_a_log, m2_dt_bias, m2_d, m2_norm_w,
              rw_mu, rw_w0, rw_w_up, rw_a0, rw_a_up, rw_g_up, rw_k_k, rw_k_a, rw_r_k,
              rw_ln_w, rw_ln_b, w_out, norm2_w, w_ffn_up, w_ffn_down, final_norm_w):
    h = x
    for l in range(DEPTH):
        a = _rmsnorm(h, norm1_w[l])
        h = h + _token_mix(a, w_in[l], gdn_conv_w[l], gdn_a_log[l], gdn_dt_bias[l], gdn_norm_w[l], ret_norm_w[l],
                           m2_conv_w[l], m2_conv_b[l], m2_a_log[l], m2_dt_bias[l], m2_d[l], m2_norm_w[l],
                           rw_mu[l], rw_w0[l], rw_w_up[l], rw_a0[l], rw_a_up[l], rw_g_up[l], rw_k_k[l], rw_k_a[l],
                           rw_r_k[l], rw_ln_w[l], rw_ln_b[l], w_out[l])
        m = _rmsnorm(h, norm2_w[l])
        h = h + _sq_relu_mlp(m, w_ffn_up[l], w_ffn_down[l])
    return _rmsnorm(h, final_norm_w)
```

```python
import math
import numpy as np
from contextlib import ExitStack
import concourse.bass as bass
import concourse.mybir as mybir
from concourse.bass_utils import run_bass_kernel_spmd

F32 = mybir.dt.float32
BF16 = mybir.dt.bfloat16
ALU = mybir.AluOpType
AF = mybir.ActivationFunctionType

ENGS = ("pe", "act", "dve", "pool", "sp")
SEM_ROT = 30000

D_MODEL = 2048
SEQ = 8192
NCORE = 8
EPS = 1e-6
NB = 512
CH = 64
NCH = NB // CH
KT = D_MODEL // 128
D_FF = 4 * D_MODEL


class Buf:
    __slots__ = ("name", "t", "lw", "rd", "dsem", "dcnt")

    def __init__(self, name, t):
        self.name = name
        self.t = t
        self.lw = None
        self.rd = []
        self.dsem = None
        self.dcnt = 0

    def __getitem__(self, idx):
        return self.t[idx]


class Prog:
    def __init__(self, nc):
        self.nc = nc
        self.es = ExitStack()
        self.ops = {e: [] for e in ENGS}
        self.cnt = {e: 0 for e in ENGS}
        self.gen = {e: 0 for e in ENGS}
        self.known = {e: {} for e in ENGS}
        self.sems = {}
        self.nbuf = 0
        self.npsum = 0
        self.psums = []

    def sem(self, key):
        if key not in self.sems:
            nm = "s_" + "_".join(str(k) for k in key)
            self.sems[key] = self.es.enter_context(self.nc.semaphore(nm))
        return self.sems[key]

    def sbuf(self, name, shape, dtype=F32):
        t = self.es.enter_context(self.nc.sbuf_tensor(name, list(shape), dtype))
        return Buf(name, t)

    def psum_banks(self, n=8):
        for i in range(n):
            t = self.es.enter_context(self.nc.psum_tensor("psb%d" % i, [128, 512], F32))
            self.psums.append(Buf("psb%d" % i, t))

    def ps(self):
        b = self.psums[self.npsum % len(self.psums)]
        self.npsum += 1
        return b

    def dram(self, name, shape, dtype=F32, kind=None):
        if kind is None:
            t = self.nc.dram_tensor(name, list(shape), dtype)
        else:
            t = self.nc.dram_tensor(name, list(shape), dtype, kind=kind)
        return Buf(name, t.ap())

    def _deps(self, eng, reads, writes):
        evs = []
        for b in reads:
            if b.lw is not None:
                evs.append(b.lw)
        for b in writes:
            if b.lw is not None:
                evs.append(b.lw)
            evs.extend(b.rd)
        need = {}
        kn = self.known[eng]
        for (k, v) in evs:
            if kn.get(k, 0) >= v:
                continue
            if need.get(k, 0) < v:
                need[k] = v
        for k, v in need.items():
            kn[k] = v
        return list(need.items())

    def _record(self, ev, reads, writes):
        for b in writes:
            b.lw = ev
            b.rd = []
        for b in reads:
            if b in writes:
                continue
            b.rd.append(ev)
            if len(b.rd) > 24:
                m = {}
                for (k, v) in b.rd:
                    if m.get(k, 0) < v:
                        m[k] = v
                b.rd = list(m.items())

    def op(self, eng, fn, reads=(), writes=()):
        reads = [b for b in reads if b is not None]
        writes = [b for b in writes if b is not None]
        waits = self._deps(eng, reads, writes)
        if self.cnt[eng] >= SEM_ROT:
            self.gen[eng] += 1
            self.cnt[eng] = 0
        self.cnt[eng] += 1
        key = (eng, self.gen[eng])
        ev = (key, self.cnt[eng])
        self.ops[eng].append((waits, fn, key, 1))
        self._record(ev, reads, writes)
        return ev

    def dma(self, eng, out_buf, out_ap, in_buf, in_ap, sem_buf=None, **kw):
        sb = sem_buf
        if sb is None:
            sb = in_buf if (out_buf.name.startswith("dr_")) else out_buf
        if sb.dsem is None:
            self.nbuf += 1
            sb.dsem = ("d", self.nbuf)
        waits = self._deps(eng, [in_buf], [out_buf])
        sb.dcnt += 16
        ev = (sb.dsem, sb.dcnt)

        def fn(e, out_ap=out_ap, in_ap=in_ap, kw=kw):
            return e.dma_start(out=out_ap, in_=in_ap, **kw)
        self.ops[eng].append((waits, fn, sb.dsem, 16))
        self._record(ev, [in_buf], [out_buf])
        return ev

    def wait_all(self, eng, events):
        need = {}
        for (k, v) in events:
            if need.get(k, 0) < v:
                need[k] = v
        self.ops[eng].append((list(need.items()), None, None, 0))

    def emit(self):
        nc = self.nc
        for e in ENGS:
            for (waits, fn, key, inc) in self.ops[e]:
                for (k, v) in waits:
                    self.sem(k)
                if key is not None:
                    self.sem(key)
        with nc.Block() as block:
            def replay(ename, eng):
                for (waits, fn, key, inc) in self.ops[ename]:
                    for (k, v) in waits:
                        eng.wait_ge(self.sems[k], v)
                    if fn is not None:
                        fn(eng).then_inc(self.sems[key], inc)

            @block.tensor
            def _(eng):
                replay("pe", eng)

            @block.scalar
            def _(eng):
                replay("act", eng)

            @block.vector
            def _(eng):
                replay("dve", eng)

            @block.gpsimd
            def _(eng):
                replay("pool", eng)

            @block.sync
            def _(eng):
                replay("sp", eng)
        self.es.close()


class K:
    def __init__(self, P):
        self.P = P
        self.flip = 0

    def mm(self, ob, oap, lb, lap, rb, rap, start=True, stop=True):
        self.P.op("pe", lambda e: e.matmul(oap, lhsT=lap, rhs=rap, start=start, stop=stop),
                  reads=[lb, rb], writes=[ob])

    def tt(self, ob, oap, ab, aap, bb, bap, op, eng="dve"):
        self.P.op(eng, lambda e: e.tensor_tensor(oap, aap, bap, op), reads=[ab, bb], writes=[ob])

    def ts(self, ob, oap, ab, aap, s1, s2, op0, op1=None, eng="dve", sb=()):
        if op1 is None:
            self.P.op(eng, lambda e: e.tensor_scalar(oap, aap, s1, None, op0), reads=[ab] + list(sb), writes=[ob])
        else:
            self.P.op(eng, lambda e: e.tensor_scalar(oap, aap, s1, s2, op0, op1), reads=[ab] + list(sb), writes=[ob])

    def stt(self, ob, oap, ab, aap, sc, bb, bap, op0, op1, eng="dve", sb=()):
        self.P.op(eng, lambda e: e.scalar_tensor_tensor(oap, aap, sc, bap, op0, op1),
                  reads=[ab, bb] + list(sb), writes=[ob])

    def act(self, ob, oap, ab, aap, func, bias=None, scale=None, sb=()):
        kw = {}
        if bias is not None:
            kw["bias"] = bias
        if scale is not None:
            kw["scale"] = scale
        self.P.op("act", lambda e: e.activation(oap, aap, func, **kw), reads=[ab] + list(sb), writes=[ob])

    def copy(self, ob, oap, ab, aap, eng=None):
        if eng is None:
            self.flip ^= 1
            eng = "act" if self.flip else "dve"
        if eng == "act":
            self.P.op("act", lambda e: e.copy(oap, aap), reads=[ab], writes=[ob])
        else:
            self.P.op(eng, lambda e: e.tensor_copy(oap, aap), reads=[ab], writes=[ob])

    def memset(self, ob, oap, val, eng="pool"):
        self.P.op(eng, lambda e: e.memset(oap, val), writes=[ob])


def c3(ap, j=CH):
    return ap.rearrange("p (c j) -> p c j", j=j)


class Core:
    def __init__(self, P, kk, cst):
        self.P = P
        self.k = kk
        self.cst = cst
        S = P.sbuf
        self.vtok = S("c_vtok", [64, NB])
        self.kT = S("c_kT", [64, NCH * 128])
        self.bT = S("c_bT", [64, NCH * 128])
        self.sc = [S("c_sc%d" % i, [64, NB]) for i in range(3)]
        self.xy = [S("c_xy%d" % i, [64, NB]) for i in range(4)]
        self.Ru = S("c_Ru", [64, NB])
        self.RU = S("c_RU", [64, NCH * 128])
        self.MT = S("c_MT", [128, NCH * 128])
        self.Qf = S("c_Qf", [128, NB])

    def transp(self, dst, src, rows):
        P, k = self.P, self.k
        ident = self.cst["ident"]
        per = 512 // rows
        for c0 in range(0, NCH, per):
            pb = P.ps()
            for c in range(c0, c0 + per):
                k.mm(pb, pb[0:64, (c - c0) * rows:(c - c0 + 1) * rows], src, src[0:rows, c * CH:(c + 1) * CH],
                     ident, ident[0:rows, 0:rows])
            k.copy(dst, dst[0:64, c0 * rows:(c0 + per) * rows], pb, pb[0:64, 0:per * rows])

    def chunk_mm(self, width, terms):
        P, k = self.P, self.k
        per = 512 // width
        outs = []
        for c0 in range(0, NCH, per):
            pb = P.ps()
            for c in range(c0, c0 + per):
                o = pb[:, (c - c0) * width:(c - c0 + 1) * width]
                for i, (lb, lf, rb, rf) in enumerate(terms):
                    lap = lf(c)
                    m = lap.shape[1]
                    k.mm(pb, pb[0:m, (c - c0) * width:(c - c0 + 1) * width], lb, lap, rb, rf(c),
                         start=(i == 0), stop=(i == len(terms) - 1))
            outs.append((pb, c0, per))
        return outs

    def block(self, dk, delta, Rs, K2s, vT, rd, kTf, gam, dec0T, S2, si, yb,
              KKs=None, Bs=None, kkd=None, bTf=None, dec1T=None, dec1=None):
        P, k, cst = self.P, self.k, self.cst
        vtok, kT, bT, Ru, RU, MT, Qf = self.vtok, self.kT, self.bT, self.Ru, self.RU, self.MT, self.Qf
        ident = cst["ident"]
        cs = lambda c: slice(c * CH, (c + 1) * CH)
        ck = lambda c: slice(c * dk, (c + 1) * dk)
        self.transp(vtok, vT, 64)
        self.transp(kT, kTf, dk)
        RKT = self.sc[0]
        (pb, _, _), = self.chunk_mm(64, [(K2s, lambda c: K2s[0:dk, cs(c)], Rs, lambda c: Rs[0:dk, cs(c)])])
        k.tt(RKT, RKT[0:64, :], pb, pb[0:64, :], dec0T, dec0T[0:64, :], ALU.mult)
        if delta:
            self.transp(bT, bTf, dk)
            RBT, KKT = self.sc[1], self.sc[2]
            X, Y = self.xy[0], self.xy[1]
            (p1, _, _), = self.chunk_mm(64, [(Bs, lambda c: Bs[0:dk, cs(c)], Rs, lambda c: Rs[0:dk, cs(c)])])
            k.tt(RBT, RBT[0:64, :], p1, p1[0:64, :], dec0T, dec0T[0:64, :], ALU.mult)
            (p2, _, _), = self.chunk_mm(64, [(K2s, lambda c: K2s[0:dk, cs(c)], KKs, lambda c: KKs[0:dk, cs(c)])])
            k.tt(KKT, KKT[0:64, :], p2, p2[0:64, :], dec1T, dec1T[0:64, :], ALU.mult)
            (p3, _, _), = self.chunk_mm(64, [(Bs, lambda c: Bs[0:dk, cs(c)], KKs, lambda c: KKs[0:dk, cs(c)])])
            k.stt(Y, Y[0:64, :], p3, p3[0:64, :], -1.0, dec1T, dec1T[0:64, :], ALU.mult, ALU.mult)
            (p4, _, _), = self.chunk_mm(64, [(KKs, lambda c: KKs[0:dk, cs(c)], Bs, lambda c: Bs[0:dk, cs(c)])])
            k.stt(X, X[0:64, :], p4, p4[0:64, :], -1.0, dec1, dec1[0:64, :], ALU.mult, ALU.mult)
            (p5, _, _), = self.chunk_mm(64, [(KKT, lambda c: KKT[0:64, cs(c)], vtok, lambda c: vtok[0:64, cs(c)])])
            k.ts(Ru, Ru[0:64, :], p5, p5[0:64, :], -1.0, None, ALU.mult)
            self.transp(RU, kkd, dk)
            xi = 0
            for lvl in range(6):
                X, Y = self.xy[xi], self.xy[xi + 1]
                (pu, _, _), = self.chunk_mm(64, [(Y, lambda c: Y[0:64, cs(c)], Ru, lambda c: Ru[0:64, cs(c)])])
                outsU = self.chunk_mm(dk, [(Y, lambda c: Y[0:64, cs(c)], RU, lambda c: RU[0:64, ck(c)])])
                if lvl < 5:
                    Xn, Yn = self.xy[2 - xi], self.xy[3 - xi]
                    (px, _, _), = self.chunk_mm(64, [(Y, lambda c: Y[0:64, cs(c)], X, lambda c: X[0:64, cs(c)])])
                    (py, _, _), = self.chunk_mm(64, [(X, lambda c: X[0:64, cs(c)], Y, lambda c: Y[0:64, cs(c)])])
                k.tt(Ru, Ru[0:64, :], Ru, Ru[0:64, :], pu, pu[0:64, :], ALU.add)
                for (pbU, c0, n) in outsU:
                    k.tt(RU, RU[0:64, c0 * dk:(c0 + n) * dk], RU, RU[0:64, c0 * dk:(c0 + n) * dk],
                         pbU, pbU[0:64, 0:n * dk], ALU.add)
                if lvl < 5:
                    k.copy(Xn, Xn[0:64, :], px, px[0:64, :], eng="act")
                    k.copy(Yn, Yn[0:64, :], py, py[0:64, :], eng="act")
                    xi = 2 - xi
            outsM = self.chunk_mm(dk, [(RU, lambda c: RU[0:64, ck(c)], bT, lambda c: bT[0:64, ck(c)])])
            for (pbM, c0, n) in outsM:
                for c in range(c0, c0 + n):
                    k.stt(MT, MT[0:dk, ck(c)], ident, ident[0:dk, 0:dk], gam[0:dk, c * CH + 63:c * CH + 64],
                          pbM, pbM[0:dk, (c - c0) * dk:(c - c0 + 1) * dk], ALU.mult, ALU.subtract, sb=[gam])
            (pq, _, _), = self.chunk_mm(64, [(RU, lambda c: RU[0:64, ck(c)], RBT, lambda c: RBT[0:64, cs(c)])])
            k.tt(Qf, Qf[0:dk, :], rd, rd[0:dk, :], pq, pq[0:dk, :], ALU.subtract)
            Q = Qf
        else:
            Q = rd
        for c in range(NCH):
            S = S2[si]
            Sn = S2[1 - si]
            sl = cs(c)
            k.mm(yb, yb[0:64, sl], S, S[0:dk, 0:64], Q, Q[0:dk, sl], start=True, stop=False)
            k.mm(yb, yb[0:64, sl], vtok, vtok[0:64, sl], RKT, RKT[0:64, sl], start=False, stop=not delta)
            if delta:
                k.mm(yb, yb[0:64, sl], Ru, Ru[0:64, sl], RBT, RBT[0:64, sl], start=False, stop=True)
            pbs = P.ps()
            if delta:
                k.mm(pbs, pbs[0:dk, 0:64], MT, MT[0:dk, ck(c)], S, S[0:dk, 0:64], start=True, stop=False)
                k.mm(pbs, pbs[0:dk, 0:64], kT, kT[0:64, ck(c)], vtok, vtok[0:64, sl], start=False, stop=False)
                k.mm(pbs, pbs[0:dk, 0:64], bT, bT[0:64, ck(c)], Ru, Ru[0:64, sl], start=False, stop=True)
                k.copy(Sn, Sn[0:dk, 0:64], pbs, pbs[0:dk, 0:64], eng="dve")
            else:
                k.mm(pbs, pbs[0:dk, 0:64], kT, kT[0:64, ck(c)], vtok, vtok[0:64, sl], start=True, stop=True)
                k.stt(Sn, Sn[0:dk, 0:64], S, S[0:dk, 0:64], gam[0:dk, c * CH + 63:c * CH + 64],
                      pbs, pbs[0:dk, 0:64], ALU.mult, ALU.add, sb=[gam])
            si = 1 - si
        return si


QROWS = [("gq", 128), ("gk", 128), ("gv", 64), ("gb", 1), ("ga", 1),
         ("rq", 64), ("rk", 64), ("rv", 64),
         ("mx", 64), ("mB", 128), ("mC", 128), ("mdt", 1),
         ("wr", 64), ("wk", 64), ("wv", 64), ("wl", 32), ("al", 32), ("gl", 96)]
QOFF = {}
_o = 0
for _n, _r in QROWS:
    QOFF[_n] = (_o, _r)
    _o += _r
NR = _o
HIST = {"gq": 3, "gk": 3, "gv": 3, "mx": 3, "mB": 3, "mC": 3,
        "wr": 1, "wk": 1, "wv": 1, "wl": 1, "al": 1, "gl": 1}
CST = {}
_o = 0
for _n, _w in [("n1w", 16), ("gcq", 4), ("gck", 4), ("gcv", 4), ("g_alog", 1), ("g_dtb", 1),
               ("r_lg", 1), ("mcx", 4), ("mcB", 4), ("mcC", 4), ("mbx", 1), ("mbB", 1), ("mbC", 1),
               ("m_alog", 1), ("m_dtb", 1), ("m_d", 1),
               ("mu_r", 1), ("mu_k", 1), ("mu_v", 1), ("mu_wl", 1), ("mu_al", 1), ("mu_gl", 1),
               ("w0", 1), ("a0", 1), ("k_k", 1), ("k_a", 1), ("r_k", 1), ("ln_w", 1), ("ln_b", 1),
               ("w_up", 64), ("a_up", 64), ("g_up", 64)]:
    CST[_n] = (_o, _w)
    _o += _w
NCST = _o
MSK = {}
_o = 0
for _n, _w in [("ident", 128), ("U8", NB), ("SU8", NB), ("SL8", NB), ("reset", NB), ("rotJ", 64), ("ones", 128)]:
    MSK[_n] = (_o, _w)
    _o += _w
NMSK = _o


def build_phase_a(T, mixers=("gdn", "ret", "ssd", "rwkv")):
    nblk = T // NB
    nc = bass.Bass("TRN2", target_bir_lowering=False)
    P = Prog(nc)
    k = K(P)
    hT = P.dram("hT", [D_MODEL, T], F32, kind="ExternalInput")
    win = P.dram("win", [128, KT * NR], F32, kind="ExternalInput")
    cstd = P.dram("cst", [128, NCST], F32, kind="ExternalInput")
    mskd = P.dram("msk", [128, NMSK], F32, kind="ExternalInput")
    rotd = P.dram("rot", [64, 2 * T], F32, kind="ExternalInput")
    om = P.dram("dr_om", [4 * 64, T], F32, kind="ExternalOutput")
    P.psum_banks(6)
    ybank = [Buf("yb%d" % i, P.es.enter_context(nc.psum_tensor("yb%d" % i, [128, 512], F32))) for i in range(2)]

    S = P.sbuf
    wsb = S("wsb", [128, KT * NR], BF16)
    cst = S("cstsb", [128, NCST])
    msk = S("msksb", [128, NMSK])
    for kt0 in range(0, KT, 4):
        P.dma("pool", wsb, wsb[:, kt0 * NR:(kt0 + 4) * NR], win, win[:, kt0 * NR:(kt0 + 4) * NR])
    P.dma("sp", cst, cst[:, :], cstd, cstd[:, :])
    P.dma("sp", msk, msk[:, :], mskd, mskd[:, :])

    def C(name, rows=128, j0=0, j1=None):
        o, w = CST[name]
        if j1 is None:
            j1 = w
        return cst[0:rows, o + j0:o + j1]

    def M(name, rows=64):
        o, w = MSK[name]
        return msk[0:rows, o:o + w]

    class MV:
        def __init__(self, name):
            self.o, self.w = MSK[name]

        def __getitem__(self, idx):
            r, c = idx
            c0 = 0 if c.start is None else c.start
            c1 = self.w if c.stop is None else c.stop
            return msk[r, self.o + c0:self.o + c1]
    class View(Buf):
        pass

    def mview(name):
        v = MV(name)
        b = Buf.__new__(Buf)
        b.name = "msk_" + name
        b.t = v
        b.lw = None
        b.rd = []
        b.dsem = None
        b.dcnt = 0
        return b
    ident, U8, SU8, SL8 = mview("ident"), mview("U8"), mview("SU8"), mview("SL8")
    core = Core(P, k, {"ident": ident})
    ones_bf = S("ones_bf", [128, 128], BF16)
    k.memset(ones_bf, ones_bf[:, :], 1.0)
    zero_ev = None

    for b in (ident, U8, SU8, SL8):
        b.lw = msk.lw

    hbuf = [S("hbuf%d" % i, [128, NB]) for i in range(3)]
    hcnt = [0]
    abf = S("abf", [128, KT * NB], BF16)
    sqb = [S("sqb%d" % i, [128, NB], BF16) for i in range(2)]
    rstd = S("rstd", [128, NB])
    raw = {n: S("raw_" + n, [max(r, 1), HIST.get(n, 0) + NB]) for n, r in QROWS}
    for n, r in QROWS:
        if HIST.get(n, 0):
            k.memset(raw[n], raw[n][0:r, 0:HIST[n]], 0.0)
    ft = [S("ft%d" % i, [128, NB]) for i in range(14)]
    rows_t = [S("row%d" % i, [1, NB]) for i in range(5)]
    rep = [S("rep%d" % i, [128, NB]) for i in range(6)]
    dect = [S("dec%d" % i, [64, NB]) for i in range(3)]
    cols = S("cols", [64, 16])
    osb = S("osb", [64, NB])
    states = {m: [S("st_%s%d" % (m, i), [128, 64]) for i in range(2)] for m in mixers}
    sidx = {m: 0 for m in mixers}
    for m in mixers:
        k.memset(states[m][0], states[m][0][:, :], 0.0)
    negA = S("negA", [128, 4])
    k.act(negA, negA[:, 0:1], cst, C("g_alog"), AF.Exp)
    k.act(negA, negA[:, 1:2], cst, C("m_alog"), AF.Exp)
    k.ts(negA, negA[:, 0:2], negA, negA[:, 0:2], -1.0, None, ALU.mult)
    omu = S("omu", [128, 16])
    for j, mun in enumerate(("mu_r", "mu_k", "mu_v", "mu_wl", "mu_al", "mu_gl")):
        k.ts(omu, omu[:, j:j + 1], cst, C(mun), -1.0, 1.0, ALU.mult, ALU.add)
    k.ts(omu, omu[:, 6:7], cst, C("w0"), -1.0, None, ALU.mult)
    k.ts(omu, omu[:, 7:8], cst, C("k_a"), -1.0, 1.0, ALU.mult, ALU.add)
    k.memset(omu, omu[:, 8:9], -0.5, eng="dve")
    onesf = mview("ones")
    onesf.lw = msk.lw
    reset = mview("reset")
    reset.lw = msk.lw

    def rsqrt_to(ob, oap, ib, iap, eps, scale=1.0):
        k.act(ob, oap, ib, iap, AF.Ln, bias=eps, scale=scale)
        k.act(ob, oap, ob, oap, AF.Exp, scale=-0.5)

    def conv_silu(dst, src, rows, wname, bias=None):
        wo = CST[wname][0]
        k.ts(dst, dst[0:rows, :], src, src[0:rows, 0:NB], cst[0:rows, wo:wo + 1], None, ALU.mult, sb=[cst])
        for i in range(1, 4):
            k.stt(dst, dst[0:rows, :], src, src[0:rows, i:i + NB], cst[0:rows, wo + i:wo + i + 1],
                  dst, dst[0:rows, :], ALU.mult, ALU.add, sb=[cst])
        if bias is None:
            k.act(dst, dst[0:rows, :], dst, dst[0:rows, :], AF.Silu)
        else:
            k.act(dst, dst[0:rows, :], dst, dst[0:rows, :], AF.Silu, bias=C(bias, rows), sb=[cst])
        k.copy(src, src[0:rows, 0:3], src, src[0:rows, NB:NB + 3], eng="act")

    def replicate(dst, row, rows=128):
        pb = P.ps()
        k.mm(pb, pb[0:rows, :], onesf, onesf[0:1, 0:rows], row, row[0:1, :])
        k.copy(dst, dst[0:rows, :], pb, pb[0:rows, :])

    def to_cols(j, row):
        pb = P.ps()
        for c in range(NCH):
            k.mm(pb, pb[0:64, c:c + 1], row, row[0:1, c * CH:(c + 1) * CH], onesf, onesf[0:1, 0:1])
        k.copy(cols, cols[0:64, j * 8:(j + 1) * 8], pb, pb[0:64, 0:8])

    def l2norm(dst, src, rows, scale):
        sq = ft[13]
        k.tt(sq, sq[0:rows, :], src, src[0:rows, :], src, src[0:rows, :], ALU.mult)
        pb = P.ps()
        k.mm(pb, pb[0:rows, :], onesf, onesf[0:rows, 0:rows], sq, sq[0:rows, :])
        rsqrt_to(sq, sq[0:rows, :], pb, pb[0:rows, :], EPS)
        k.stt(dst, dst[0:rows, :], src, src[0:rows, :], scale, sq, sq[0:rows, :], ALU.mult, ALU.mult)

    def scalar_decay(la, dk, delta):
        g, gp = rows_t[3], rows_t[4]
        k.P.op("dve", lambda e: e.tensor_tensor_scan(g[0:1, :], reset[0:1, :], la[0:1, :], 0.0, ALU.mult, ALU.add),
               reads=[reset, la], writes=[g])
        Grep, EG, Etail = rep[0], rep[1], rep[2]
        replicate(Grep, g)
        to_cols(0, g)
        k.act(EG, EG[:, :], Grep, Grep[:, :], AF.Exp)
        k.tt(Etail, c3(Etail[:, :]), Grep, Grep[:, 63:NB:64].unsqueeze(2).to_broadcast([128, NCH, CH]),
             Grep, c3(Grep[:, :]), ALU.subtract)
        k.act(Etail, Etail[:, :], Etail, Etail[:, :], AF.Exp)
        d0 = dect[0]
        gcolB = cols[0:64, 0:8].unsqueeze(2).to_broadcast([64, NCH, CH])
        k.tt(d0, c3(d0[0:64, :]), Grep, c3(Grep[0:64, :]), cols, gcolB, ALU.subtract)
        k.ts(d0, d0[0:64, :], d0, d0[0:64, :], 0.0, None, ALU.min)
        k.act(d0, d0[0:64, :], d0, d0[0:64, :], AF.Exp)
        k.tt(d0, d0[0:64, :], d0, d0[0:64, :], U8, U8[0:64, :], ALU.mult)
        res = {"Grep": Grep, "EG": EG, "Etail": Etail, "dec0T": d0}
        if delta:
            k.tt(gp, gp[0:1, :], g, g[0:1, :], la, la[0:1, :], ALU.subtract)
            Gprep, EGp = rep[3], rep[4]
            replicate(Gprep, gp)
            to_cols(1, gp)
            k.act(EGp, EGp[:, :], Gprep, Gprep[:, :], AF.Exp)
            d1T, d1 = dect[1], dect[2]
            k.tt(d1T, c3(d1T[0:64, :]), Gprep, c3(Gprep[0:64, :]), cols, gcolB, ALU.subtract)
            k.ts(d1T, d1T[0:64, :], d1T, d1T[0:64, :], 0.0, None, ALU.min)
            k.act(d1T, d1T[0:64, :], d1T, d1T[0:64, :], AF.Exp)
            k.tt(d1T, d1T[0:64, :], d1T, d1T[0:64, :], SU8, SU8[0:64, :], ALU.mult)
            gpcolB = cols[0:64, 8:16].unsqueeze(2).to_broadcast([64, NCH, CH])
            k.tt(d1, c3(d1[0:64, :]), cols, gpcolB, Grep, c3(Grep[0:64, :]), ALU.subtract)
            k.ts(d1, d1[0:64, :], d1, d1[0:64, :], 0.0, None, ALU.min)
            k.act(d1, d1[0:64, :], d1, d1[0:64, :], AF.Exp)
            k.tt(d1, d1[0:64, :], d1, d1[0:64, :], SL8, SL8[0:64, :], ALU.mult)
            res.update({"EGp": EGp, "dec1T": d1T, "dec1": d1})
        return res

    out_events = []
    for blk in range(nblk):
        t0 = blk * NB
        pss = P.ps()
        for kt in range(KT):
            hb = hbuf[hcnt[0] % 3]
            hcnt[0] += 1
            P.dma("sp", hb, hb[:, :], hT, hT[kt * 128:(kt + 1) * 128, t0:t0 + NB])
            sq = sqb[kt % 2]
            k.act(sq, sq[:, :], hb, hb[:, :], AF.Square)
            k.mm(pss, pss[:, :], ones_bf, ones_bf[:, :], sq, sq[:, :], start=(kt == 0), stop=(kt == KT - 1))
        rsqrt_to(rstd, rstd[:, :], pss, pss[:, :], EPS, scale=1.0 / D_MODEL)
        for kt in range(KT):
            hb = hbuf[hcnt[0] % 3]
            hcnt[0] += 1
            P.dma("sp", hb, hb[:, :], hT, hT[kt * 128:(kt + 1) * 128, t0:t0 + NB])
            k.stt(abf, abf[:, kt * NB:(kt + 1) * NB], hb, hb[:, :],
                  C("n1w", 128, kt, kt + 1), rstd, rstd[:, :], ALU.mult, ALU.mult, sb=[cst])
        need = []
        if "gdn" in mixers:
            need += ["gq", "gk", "gv", "gb", "ga"]
        if "ret" in mixers:
            need += ["rq", "rk", "rv"]
        if "ssd" in mixers:
            need += ["mx", "mB", "mC", "mdt"]
        if "rwkv" in mixers:
            need += ["wr", "wk", "wv", "wl", "al", "gl"]
        for n in need:
            off, r = QOFF[n]
            pb = P.ps()
            for kt in range(KT):
                k.mm(pb, pb[0:r, :], wsb, wsb[:, kt * NR + off:kt * NR + off + r], abf, abf[:, kt * NB:(kt + 1) * NB],
                     start=(kt == 0), stop=(kt == KT - 1))
            h = HIST.get(n, 0)
            k.copy(raw[n], raw[n][0:r, h:h + NB], pb, pb[0:r, :])

        if "gdn" in mixers:
            qc, kc, vc, qn, kn = ft[0], ft[1], ft[2], ft[3], ft[4]
            conv_silu(qc, raw["gq"], 128, "gcq")
            conv_silu(kc, raw["gk"], 128, "gck")
            conv_silu(vc, raw["gv"], 64, "gcv")
            l2norm(qn, qc, 128, 128 ** -0.5)
            l2norm(kn, kc, 128, 1.0)
            beta, la, ba = rows_t[0], rows_t[1], rows_t[2]
            k.act(beta, beta[0:1, :], raw["gb"], raw["gb"][0:1, :], AF.Sigmoid)
            k.act(la, la[0:1, :], raw["ga"], raw["ga"][0:1, :], AF.Exp, bias=C("g_dtb", 1), sb=[cst])
            k.act(la, la[0:1, :], la, la[0:1, :], AF.Ln, bias=1.0)
            k.ts(la, la[0:1, :], la, la[0:1, :], negA[0:1, 0:1], None, ALU.mult, sb=[negA])
            k.act(ba, ba[0:1, :], la, la[0:1, :], AF.Exp)
            k.tt(ba, ba[0:1, :], ba, ba[0:1, :], beta, beta[0:1, :], ALU.mult)
            sd = scalar_decay(la, 128, True)
            Brep, BArep = rep[5], ft[5]
            replicate(Brep, beta)
            replicate(BArep, ba)
            Bs, K2s, rd, kkd, kTf, bTf = ft[6], ft[7], ft[8], ft[9], ft[10], ft[11]
            k.tt(Bs, Bs[:, :], kn, kn[:, :], BArep, BArep[:, :], ALU.mult)
            k.tt(K2s, K2s[:, :], kn, kn[:, :], Brep, Brep[:, :], ALU.mult)
            k.tt(rd, rd[:, :], qn, qn[:, :], sd["EG"], sd["EG"][:, :], ALU.mult)
            k.tt(kkd, kkd[:, :], kn, kn[:, :], sd["EGp"], sd["EGp"][:, :], ALU.mult)
            k.tt(kTf, kTf[:, :], K2s, K2s[:, :], sd["Etail"], sd["Etail"][:, :], ALU.mult)
            k.tt(bTf, bTf[:, :], Bs, Bs[:, :], sd["Etail"], sd["Etail"][:, :], ALU.mult)
            yb = ybank[0]
            sidx["gdn"] = core.block(128, True, qn, K2s, vc, rd, kTf, sd["EG"], sd["dec0T"], states["gdn"], sidx["gdn"], yb,
                                     KKs=kn, Bs=Bs, kkd=kkd, bTf=bTf, dec1T=sd["dec1T"], dec1=sd["dec1"])
            k.copy(osb, osb[0:64, :], yb, yb[0:64, :], eng="act")
            P.dma("sp", om, om[0:64, t0:t0 + NB], osb, osb[0:64, :])


        if "ret" in mixers:
            cs_t, sn_t = ft[0], ft[1]
            P.dma("sp", cs_t, cs_t[0:64, :], rotd, rotd[:, t0:t0 + NB])
            P.dma("sp", sn_t, sn_t[0:64, :], rotd, rotd[:, T + t0:T + t0 + NB])
            rotJ = mview("rotJ")
            rotJ.lw = msk.lw
            qr, kr = ft[2], ft[3]
            for (dst, src, sc) in ((qr, raw["rq"], 1.0), (kr, raw["rk"], 0.125)):
                pj = P.ps()
                k.mm(pj, pj[0:64, :], rotJ, rotJ[0:64, 0:64], src, src[0:64, :])
                k.stt(ft[4], ft[4][0:64, :], pj, pj[0:64, :], sc, sn_t, sn_t[0:64, :], ALU.mult, ALU.mult)
                k.stt(dst, dst[0:64, :], src, src[0:64, :], sc, cs_t, cs_t[0:64, :], ALU.mult, ALU.mult)
                k.tt(dst, dst[0:64, :], dst, dst[0:64, :], ft[4], ft[4][0:64, :], ALU.add)
            la = rows_t[1]
            k.ts(la, la[0:1, :], reset, reset[0:1, :], 0.0, C("r_lg", 1), ALU.mult, ALU.add, sb=[cst])
            sd = scalar_decay(la, 64, False)
            rd, kTf = ft[5], ft[6]
            k.tt(rd, rd[0:64, :], qr, qr[0:64, :], sd["EG"], sd["EG"][0:64, :], ALU.mult)
            k.tt(kTf, kTf[0:64, :], kr, kr[0:64, :], sd["Etail"], sd["Etail"][0:64, :], ALU.mult)
            yb = ybank[1]
            sidx["ret"] = core.block(64, False, qr, kr, raw["rv"], rd, kTf, sd["EG"], sd["dec0T"], states["ret"], sidx["ret"], yb)
            k.copy(osb, osb[0:64, :], yb, yb[0:64, :], eng="act")
            P.dma("sp", om, om[64:128, t0:t0 + NB], osb, osb[0:64, :])

        if "ssd" in mixers:
            xc, Bc, Cc = ft[0], ft[1], ft[2]
            conv_silu(xc, raw["mx"], 64, "mcx", bias="mbx")
            conv_silu(Bc, raw["mB"], 128, "mcB", bias="mbB")
            conv_silu(Cc, raw["mC"], 128, "mcC", bias="mbC")
            dt, la = rows_t[0], rows_t[1]
            k.act(dt, dt[0:1, :], raw["mdt"], raw["mdt"][0:1, :], AF.Exp, bias=C("m_dtb", 1), sb=[cst])
            k.act(dt, dt[0:1, :], dt, dt[0:1, :], AF.Ln, bias=1.0)
            k.ts(la, la[0:1, :], dt, dt[0:1, :], negA[0:1, 1:2], None, ALU.mult, sb=[negA])
            sd = scalar_decay(la, 128, False)
            DTrep = rep[5]
            replicate(DTrep, dt)
            K2s, rd, kTf = ft[3], ft[4], ft[5]
            k.tt(K2s, K2s[:, :], Bc, Bc[:, :], DTrep, DTrep[:, :], ALU.mult)
            k.tt(rd, rd[:, :], Cc, Cc[:, :], sd["EG"], sd["EG"][:, :], ALU.mult)
            k.tt(kTf, kTf[:, :], K2s, K2s[:, :], sd["Etail"], sd["Etail"][:, :], ALU.mult)
            yb = ybank[0]
            sidx["ssd"] = core.block(128, False, Cc, K2s, xc, rd, kTf, sd["EG"], sd["dec0T"], states["ssd"], sidx["ssd"], yb)
            k.stt(osb, osb[0:64, :], xc, xc[0:64, :], C("m_d", 64), yb, yb[0:64, :], ALU.mult, ALU.add, sb=[cst])
            P.dma("sp", om, om[128:192, t0:t0 + NB], osb, osb[0:64, :])

        if "rwkv" in mixers:
            mixed = {}
            for (nm, mun, j, r_, dst) in (("wr", "mu_r", 0, 64, ft[0]), ("wk", "mu_k", 1, 64, ft[1]), ("wv", "mu_v", 2, 64, ft[2]),
                                          ("wl", "mu_wl", 3, 32, ft[3]), ("al", "mu_al", 4, 32, ft[4]), ("gl", "mu_gl", 5, 96, ft[5])):
                src = raw[nm]
                k.ts(ft[13], ft[13][0:r_, :], src, src[0:r_, 0:NB], C(mun, r_), None, ALU.mult, sb=[cst])
                k.stt(dst, dst[0:r_, :], src, src[0:r_, 1:1 + NB], omu[0:r_, j:j + 1], ft[13], ft[13][0:r_, :],
                      ALU.mult, ALU.add, sb=[omu])
                k.copy(src, src[0:r_, 0:1], src, src[0:r_, NB:NB + 1], eng="act")
            r, kx, v, tw, al, gl = ft[0], ft[1], ft[2], ft[3], ft[4], ft[5]
            k.act(tw, tw[0:32, :], tw, tw[0:32, :], AF.Tanh)
            pz = P.ps()
            k.mm(pz, pz[0:64, :], cst, C("w_up", 32), tw, tw[0:32, :])
            lw = ft[6]
            k.act(lw, lw[0:64, :], pz, pz[0:64, :], AF.Exp, bias=omu[0:64, 6:7], scale=-1.0, sb=[omu])
            k.act(lw, lw[0:64, :], lw, lw[0:64, :], AF.Ln, bias=1.0)
            k.act(lw, lw[0:64, :], lw, lw[0:64, :], AF.Exp, bias=omu[0:64, 8:9], scale=-1.0, sb=[omu])
            k.ts(lw, lw[0:64, :], lw, lw[0:64, :], -1.0, None, ALU.mult)
            gW, gWp = ft[7], ft[8]
            P.op("dve", lambda e: e.tensor_tensor_scan(gW[0:64, :], reset[0:64, :], lw[0:64, :], 0.0, ALU.mult, ALU.add),
                 reads=[reset, lw], writes=[gW])
            k.tt(gWp, gWp[0:64, :], gW, gW[0:64, :], lw, lw[0:64, :], ALU.subtract)
            k.act(gWp, gWp[0:64, :], gWp, gWp[0:64, :], AF.Exp)
            pa = P.ps()
            k.mm(pa, pa[0:64, :], cst, C("a_up", 32), al, al[0:32, :])
            a = ft[3]
            k.act(a, a[0:64, :], pa, pa[0:64, :], AF.Sigmoid, bias=C("a0", 64), sb=[cst])
            k.act(gl, gl[0:96, :], gl, gl[0:96, :], AF.Sigmoid)
            pg = P.ps()
            k.mm(pg, pg[0:64, :], cst, C("g_up", 96), gl, gl[0:96, :])
            gg = ft[4]
            k.copy(gg, gg[0:64, :], pg, pg[0:64, :], eng="act")
            kk = ft[5]
            k.ts(kk, kk[0:64, :], kx, kx[0:64, :], C("k_k", 64), None, ALU.mult, sb=[cst])
            l2norm(kk, kk, 64, 1.0)
            k2, b = ft[9], ft[10]
            k.ts(k2, k2[0:64, :], a, a[0:64, :], C("k_a", 64), omu[0:64, 7:8], ALU.mult, ALU.add, sb=[cst, omu])
            k.tt(k2, k2[0:64, :], k2, k2[0:64, :], kx, kx[0:64, :], ALU.mult)
            k.tt(b, b[0:64, :], a, a[0:64, :], kk, kk[0:64, :], ALU.mult)
            EGW, EI, Etail = ft[11], ft[12], rep[0]
            k.act(EGW, EGW[0:64, :], gW, gW[0:64, :], AF.Exp)
            k.act(EI, EI[0:64, :], gW, gW[0:64, :], AF.Exp, scale=-1.0)
            k.tt(Etail, c3(Etail[0:64, :]), gW, gW[0:64, 63:NB:64].unsqueeze(2).to_broadcast([64, NCH, CH]),
                 gW, c3(gW[0:64, :]), ALU.subtract)
            k.act(Etail, Etail[0:64, :], Etail, Etail[0:64, :], AF.Exp)
            rd, kkd, Bs, K2s, kTf, bTf = rep[1], rep[2], rep[3], rep[4], rep[5], dect[0]
            k.tt(rd, rd[0:64, :], r, r[0:64, :], EGW, EGW[0:64, :], ALU.mult)
            k.tt(kkd, kkd[0:64, :], kk, kk[0:64, :], gWp, gWp[0:64, :], ALU.mult)
            k.tt(Bs, Bs[0:64, :], b, b[0:64, :], EI, EI[0:64, :], ALU.mult)
            k.tt(K2s, K2s[0:64, :], k2, k2[0:64, :], EI, EI[0:64, :], ALU.mult)
            k.tt(kTf, kTf[0:64, :], k2, k2[0:64, :], Etail, Etail[0:64, :], ALU.mult)
            k.tt(bTf, bTf[0:64, :], b, b[0:64, :], Etail, Etail[0:64, :], ALU.mult)
            yb = ybank[1]
            sidx["rwkv"] = core.block(64, True, rd, K2s, v, rd, kTf, EGW, U8, states["rwkv"], sidx["rwkv"], yb,
                                      KKs=kkd, Bs=Bs, kkd=kkd, bTf=bTf, dec1T=SU8, dec1=SL8)
            y, yc, t1 = ft[6], ft[7], ft[8]
            k.copy(y, y[0:64, :], yb, yb[0:64, :], eng="act")
            pm = P.ps()
            k.mm(pm, pm[0:64, :], onesf, onesf[0:64, 0:64], y, y[0:64, :])
            k.stt(yc, yc[0:64, :], pm, pm[0:64, :], -1.0 / 64, y, y[0:64, :], ALU.mult, ALU.add)
            k.tt(t1, t1[0:64, :], yc, yc[0:64, :], yc, yc[0:64, :], ALU.mult)
            pv = P.ps()
            k.mm(pv, pv[0:64, :], onesf, onesf[0:64, 0:64], t1, t1[0:64, :])
            rsqrt_to(t1, t1[0:64, :], pv, pv[0:64, :], 64e-5, scale=1.0 / 64)
            k.tt(yc, yc[0:64, :], yc, yc[0:64, :], t1, t1[0:64, :], ALU.mult)
            k.ts(yc, yc[0:64, :], yc, yc[0:64, :], C("ln_w", 64), C("ln_b", 64), ALU.mult, ALU.add, sb=[cst])
            k.tt(t1, t1[0:64, :], r, r[0:64, :], k2, k2[0:64, :], ALU.mult)
            k.ts(t1, t1[0:64, :], t1, t1[0:64, :], C("r_k", 64), None, ALU.mult, sb=[cst])
            pb_ = P.ps()
            k.mm(pb_, pb_[0:64, :], onesf, onesf[0:64, 0:64], t1, t1[0:64, :])
            k.tt(t1, t1[0:64, :], pb_, pb_[0:64, :], v, v[0:64, :], ALU.mult)
            k.tt(yc, yc[0:64, :], yc, yc[0:64, :], t1, t1[0:64, :], ALU.add)
            k.tt(osb, osb[0:64, :], yc, yc[0:64, :], gg, gg[0:64, :], ALU.mult)
            P.dma("sp", om, om[192:256, t0:t0 + NB], osb, osb[0:64, :])

    P.wait_all("sp", [om.lw] if om.lw else [])
    P.emit()
    return nc


GDN_OFF, RET_OFF, M2_OFF, RW_OFF = 0, 2056, 3592, 5136


def core_cols(c):
    hg, half = c // 2, c % 2
    grp = c // 4
    cols = {}
    cols["gq"] = np.arange(GDN_OFF + hg * 128, GDN_OFF + hg * 128 + 128)
    cols["gk"] = np.arange(GDN_OFF + 512 + hg * 128, GDN_OFF + 512 + hg * 128 + 128)
    cols["gv"] = np.arange(GDN_OFF + 1024 + hg * 128 + half * 64, GDN_OFF + 1024 + hg * 128 + half * 64 + 64)
    cols["gb"] = np.array([GDN_OFF + 2048 + hg])
    cols["ga"] = np.array([GDN_OFF + 2052 + hg])
    cols["rq"] = np.arange(RET_OFF + hg * 64, RET_OFF + hg * 64 + 64)
    cols["rk"] = np.arange(RET_OFF + 256 + hg * 64, RET_OFF + 256 + hg * 64 + 64)
    cols["rv"] = np.arange(RET_OFF + 512 + hg * 128 + half * 64, RET_OFF + 512 + hg * 128 + half * 64 + 64)
    cols["mx"] = np.arange(M2_OFF + 512 + c * 64, M2_OFF + 512 + c * 64 + 64)
    cols["mB"] = np.arange(M2_OFF + 1024 + grp * 128, M2_OFF + 1024 + grp * 128 + 128)
    cols["mC"] = np.arange(M2_OFF + 1280 + grp * 128, M2_OFF + 1280 + grp * 128 + 128)
    cols["mdt"] = np.array([M2_OFF + 1536 + c])
    cols["wr"] = np.arange(RW_OFF + c * 64, RW_OFF + c * 64 + 64)
    cols["wk"] = np.arange(RW_OFF + 512 + c * 64, RW_OFF + 512 + c * 64 + 64)
    cols["wv"] = np.arange(RW_OFF + 1024 + c * 64, RW_OFF + 1024 + c * 64 + 64)
    cols["wl"] = np.arange(RW_OFF + 1536, RW_OFF + 1568)
    cols["al"] = np.arange(RW_OFF + 1568, RW_OFF + 1600)
    cols["gl"] = np.arange(RW_OFF + 1600, RW_OFF + 1696)
    return cols


def const_masks():
    m = np.zeros((128, NMSK), np.float32)
    o = MSK["ident"][0]
    m[:, o:o + 128] = np.eye(128, dtype=np.float32)
    i = np.arange(64)
    U = (i[:, None] <= i[None, :]).astype(np.float32)
    SU = (i[:, None] < i[None, :]).astype(np.float32)
    SL = (i[:, None] > i[None, :]).astype(np.float32)
    for nm, a in (("U8", U), ("SU8", SU), ("SL8", SL)):
        o = MSK[nm][0]
        m[0:64, o:o + NB] = np.tile(a, (1, NCH))
    o = MSK["reset"][0]
    r = np.ones(NB, np.float32)
    r[0::CH] = 0.0
    m[:, o:o + NB] = r[None, :]
    o = MSK["rotJ"][0]
    J = np.zeros((64, 64), np.float32)
    for p in range(32):
        J[2 * p + 1, 2 * p] = -1.0
        J[2 * p, 2 * p + 1] = 1.0
    m[0:64, o:o + 64] = J
    o = MSK["ones"][0]
    m[:, o:o + 128] = 1.0
    return m


def rot_table(T):
    theta = 1.0 / (10000.0 ** np.linspace(0.0, 1.0, 32, dtype=np.float32))
    ang = np.arange(T, dtype=np.float32)[None, :] * np.repeat(theta, 2)[:, None].astype(np.float32)
    return np.concatenate([np.cos(ang), np.sin(ang)], axis=1).astype(np.float32)


def prep_a(inp, l, c):
    hg, half, grp = c // 2, c % 2, c // 4
    cols = core_cols(c)
    allc = np.concatenate([cols[n] for n, _ in QROWS])
    w = inp["w_in"][l][:, allc]
    win = np.ascontiguousarray(w.reshape(KT, 128, NR).transpose(1, 0, 2).reshape(128, KT * NR))
    cst = np.zeros((128, NCST), np.float32)

    def put(name, arr, rows=None):
        o, wd = CST[name]
        arr = np.asarray(arr, np.float32)
        if arr.ndim == 0:
            cst[:, o] = arr
        elif arr.ndim == 1:
            cst[0:arr.shape[0], o] = arr
        else:
            cst[0:arr.shape[0], o:o + arr.shape[1]] = arr
    put("n1w", inp["norm1_w"][l].reshape(KT, 128).T)
    gcw = inp["gdn_conv_w"][l]
    put("gcq", gcw[:, cols["gq"] - GDN_OFF].T)
    put("gck", gcw[:, cols["gk"] - GDN_OFF].T)
    put("gcv", gcw[:, cols["gv"] - GDN_OFF].T)
    put("g_alog", inp["gdn_a_log"][l][hg])
    put("g_dtb", inp["gdn_dt_bias"][l][hg])
    put("r_lg", np.float32(np.log1p(-np.exp2(np.float32(-5.0 - hg)))))
    mcw = inp["m2_conv_w"][l]
    mcb = inp["m2_conv_b"][l]
    for nm, q in (("x", "mx"), ("B", "mB"), ("C", "mC")):
        ci = cols[q] - (M2_OFF + 512)
        put("mc" + nm, mcw[:, ci].T)
        put("mb" + nm, mcb[ci])
    put("m_alog", inp["m2_a_log"][l][c])
    put("m_dtb", inp["m2_dt_bias"][l][c])
    put("m_d", inp["m2_d"][l][c])
    mu = inp["rw_mu"][l]
    for nm, q in (("mu_r", "wr"), ("mu_k", "wk"), ("mu_v", "wv"), ("mu_wl", "wl"), ("mu_al", "al"), ("mu_gl", "gl")):
        put(nm, mu[cols[q] - RW_OFF])
    hs = slice(c * 64, (c + 1) * 64)
    put("w0", inp["rw_w0"][l][hs])
    put("a0", inp["rw_a0"][l][hs])
    put("k_k", inp["rw_k_k"][l][hs])
    put("k_a", inp["rw_k_a"][l][hs])
    put("r_k", inp["rw_r_k"][l][c])
    put("ln_w", inp["rw_ln_w"][l][hs])
    put("ln_b", inp["rw_ln_b"][l][hs])
    put("w_up", inp["rw_w_up"][l][:, hs])
    put("a_up", inp["rw_a_up"][l][:, hs])
    put("g_up", inp["rw_g_up"][l][:, hs])
    return win, cst


CSTB = {}
_o = 0
for _n, _w in [("n1w", 16), ("n2w", 16), ("fnw", 16), ("gnw", 1), ("rnw", 4), ("mnw", 4)]:
    CSTB[_n] = (_o, _w)
    _o += _w
NCSTB = _o
NGT = 12
NFF = D_FF // 128


def build_phase_b(TB, final):
    nblk = TB // NB
    nc = bass.Bass("TRN2", target_bir_lowering=False)
    P = Prog(nc)
    k = K(P)
    hT = P.dram("hT", [D_MODEL, TB], F32, kind="ExternalInput")
    omx = P.dram("omx", [D_MODEL, TB], F32, kind="ExternalInput")
    wg = P.dram("wg", [NGT, 128, KT * 128], F32, kind="ExternalInput")
    wo = P.dram("wo", [KT, 128, KT * 128], F32, kind="ExternalInput")
    wu = P.dram("wu", [NFF, 128, KT * 128], F32, kind="ExternalInput")
    wd = P.dram("wd", [KT, 128, NFF * 128], F32, kind="ExternalInput")
    cstd = P.dram("cst", [128, NCSTB], F32, kind="ExternalInput")
    h2 = P.dram("dr_h2", [D_MODEL, TB], F32, kind="ExternalOutput")
    if final:
        yo = P.dram("dr_y", [D_MODEL, TB], F32, kind="ExternalOutput")
    P.psum_banks(8)
    S = P.sbuf
    cst = S("cstsb", [128, NCSTB])
    P.dma("sp", cst, cst[:, :], cstd, cstd[:, :])

    def C(name, j0=0, j1=None):
        o, w = CSTB[name]
        if j1 is None:
            j1 = w
        return cst[:, o + j0:o + j1]
    ones_bf = S("ones_bf", [128, 128], BF16)
    k.memset(ones_bf, ones_bf[:, :], 1.0)
    onesf = S("onesf", [128, 128])
    k.memset(onesf, onesf[:, :], 1.0)
    hsb = S("hsb", [128, KT * NB])
    abf = S("abf", [128, KT * NB], BF16)
    obf = S("obf", [128, KT * NB], BF16)
    ubf = S("ubf", [128, NFF * NB], BF16)
    sqb = [S("sqb%d" % i, [128, NB], BF16) for i in range(2)]
    rstd = S("rstd", [128, NB])
    wsl = [S("wsl%d" % i, [128, KT * 128], BF16) for i in range(3)]
    wdl = [S("wdl%d" % i, [128, NFF * 128], BF16) for i in range(2)]
    omt = [S("omt%d" % i, [128, NB]) for i in range(3)]
    gt = [S("gt%d" % i, [128, NB]) for i in range(3)]
    t1 = S("t1", [128, NB])
    t2 = S("t2", [128, NB])
    osb = [S("osb%d" % i, [128, NB]) for i in range(2)]
    cnt = {"w": 0, "wd": 0, "om": 0, "g": 0, "o": 0}

    def rsqrt_to(ob, oap, ib, iap, eps, scale=1.0):
        k.act(ob, oap, ib, iap, AF.Ln, bias=eps, scale=scale)
        k.act(ob, oap, ob, oap, AF.Exp, scale=-0.5)

    def rmsnorm(dst, wname):
        pss = P.ps()
        for kt in range(KT):
            sq = sqb[kt % 2]
            k.act(sq, sq[:, :], hsb, hsb[:, kt * NB:(kt + 1) * NB], AF.Square)
            k.mm(pss, pss[:, :], ones_bf, ones_bf[:, :], sq, sq[:, :], start=(kt == 0), stop=(kt == KT - 1))
        rsqrt_to(rstd, rstd[:, :], pss, pss[:, :], EPS, scale=1.0 / D_MODEL)
        for kt in range(KT):
            k.stt(dst, dst[:, kt * NB:(kt + 1) * NB], hsb, hsb[:, kt * NB:(kt + 1) * NB],
                  C(wname, kt, kt + 1), rstd, rstd[:, :], ALU.mult, ALU.mult, sb=[cst])

    def load_w(src, j):
        w = wsl[cnt["w"] % 3]
        cnt["w"] += 1
        P.dma("pool", w, w[:, :], src, src[j])
        return w

    def proj(w, rhs):
        pb = P.ps()
        for kt in range(KT):
            k.mm(pb, pb[:, :], w, w[:, kt * 128:(kt + 1) * 128], rhs, rhs[:, kt * NB:(kt + 1) * NB],
                 start=(kt == 0), stop=(kt == KT - 1))
        return pb

    def load_om(j, t0):
        o = omt[cnt["om"] % 3]
        cnt["om"] += 1
        P.dma("sp", o, o[:, :], omx, omx[j * 128:(j + 1) * 128, t0:t0 + NB])
        return o

    def gate(j):
        w = load_w(wg, j)
        pb = proj(w, abf)
        g = gt[cnt["g"] % 3]
        cnt["g"] += 1
        k.act(g, g[:, :], pb, pb[:, :], AF.Silu)
        return g

    for blk in range(nblk):
        t0 = blk * NB
        for kt in range(KT):
            P.dma("sp", hsb, hsb[:, kt * NB:(kt + 1) * NB], hT, hT[kt * 128:(kt + 1) * 128, t0:t0 + NB])
        rmsnorm(abf, "n1w")
        for j in range(4):
            o = load_om(j, t0)
            g = gate(j)
            k.tt(t1, t1[:, :], o, o[:, :], o, o[:, :], ALU.mult)
            pb = P.ps()
            k.mm(pb, pb[:, :], onesf, onesf[:, :], t1, t1[:, :])
            rsqrt_to(t1, t1[:, :], pb, pb[:, :], EPS, scale=1.0 / 128)
            k.stt(t2, t2[:, :], o, o[:, :], C("gnw"), t1, t1[:, :], ALU.mult, ALU.mult, sb=[cst])
            k.tt(obf, obf[:, j * NB:(j + 1) * NB], t2, t2[:, :], g, g[:, :], ALU.mult)
        for j in range(4):
            o = load_om(4 + j, t0)
            g = gate(4 + j)
            pb = P.ps()
            k.mm(pb, pb[:, :], onesf, onesf[:, :], o, o[:, :])
            k.stt(t2, t2[:, :], pb, pb[:, :], -1.0 / 128, o, o[:, :], ALU.mult, ALU.add)
            k.tt(t1, t1[:, :], t2, t2[:, :], t2, t2[:, :], ALU.mult)
            pb2 = P.ps()
            k.mm(pb2, pb2[:, :], onesf, onesf[:, :], t1, t1[:, :])
            rsqrt_to(t1, t1[:, :], pb2, pb2[:, :], EPS, scale=1.0 / 128)
            k.stt(t2, t2[:, :], t2, t2[:, :], C("rnw", j, j + 1), t1, t1[:, :], ALU.mult, ALU.mult, sb=[cst])
            k.tt(obf, obf[:, (4 + j) * NB:(5 + j) * NB], t2, t2[:, :], g, g[:, :], ALU.mult)
        for gi in range(2):
            ys = []
            pb = P.ps()
            for jj in range(2):
                j = gi * 2 + jj
                o = load_om(8 + j, t0)
                g = gate(8 + j)
                y = osb[jj]
                k.tt(y, y[:, :], o, o[:, :], g, g[:, :], ALU.mult)
                k.tt(t1, t1[:, :], y, y[:, :], y, y[:, :], ALU.mult)
                k.mm(pb, pb[:, :], onesf, onesf[:, :], t1, t1[:, :], start=(jj == 0), stop=(jj == 1))
                ys.append(y)
            rsqrt_to(t1, t1[:, :], pb, pb[:, :], EPS, scale=1.0 / 256)
            for jj in range(2):
                j = gi * 2 + jj
                k.stt(obf, obf[:, (8 + j) * NB:(9 + j) * NB], ys[jj], ys[jj][:, :], C("mnw", j, j + 1),
                      t1, t1[:, :], ALU.mult, ALU.mult, sb=[cst])
        for j in range(4):
            o = load_om(12 + j, t0)
            k.copy(obf, obf[:, (12 + j) * NB:(13 + j) * NB], o, o[:, :])
        for j in range(KT):
            w = load_w(wo, j)
            pb = proj(w, obf)
            k.tt(hsb, hsb[:, j * NB:(j + 1) * NB], hsb, hsb[:, j * NB:(j + 1) * NB], pb, pb[:, :], ALU.add)
        rmsnorm(abf, "n2w")
        for j in range(NFF):
            w = load_w(wu, j)
            pb = proj(w, abf)
            k.act(ubf, ubf[:, j * NB:(j + 1) * NB], pb, pb[:, :], AF.Relu)
            k.tt(ubf, ubf[:, j * NB:(j + 1) * NB], ubf, ubf[:, j * NB:(j + 1) * NB], ubf, ubf[:, j * NB:(j + 1) * NB],
                 ALU.mult, eng="pool")
        for j in range(KT):
            w = wdl[cnt["wd"] % 2]
            cnt["wd"] += 1
            P.dma("pool", w, w[:, :], wd, wd[j])
            pb = P.ps()
            for kk_ in range(NFF):
                k.mm(pb, pb[:, :], w, w[:, kk_ * 128:(kk_ + 1) * 128], ubf, ubf[:, kk_ * NB:(kk_ + 1) * NB],
                     start=(kk_ == 0), stop=(kk_ == NFF - 1))
            k.tt(hsb, hsb[:, j * NB:(j + 1) * NB], hsb, hsb[:, j * NB:(j + 1) * NB], pb, pb[:, :], ALU.add)
        for kt in range(KT):
            P.dma("sp", h2, h2[kt * 128:(kt + 1) * 128, t0:t0 + NB], hsb, hsb[:, kt * NB:(kt + 1) * NB], sem_buf=osb[0])
        if final:
            pss = P.ps()
            for kt in range(KT):
                sq = sqb[kt % 2]
                k.act(sq, sq[:, :], hsb, hsb[:, kt * NB:(kt + 1) * NB], AF.Square)
                k.mm(pss, pss[:, :], ones_bf, ones_bf[:, :], sq, sq[:, :], start=(kt == 0), stop=(kt == KT - 1))
            rsqrt_to(rstd, rstd[:, :], pss, pss[:, :], EPS, scale=1.0 / D_MODEL)
            for kt in range(KT):
                y = osb[kt % 2]
                k.stt(y, y[:, :], hsb, hsb[:, kt * NB:(kt + 1) * NB], C("fnw", kt, kt + 1), rstd, rstd[:, :],
                      ALU.mult, ALU.mult, sb=[cst])
                P.dma("sp", yo, yo[kt * 128:(kt + 1) * 128, t0:t0 + NB], y, y[:, :], sem_buf=osb[1])
    evs = [h2.lw]
    if final:
        evs.append(yo.lw)
    P.wait_all("sp", evs)
    P.emit()
    return nc


def prep_b_weights(inp, l):
    def slabs(W, ntile):
        Kd, Md = W.shape
        return np.ascontiguousarray(W.reshape(Kd // 128, 128, Md // 128, 128).transpose(2, 1, 0, 3).reshape(Md // 128, 128, (Kd // 128) * 128))
    w_in = inp["w_in"][l]
    gcols = np.concatenate([np.arange(GDN_OFF + 1536, GDN_OFF + 2048), np.arange(RET_OFF + 1024, RET_OFF + 1536),
                            np.arange(M2_OFF, M2_OFF + 512)])
    wg = slabs(w_in[:, gcols], NGT)
    wo = slabs(inp["w_out"][l], KT)
    wu = slabs(inp["w_ffn_up"][l], NFF)
    wd = slabs(inp["w_ffn_down"][l], KT)
    cst = np.zeros((128, NCSTB), np.float32)
    cst[:, 0:16] = inp["norm1_w"][l].reshape(KT, 128).T
    cst[:, 16:32] = inp["norm2_w"][l].reshape(KT, 128).T
    cst[:, 32:48] = inp["final_norm_w"].reshape(KT, 128).T
    cst[:, 48] = inp["gdn_norm_w"][l]
    cst[:, 49:53] = inp["ret_norm_w"][l].reshape(4, 128).T
    cst[:, 53:57] = inp["m2_norm_w"][l].reshape(4, 128).T
    return {"wg": wg, "wo": wo, "wu": wu, "wd": wd, "cst": cst}


_CACHE = {}


def _prog(key, fn):
    if key not in _CACHE:
        _CACHE[key] = fn()
    return _CACHE[key]


def kernel(**inp):
    inp = {k_: np.asarray(v) for k_, v in inp.items()}
    T = SEQ
    TB = T // NCORE
    msk = const_masks()
    rot = rot_table(T)
    hT = np.ascontiguousarray(inp["x"][0].T)
    cores = list(range(NCORE))
    out = None
    for l in range(2):
        nca = _prog(("a", T), lambda: build_phase_a(T))
        maps = []
        for c in cores:
            win, cst = prep_a(inp, l, c)
            maps.append({"hT": hT, "win": win, "cst": cst, "msk": msk, "rot": rot})
        res = run_bass_kernel_spmd(nca, maps, core_ids=cores)
        omx = np.empty((D_MODEL, T), np.float32)
        for c in cores:
            om = res.results[c]["dr_om"]
            hg, half = c // 2, c % 2
            omx[hg * 128 + half * 64: hg * 128 + half * 64 + 64] = om[0:64]
            omx[512 + hg * 128 + half * 64: 512 + hg * 128 + half * 64 + 64] = om[64:128]
            omx[1024 + c * 64: 1024 + (c + 1) * 64] = om[128:192]
            omx[1536 + c * 64: 1536 + (c + 1) * 64] = om[192:256]
        final = (l == 1)
        ncb = _prog(("b", TB, final), lambda: build_phase_b(TB, final))
        wb = prep_b_weights(inp, l)
        maps = []
        for c in cores:
            m = dict(wb)
            m["hT"] = np.ascontiguousarray(hT[:, c * TB:(c + 1) * TB])
            m["omx"] = np.ascontiguousarray(omx[:, c * TB:(c + 1) * TB])
            maps.append(m)
        res = run_bass_kernel_spmd(ncb, maps, core_ids=cores)
        hT = np.concatenate([res.results[c]["dr_h2"] for c in cores], axis=1)
        if final:
            out = np.concatenate([res.results[c]["dr_y"] for c in cores], axis=1)
    return np.ascontiguousarray(out.T)[None].astype(np.float32)
```

```python
import math
import numpy as np
from contextlib import ExitStack
import concourse.bass as bass
import concourse.mybir as mybir
from concourse.bass_utils import run_bass_kernel_spmd

F32 = mybir.dt.float32
BF16 = mybir.dt.bfloat16
ALU = mybir.AluOpType
AF = mybir.ActivationFunctionType

ENGS = ("pe", "act", "dve", "pool", "sp")
SEM_ROT = 30000

D_MODEL = 2048
SEQ = 8192
NCORE = 8
EPS = 1e-6
NB = 512
CH = 64
NCH = NB // CH
KT = D_MODEL // 128
D_FF = 4 * D_MODEL


class Buf:
    __slots__ = ("name", "t", "lw", "rd", "dsem", "dcnt")

    def __init__(self, name, t):
        self.name = name
        self.t = t
        self.lw = None
        self.rd = []
        self.dsem = None
        self.dcnt = 0

    def __getitem__(self, idx):
        return self.t[idx]


class Prog:
    def __init__(self, nc):
        self.nc = nc
        self.es = ExitStack()
        self.ops = {e: [] for e in ENGS}
        self.cnt = {e: 0 for e in ENGS}
        self.gen = {e: 0 for e in ENGS}
        self.known = {e: {} for e in ENGS}
        self.sems = {}
        self.nbuf = 0
        self.npsum = 0
        self.psums = []
        self.rot = []
        self.arena = None
        self.aoff = 0
        self.barrier = []
        self.dsems = {}

    def sem(self, key):
        if key not in self.sems:
            nm = "s_" + "_".join(str(k) for k in key)
            self.sems[key] = self.es.enter_context(self.nc.semaphore(nm))
        return self.sems[key]

    def use_arena(self, nf32):
        self.arena = self.es.enter_context(self.nc.sbuf_tensor("arena", [128, nf32], F32))
        self.asize = nf32

    def sbuf(self, name, shape, dtype=F32):
        if self.arena is None:
            t = self.es.enter_context(self.nc.sbuf_tensor(name, list(shape), dtype))
            return Buf(name, t)
        rows, cols = shape
        esz = 2 if dtype == BF16 else 4
        n32 = (cols * esz + 3) // 4
        assert self.aoff + n32 <= self.asize, ("arena overflow", name, self.aoff, n32)
        ap = self.arena[0:rows, self.aoff:self.aoff + n32]
        if dtype != F32:
            ap = ap.bitcast(dtype)[:, 0:cols]
        self.aoff += n32
        b = Buf(name, ap)
        b.rd = list(self.barrier)
        return b

    def phase_reset(self):
        evs = [((e, self.gen[e]), self.cnt[e]) for e in ENGS if self.cnt[e] > 0]
        evs += list(self.dsems.items())
        self.barrier = evs
        self.aoff = 0

    def psum_banks(self, n=8):
        for i in range(n):
            t = self.es.enter_context(self.nc.psum_tensor("psb%d" % i, [128, 512], F32))
            self.psums.append(Buf("psb%d" % i, t))

    def ps(self):
        pool = self.rot if self.rot else self.psums
        b = pool[self.npsum % len(pool)]
        self.npsum += 1
        return b

    def dram(self, name, shape, dtype=F32, kind=None):
        if kind is None:
            t = self.nc.dram_tensor(name, list(shape), dtype)
        else:
            t = self.nc.dram_tensor(name, list(shape), dtype, kind=kind)
        return Buf(name, t.ap())

    def _deps(self, eng, reads, writes):
        evs = []
        for b in reads:
            if b.lw is not None:
                evs.append(b.lw)
            if b.name.startswith("psb"):
                evs.extend(b.rd)
        for b in writes:
            if b.lw is not None:
                evs.append(b.lw)
            evs.extend(b.rd)
        need = {}
        kn = self.known[eng]
        for (k, v) in evs:
            if kn.get(k, 0) >= v:
                continue
            if need.get(k, 0) < v:
                need[k] = v
        for k, v in need.items():
            kn[k] = v
        return list(need.items())

    def _record(self, ev, reads, writes):
        for b in writes:
            b.lw = ev
            b.rd = []
        for b in reads:
            if b in writes:
                continue
            b.rd.append(ev)
            if len(b.rd) > 24:
                m = {}
                for (k, v) in b.rd:
                    if m.get(k, 0) < v:
                        m[k] = v
                b.rd = list(m.items())

    def op(self, eng, fn, reads=(), writes=()):
        reads = [b for b in reads if b is not None]
        writes = [b for b in writes if b is not None]
        waits = self._deps(eng, reads, writes)
        if self.cnt[eng] >= SEM_ROT:
            self.gen[eng] += 1
            self.cnt[eng] = 0
        self.cnt[eng] += 1
        key = (eng, self.gen[eng])
        ev = (key, self.cnt[eng])
        self.ops[eng].append((waits, fn, key, 1))
        self._record(ev, reads, writes)
        return ev

    def dma(self, eng, out_buf, out_ap, in_buf, in_ap, sem_buf=None, **kw):
        sb = sem_buf
        if sb is None:
            sb = in_buf if (out_buf.name.startswith("dr_")) else out_buf
        if sb.dsem is None:
            self.nbuf += 1
            sb.dsem = ("d", self.nbuf)
        waits = self._deps(eng, [in_buf], [out_buf])
        sb.dcnt += 16
        ev = (sb.dsem, sb.dcnt)
        self.dsems[sb.dsem] = sb.dcnt

        def fn(e, out_ap=out_ap, in_ap=in_ap, kw=kw):
            return e.dma_start(out=out_ap, in_=in_ap, **kw)
        self.ops[eng].append((waits, fn, sb.dsem, 16))
        self._record(ev, [in_buf], [out_buf])
        return ev

    def gather(self, out_buf, out_ap, in_buf, in_ap, idx_buf, idx_ap):
        sb = out_buf
        if sb.dsem is None:
            self.nbuf += 1
            sb.dsem = ("d", self.nbuf)
        waits = self._deps("pool", [in_buf, idx_buf], [out_buf])
        sb.dcnt += 16
        ev = (sb.dsem, sb.dcnt)
        self.dsems[sb.dsem] = sb.dcnt

        def fn(e):
            return e.indirect_dma_start(out=out_ap, out_offset=None, in_=in_ap,
                                        in_offset=bass.IndirectOffsetOnAxis(ap=idx_ap, axis=0))
        self.ops["pool"].append((waits, fn, sb.dsem, 16))
        self._record(ev, [in_buf, idx_buf], [out_buf])
        return ev

    def allgather(self, in_buf, out_buf, ncores):
        self.ncc = getattr(self, "ncc", 0) + 1
        key = ("cc", self.ncc)
        waits = self._deps("pool", [in_buf], [out_buf])
        ev = (key, 1)
        self.dsems[key] = 1
        iap, oap = in_buf.t.opt(), out_buf.t.opt()

        def fn(e):
            return e.collective_compute("AllGather", ALU.bypass, replica_groups=[list(range(ncores))],
                                        ins=[iap], outs=[oap])
        self.ops["pool"].append((waits, fn, key, 1))
        self._record(ev, [in_buf], [out_buf])
        return ev

    def wait_all(self, eng, events):
        need = {}
        for (k, v) in events:
            if need.get(k, 0) < v:
                need[k] = v
        self.ops[eng].append((list(need.items()), None, None, 0))

    def emit(self):
        nc = self.nc
        for e in ENGS:
            for (waits, fn, key, inc) in self.ops[e]:
                for (k, v) in waits:
                    self.sem(k)
                if key is not None:
                    self.sem(key)
        with nc.Block() as block:
            def replay(ename, eng):
                for (waits, fn, key, inc) in self.ops[ename]:
                    if fn is None or inc != 1 or not waits:
                        for (k, v) in waits:
                            eng.wait_ge(self.sems[k], v)
                        if fn is not None:
                            fn(eng).then_inc(self.sems[key], inc)
                    else:
                        for (k, v) in waits[:-1]:
                            eng.wait_ge(self.sems[k], v)
                        ins = fn(eng)
                        ins._wait_ge(self.sems[waits[-1][0]], waits[-1][1])
                        ins.then_inc(self.sems[key], inc)

            @block.tensor
            def _(eng):
                replay("pe", eng)

            @block.scalar
            def _(eng):
                replay("act", eng)

            @block.vector
            def _(eng):
                replay("dve", eng)

            @block.gpsimd
            def _(eng):
                replay("pool", eng)

            @block.sync
            def _(eng):
                replay("sp", eng)
        self.es.close()


class K:
    def __init__(self, P):
        self.P = P
        self.flip = 0

    def mm(self, ob, oap, lb, lap, rb, rap, start=True, stop=True):
        self.P.op("pe", lambda e: e.matmul(oap, lhsT=lap, rhs=rap, start=start, stop=stop),
                  reads=[lb, rb], writes=[ob])

    def tt(self, ob, oap, ab, aap, bb, bap, op, eng="dve"):
        self.P.op(eng, lambda e: e.tensor_tensor(oap, aap, bap, op), reads=[ab, bb], writes=[ob])

    def ts(self, ob, oap, ab, aap, s1, s2, op0, op1=None, eng="dve", sb=()):
        if op1 is None:
            self.P.op(eng, lambda e: e.tensor_scalar(oap, aap, s1, None, op0), reads=[ab] + list(sb), writes=[ob])
        else:
            self.P.op(eng, lambda e: e.tensor_scalar(oap, aap, s1, s2, op0, op1), reads=[ab] + list(sb), writes=[ob])

    def stt(self, ob, oap, ab, aap, sc, bb, bap, op0, op1, eng="dve", sb=()):
        self.P.op(eng, lambda e: e.scalar_tensor_tensor(oap, aap, sc, bap, op0, op1),
                  reads=[ab, bb] + list(sb), writes=[ob])

    def act(self, ob, oap, ab, aap, func, bias=None, scale=None, sb=()):
        kw = {}
        if bias is not None:
            kw["bias"] = bias
        if scale is not None:
            kw["scale"] = scale
        self.P.op("act", lambda e: e.activation(oap, aap, func, **kw), reads=[ab] + list(sb), writes=[ob])

    def copy(self, ob, oap, ab, aap, eng=None):
        if eng is None:
            self.flip ^= 1
            eng = "act" if self.flip else "dve"
        if eng == "act":
            self.P.op("act", lambda e: e.copy(oap, aap), reads=[ab], writes=[ob])
        else:
            self.P.op(eng, lambda e: e.tensor_copy(oap, aap), reads=[ab], writes=[ob])

    def memset(self, ob, oap, val, eng="pool"):
        self.P.op(eng, lambda e: e.memset(oap, val), writes=[ob])


def c3(ap, j=CH):
    return ap.rearrange("p (c j) -> p c j", j=j)


class Core:
    def __init__(self, P, kk, cst):
        self.P = P
        self.k = kk
        self.cst = cst
        S = P.sbuf
        B = lambda n, shp: P.sbuf(n, shp, BF16)
        self.vtok = B("c_vtok", [64, NB])
        self.kT = B("c_kT", [64, NCH * 128])
        self.bT = B("c_bT", [64, NCH * 128])
        self.sc = [B("c_sc%d" % i, [64, NB]) for i in range(3)]
        self.xy = [B("c_xy%d" % i, [64, NB]) for i in range(4)]
        self.Ru = B("c_Ru", [64, NB])
        self.RU = B("c_RU", [64, NCH * 128])
        self.Ru32 = S("c_Ru32", [64, NB])
        self.RU32 = S("c_RU32", [64, NCH * 128])
        self.MT = B("c_MT", [128, NCH * 128])
        self.Qf = B("c_Qf", [128, NB])

    def transp(self, dst, src, rows, dst2=None):
        P, k = self.P, self.k
        ident = self.cst["identb"]
        per = 512 // rows
        for c0 in range(0, NCH, per):
            pb = P.ps()
            for c in range(c0, c0 + per):
                k.mm(pb, pb[0:64, (c - c0) * rows:(c - c0 + 1) * rows], src, src[0:rows, c * CH:(c + 1) * CH],
                     ident, ident[0:rows, 0:rows])
            if dst2 is not None:
                k.copy(dst2, dst2[0:64, c0 * rows:(c0 + per) * rows], pb, pb[0:64, 0:per * rows], eng="dve")
                k.copy(dst, dst[0:64, c0 * rows:(c0 + per) * rows], dst2, dst2[0:64, c0 * rows:(c0 + per) * rows], eng="act")
            else:
                k.copy(dst, dst[0:64, c0 * rows:(c0 + per) * rows], pb, pb[0:64, 0:per * rows])

    def chunk_mm(self, width, terms):
        P, k = self.P, self.k
        per = 512 // width
        outs = []
        for c0 in range(0, NCH, per):
            pb = P.ps()
            for c in range(c0, c0 + per):
                o = pb[:, (c - c0) * width:(c - c0 + 1) * width]
                for i, (lb, lf, rb, rf) in enumerate(terms):
                    lap = lf(c)
                    m = lap.shape[1]
                    k.mm(pb, pb[0:m, (c - c0) * width:(c - c0 + 1) * width], lb, lap, rb, rf(c),
                         start=(i == 0), stop=(i == len(terms) - 1))
            outs.append((pb, c0, per))
        return outs

    def block(self, dk, delta, Rs, K2s, vT, rd, kTf, gam, dec0T, S2, si, yb, S32=None,
              KKs=None, Bs=None, kkd=None, bTf=None, dec1T=None, dec1=None):
        P, k, cst = self.P, self.k, self.cst
        vtok, kT, bT, Ru, RU, MT, Qf = self.vtok, self.kT, self.bT, self.Ru, self.RU, self.MT, self.Qf
        Ru32, RU32 = self.Ru32, self.RU32
        ident = cst["ident"]
        cs = lambda c: slice(c * CH, (c + 1) * CH)
        ck = lambda c: slice(c * dk, (c + 1) * dk)
        self.transp(vtok, vT, 64)
        self.transp(kT, kTf, dk)
        RKT = self.sc[0]
        (pb, _, _), = self.chunk_mm(64, [(K2s, lambda c: K2s[0:dk, cs(c)], Rs, lambda c: Rs[0:dk, cs(c)])])
        k.tt(RKT, RKT[0:64, :], pb, pb[0:64, :], dec0T, dec0T[0:64, :], ALU.mult)
        if delta:
            self.transp(bT, bTf, dk)
            RBT, KKT = self.sc[1], self.sc[2]
            X, Y = self.xy[0], self.xy[1]
            (p1, _, _), = self.chunk_mm(64, [(Bs, lambda c: Bs[0:dk, cs(c)], Rs, lambda c: Rs[0:dk, cs(c)])])
            k.tt(RBT, RBT[0:64, :], p1, p1[0:64, :], dec0T, dec0T[0:64, :], ALU.mult)
            (p2, _, _), = self.chunk_mm(64, [(K2s, lambda c: K2s[0:dk, cs(c)], KKs, lambda c: KKs[0:dk, cs(c)])])
            k.tt(KKT, KKT[0:64, :], p2, p2[0:64, :], dec1T, dec1T[0:64, :], ALU.mult)
            (p3, _, _), = self.chunk_mm(64, [(Bs, lambda c: Bs[0:dk, cs(c)], KKs, lambda c: KKs[0:dk, cs(c)])])
            k.stt(Y, Y[0:64, :], p3, p3[0:64, :], -1.0, dec1T, dec1T[0:64, :], ALU.mult, ALU.mult)
            (p4, _, _), = self.chunk_mm(64, [(KKs, lambda c: KKs[0:dk, cs(c)], Bs, lambda c: Bs[0:dk, cs(c)])])
            k.stt(X, X[0:64, :], p4, p4[0:64, :], -1.0, dec1, dec1[0:64, :], ALU.mult, ALU.mult)
            (p5, _, _), = self.chunk_mm(64, [(KKT, lambda c: KKT[0:64, cs(c)], vtok, lambda c: vtok[0:64, cs(c)])])
            k.ts(Ru32, Ru32[0:64, :], p5, p5[0:64, :], -1.0, None, ALU.mult)
            k.copy(Ru, Ru[0:64, :], Ru32, Ru32[0:64, :], eng="act")
            self.transp(RU, kkd, dk, dst2=RU32)
            xi = 0
            for lvl in range(6):
                X, Y = self.xy[xi], self.xy[xi + 1]
                (pu, _, _), = self.chunk_mm(64, [(Y, lambda c: Y[0:64, cs(c)], Ru, lambda c: Ru[0:64, cs(c)])])
                outsU = self.chunk_mm(dk, [(Y, lambda c: Y[0:64, cs(c)], RU, lambda c: RU[0:64, ck(c)])])
                if lvl < 5:
                    Xn, Yn = self.xy[2 - xi], self.xy[3 - xi]
                    (px, _, _), = self.chunk_mm(64, [(Y, lambda c: Y[0:64, cs(c)], X, lambda c: X[0:64, cs(c)])])
                    (py, _, _), = self.chunk_mm(64, [(X, lambda c: X[0:64, cs(c)], Y, lambda c: Y[0:64, cs(c)])])
                k.tt(Ru32, Ru32[0:64, :], Ru32, Ru32[0:64, :], pu, pu[0:64, :], ALU.add)
                k.copy(Ru, Ru[0:64, :], Ru32, Ru32[0:64, :], eng="act")
                for (pbU, c0, n) in outsU:
                    k.tt(RU32, RU32[0:64, c0 * dk:(c0 + n) * dk], RU32, RU32[0:64, c0 * dk:(c0 + n) * dk],
                         pbU, pbU[0:64, 0:n * dk], ALU.add)
                    k.copy(RU, RU[0:64, c0 * dk:(c0 + n) * dk], RU32, RU32[0:64, c0 * dk:(c0 + n) * dk], eng="pool")
                if lvl < 5:
                    k.copy(Xn, Xn[0:64, :], px, px[0:64, :], eng="act")
                    k.copy(Yn, Yn[0:64, :], py, py[0:64, :], eng="act")
                    xi = 2 - xi
            outsM = self.chunk_mm(dk, [(RU, lambda c: RU[0:64, ck(c)], bT, lambda c: bT[0:64, ck(c)])])
            for (pbM, c0, n) in outsM:
                for c in range(c0, c0 + n):
                    k.stt(MT, MT[0:dk, ck(c)], ident, ident[0:dk, 0:dk], gam[0:dk, c * CH + 63:c * CH + 64],
                          pbM, pbM[0:dk, (c - c0) * dk:(c - c0 + 1) * dk], ALU.mult, ALU.subtract, sb=[gam])
            (pq, _, _), = self.chunk_mm(64, [(RU, lambda c: RU[0:64, ck(c)], RBT, lambda c: RBT[0:64, cs(c)])])
            k.tt(Qf, Qf[0:dk, :], rd, rd[0:dk, :], pq, pq[0:dk, :], ALU.subtract)
            Q = Qf
        else:
            Q = rd
        for c in range(NCH):
            S = S2[si]
            Sn = S2[1 - si]
            sl = cs(c)
            k.mm(yb, yb[0:64, sl], S, S[0:dk, 0:64], Q, Q[0:dk, sl], start=True, stop=False)
            k.mm(yb, yb[0:64, sl], vtok, vtok[0:64, sl], RKT, RKT[0:64, sl], start=False, stop=not delta)
            if delta:
                k.mm(yb, yb[0:64, sl], Ru, Ru[0:64, sl], RBT, RBT[0:64, sl], start=False, stop=True)
            pbs = P.ps()
            if delta:
                k.mm(pbs, pbs[0:dk, 0:64], MT, MT[0:dk, ck(c)], S, S[0:dk, 0:64], start=True, stop=False)
                k.mm(pbs, pbs[0:dk, 0:64], kT, kT[0:64, ck(c)], vtok, vtok[0:64, sl], start=False, stop=False)
                k.mm(pbs, pbs[0:dk, 0:64], bT, bT[0:64, ck(c)], Ru, Ru[0:64, sl], start=False, stop=True)
                k.copy(Sn, Sn[0:dk, 0:64], pbs, pbs[0:dk, 0:64], eng="dve")
            else:
                k.mm(pbs, pbs[0:dk, 0:64], kT, kT[0:64, ck(c)], vtok, vtok[0:64, sl], start=True, stop=True)
                k.stt(S32, S32[0:dk, 0:64], S32, S32[0:dk, 0:64], gam[0:dk, c * CH + 63:c * CH + 64],
                      pbs, pbs[0:dk, 0:64], ALU.mult, ALU.add, sb=[gam])
                k.copy(Sn, Sn[0:dk, 0:64], S32, S32[0:dk, 0:64], eng="act")
            si = 1 - si
        return si


QROWS = [("gq", 128), ("gk", 128), ("gv", 64), ("gb", 1), ("ga", 1),
         ("rq", 64), ("rk", 64), ("rv", 64),
         ("mx", 64), ("mB", 128), ("mC", 128), ("mdt", 1),
         ("wr", 64), ("wk", 64), ("wv", 64), ("wl", 32), ("al", 32), ("gl", 96)]
QOFF = {}
_o = 0
for _n, _r in QROWS:
    QOFF[_n] = (_o, _r)
    _o += _r
NR = _o
HIST = {"gq": 3, "gk": 3, "gv": 3, "mx": 3, "mB": 3, "mC": 3,
        "wr": 1, "wk": 1, "wv": 1, "wl": 1, "al": 1, "gl": 1}
CST = {}
_o = 0
for _n, _w in [("n1w", 16), ("gcq", 4), ("gck", 4), ("gcv", 4), ("g_alog", 1), ("g_dtb", 1),
               ("r_lg", 1), ("mcx", 4), ("mcB", 4), ("mcC", 4), ("mbx", 1), ("mbB", 1), ("mbC", 1),
               ("m_alog", 1), ("m_dtb", 1), ("m_d", 1),
               ("mu_r", 1), ("mu_k", 1), ("mu_v", 1), ("mu_wl", 1), ("mu_al", 1), ("mu_gl", 1),
               ("w0", 1), ("a0", 1), ("k_k", 1), ("k_a", 1), ("r_k", 1), ("ln_w", 1), ("ln_b", 1),
               ("w_up", 64), ("a_up", 64), ("g_up", 64)]:
    CST[_n] = (_o, _w)
    _o += _w
NCST = _o
MSK = {}
_o = 0
for _n, _w in [("ident", 128), ("U8", NB), ("SU8", NB), ("SL8", NB), ("reset", NB), ("rotJ", 64), ("ones", 128)]:
    MSK[_n] = (_o, _w)
    _o += _w
NMSK = _o


def build_phase_a(T, mixers=("gdn", "ret", "ssd", "rwkv")):
    nc = bass.Bass("TRN2", target_bir_lowering=False)
    P = Prog(nc)
    k = K(P)
    hT = P.dram("hT", [D_MODEL, T], F32, kind="ExternalInput")
    win = P.dram("win", [128, KT * NR], F32, kind="ExternalInput")
    cstd = P.dram("cst", [128, NCST], F32, kind="ExternalInput")
    mskd = P.dram("msk", [128, NMSK], F32, kind="ExternalInput")
    rotd = P.dram("rot", [64, 2 * T], F32, kind="ExternalInput")
    om = P.dram("dr_om", [4 * 64, T], F32, kind="ExternalOutput")
    P.psum_banks(8)
    emit_phase_a(P, k, T, lambda kt, blk: (hT, hT[kt * 128:(kt + 1) * 128, blk * NB:(blk + 1) * NB]),
                 win, cstd, mskd, rotd, lambda j, blk: (om, om[j * 64:(j + 1) * 64, blk * NB:(blk + 1) * NB]), mixers)
    P.wait_all("sp", [om.lw] if om.lw else [])
    P.emit()
    return nc


def emit_phase_a(P, k, T, hsrc, win, cstd, mskd, rotd, omdst, mixers=("gdn", "ret", "ssd", "rwkv")):
    nblk = T // NB
    P.rot = P.psums[0:6]
    ybank = P.psums[6:8]
    S = P.sbuf
    wsb = S("wsb", [128, KT * NR], BF16)
    cst = S("cstsb", [128, NCST])
    msk = S("msksb", [128, NMSK])
    P.dma("sp", cst, cst[:, :], cstd, cstd[:, :])
    P.dma("sp", msk, msk[:, :], mskd, mskd[:, :])

    def C(name, rows=128, j0=0, j1=None):
        o, w = CST[name]
        if j1 is None:
            j1 = w
        return cst[0:rows, o + j0:o + j1]

    def M(name, rows=64):
        o, w = MSK[name]
        return msk[0:rows, o:o + w]

    class MV:
        def __init__(self, name):
            self.o, self.w = MSK[name]

        def __getitem__(self, idx):
            r, c = idx
            c0 = 0 if c.start is None else c.start
            c1 = self.w if c.stop is None else c.stop
            return msk[r, self.o + c0:self.o + c1]
    class View(Buf):
        pass

    def mview(name):
        v = MV(name)
        b = Buf.__new__(Buf)
        b.name = "msk_" + name
        b.t = v
        b.lw = None
        b.rd = []
        b.dsem = None
        b.dcnt = 0
        return b
    ident, U8, SU8, SL8 = mview("ident"), mview("U8"), mview("SU8"), mview("SL8")
    identb = S("identb", [128, 128], BF16)
    k.copy(identb, identb[:, :], msk, msk[:, MSK["ident"][0]:MSK["ident"][0] + 128], eng="act")
    core = Core(P, k, {"ident": ident, "identb": identb})
    ones_bf = S("ones_bf", [128, 128], BF16)
    k.memset(ones_bf, ones_bf[:, :], 1.0)
    fb = [S("fb%d" % i, [128, NB], BF16) for i in range(9)]
    zero_ev = None

    for b in (ident, U8, SU8, SL8):
        b.lw = msk.lw

    hbuf = [S("hbuf%d" % i, [128, NB]) for i in range(3)]
    hcnt = [0]
    abf = S("abf", [128, KT * NB], BF16)
    sqb = [S("sqb%d" % i, [128, NB], BF16) for i in range(2)]
    rstd = S("rstd", [128, NB])
    raw = {n: S("raw_" + n, [max(r, 1), HIST.get(n, 0) + NB]) for n, r in QROWS}
    for n, r in QROWS:
        if HIST.get(n, 0):
            k.memset(raw[n], raw[n][0:r, 0:HIST[n]], 0.0)
    ft = [S("ft%d" % i, [128, NB]) for i in range(14)]
    _si = 0
    for kt in range(KT):
        for c0 in range(0, NR, NB):
            c1 = min(NR, c0 + NB)
            stg = ft[_si % 14]
            _si += 1
            P.dma("sp", stg, stg[:, 0:c1 - c0], win, win[:, kt * NR + c0:kt * NR + c1])
            k.copy(wsb, wsb[:, kt * NR + c0:kt * NR + c1], stg, stg[:, 0:c1 - c0], eng=("pool" if _si % 2 else "act"))
    rows_t = [S("row%d" % i, [1, NB]) for i in range(5)]
    rep = [S("rep%d" % i, [128, NB]) for i in range(6)]
    dect = [S("dec%d" % i, [64, NB]) for i in range(3)]
    cols = S("cols", [64, 16])
    osb = S("osb", [64, NB])
    states = {m: [S("st_%s%d" % (m, i), [128, 64], BF16) for i in range(2)] for m in mixers}
    st32 = {m: S("st32_%s" % m, [128, 64]) for m in mixers if m in ("ret", "ssd")}
    sidx = {m: 0 for m in mixers}
    for m in mixers:
        k.memset(states[m][0], states[m][0][:, :], 0.0)
    for m in st32:
        k.memset(st32[m], st32[m][:, :], 0.0)
    negA = S("negA", [128, 4])
    k.act(negA, negA[:, 0:1], cst, C("g_alog"), AF.Exp)
    k.act(negA, negA[:, 1:2], cst, C("m_alog"), AF.Exp)
    k.ts(negA, negA[:, 0:2], negA, negA[:, 0:2], -1.0, None, ALU.mult)
    omu = S("omu", [128, 16])
    for j, mun in enumerate(("mu_r", "mu_k", "mu_v", "mu_wl", "mu_al", "mu_gl")):
        k.ts(omu, omu[:, j:j + 1], cst, C(mun), -1.0, 1.0, ALU.mult, ALU.add)
    k.ts(omu, omu[:, 6:7], cst, C("w0"), -1.0, None, ALU.mult)
    k.ts(omu, omu[:, 7:8], cst, C("k_a"), -1.0, 1.0, ALU.mult, ALU.add)
    k.memset(omu, omu[:, 8:9], -0.5, eng="dve")
    onesf = mview("ones")
    onesf.lw = msk.lw
    reset = mview("reset")
    reset.lw = msk.lw

    def rsqrt_to(ob, oap, ib, iap, eps, scale=1.0):
        k.act(ob, oap, ib, iap, AF.Ln, bias=eps, scale=scale)
        k.act(ob, oap, ob, oap, AF.Exp, scale=-0.5)

    def conv_silu(dst, src, rows, wname, bias=None, out=None):
        wo = CST[wname][0]
        k.ts(dst, dst[0:rows, :], src, src[0:rows, 0:NB], cst[0:rows, wo:wo + 1], None, ALU.mult, sb=[cst])
        for i in range(1, 4):
            k.stt(dst, dst[0:rows, :], src, src[0:rows, i:i + NB], cst[0:rows, wo + i:wo + i + 1],
                  dst, dst[0:rows, :], ALU.mult, ALU.add, sb=[cst])
        out = out or dst
        if bias is None:
            k.act(out, out[0:rows, :], dst, dst[0:rows, :], AF.Silu)
        else:
            k.act(out, out[0:rows, :], dst, dst[0:rows, :], AF.Silu, bias=C(bias, rows), sb=[cst])
        k.copy(src, src[0:rows, 0:3], src, src[0:rows, NB:NB + 3], eng="act")

    def replicate(dst, row, rows=128):
        pb = P.ps()
        k.mm(pb, pb[0:rows, :], onesf, onesf[0:1, 0:rows], row, row[0:1, :])
        k.copy(dst, dst[0:rows, :], pb, pb[0:rows, :])

    def to_cols(j, row):
        pb = P.ps()
        for c in range(NCH):
            k.mm(pb, pb[0:64, c:c + 1], row, row[0:1, c * CH:(c + 1) * CH], onesf, onesf[0:1, 0:1])
        k.copy(cols, cols[0:64, j * 8:(j + 1) * 8], pb, pb[0:64, 0:8])

    def l2norm(dst, src, rows, scale):
        sq = ft[13]
        sqh = sqb[0]
        k.tt(sqh, sqh[0:rows, :], src, src[0:rows, :], src, src[0:rows, :], ALU.mult)
        pb = P.ps()
        k.mm(pb, pb[0:rows, :], ones_bf, ones_bf[0:rows, 0:rows], sqh, sqh[0:rows, :])
        rsqrt_to(sq, sq[0:rows, :], pb, pb[0:rows, :], EPS)
        k.stt(dst, dst[0:rows, :], src, src[0:rows, :], scale, sq, sq[0:rows, :], ALU.mult, ALU.mult)

    def scalar_decay(la, dk, delta):
        g, gp = rows_t[3], rows_t[4]
        k.P.op("dve", lambda e: e.tensor_tensor_scan(g[0:1, :], reset[0:1, :], la[0:1, :], 0.0, ALU.mult, ALU.add),
               reads=[reset, la], writes=[g])
        Grep, EG, Etail = rep[0], rep[1], rep[2]
        replicate(Grep, g)
        to_cols(0, g)
        k.act(EG, EG[:, :], Grep, Grep[:, :], AF.Exp)
        k.tt(Etail, c3(Etail[:, :]), Grep, Grep[:, 63:NB:64].unsqueeze(2).to_broadcast([128, NCH, CH]),
             Grep, c3(Grep[:, :]), ALU.subtract)
        k.act(Etail, Etail[:, :], Etail, Etail[:, :], AF.Exp)
        d0 = dect[0]
        gcolB = cols[0:64, 0:8].unsqueeze(2).to_broadcast([64, NCH, CH])
        k.tt(d0, c3(d0[0:64, :]), Grep, c3(Grep[0:64, :]), cols, gcolB, ALU.subtract)
        k.ts(d0, d0[0:64, :], d0, d0[0:64, :], 0.0, None, ALU.min)
        k.act(d0, d0[0:64, :], d0, d0[0:64, :], AF.Exp)
        k.tt(d0, d0[0:64, :], d0, d0[0:64, :], U8, U8[0:64, :], ALU.mult)
        res = {"Grep": Grep, "EG": EG, "Etail": Etail, "dec0T": d0}
        if delta:
            k.tt(gp, gp[0:1, :], g, g[0:1, :], la, la[0:1, :], ALU.subtract)
            Gprep, EGp = rep[3], rep[4]
            replicate(Gprep, gp)
            to_cols(1, gp)
            k.act(EGp, EGp[:, :], Gprep, Gprep[:, :], AF.Exp)
            d1T, d1 = dect[1], dect[2]
            k.tt(d1T, c3(d1T[0:64, :]), Gprep, c3(Gprep[0:64, :]), cols, gcolB, ALU.subtract)
            k.ts(d1T, d1T[0:64, :], d1T, d1T[0:64, :], 0.0, None, ALU.min)
            k.act(d1T, d1T[0:64, :], d1T, d1T[0:64, :], AF.Exp)
            k.tt(d1T, d1T[0:64, :], d1T, d1T[0:64, :], SU8, SU8[0:64, :], ALU.mult)
            gpcolB = cols[0:64, 8:16].unsqueeze(2).to_broadcast([64, NCH, CH])
            k.tt(d1, c3(d1[0:64, :]), cols, gpcolB, Grep, c3(Grep[0:64, :]), ALU.subtract)
            k.ts(d1, d1[0:64, :], d1, d1[0:64, :], 0.0, None, ALU.min)
            k.act(d1, d1[0:64, :], d1, d1[0:64, :], AF.Exp)
            k.tt(d1, d1[0:64, :], d1, d1[0:64, :], SL8, SL8[0:64, :], ALU.mult)
            res.update({"EGp": EGp, "dec1T": d1T, "dec1": d1})
        return res

    def seg_rms(blk):
        t0 = blk * NB
        pss = P.ps()
        for kt in range(KT):
            hb = hbuf[hcnt[0] % 3]
            hcnt[0] += 1
            hsb_, hap_ = hsrc(kt, blk)
            P.dma("sp", hb, hb[:, :], hsb_, hap_)
            sq = sqb[kt % 2]
            k.act(sq, sq[:, :], hb, hb[:, :], AF.Square)
            k.mm(pss, pss[:, :], ones_bf, ones_bf[:, :], sq, sq[:, :], start=(kt == 0), stop=(kt == KT - 1))
        rsqrt_to(rstd, rstd[:, :], pss, pss[:, :], EPS, scale=1.0 / D_MODEL)
        for kt in range(KT):
            hb = hbuf[hcnt[0] % 3]
            hcnt[0] += 1
            hsb_, hap_ = hsrc(kt, blk)
            P.dma("sp", hb, hb[:, :], hsb_, hap_)
            k.stt(abf, abf[:, kt * NB:(kt + 1) * NB], hb, hb[:, :],
                  C("n1w", 128, kt, kt + 1), rstd, rstd[:, :], ALU.mult, ALU.mult, sb=[cst])
    def seg_inproj(blk):
        t0 = blk * NB
        need = []
        if "gdn" in mixers:
            need += ["gq", "gk", "gv", "gb", "ga"]
        if "ret" in mixers:
            need += ["rq", "rk", "rv"]
        if "ssd" in mixers:
            need += ["mx", "mB", "mC", "mdt"]
        if "rwkv" in mixers:
            need += ["wr", "wk", "wv", "wl", "al", "gl"]
        for n in need:
            off, r = QOFF[n]
            pb = P.ps()
            for kt in range(KT):
                k.mm(pb, pb[0:r, :], wsb, wsb[:, kt * NR + off:kt * NR + off + r], abf, abf[:, kt * NB:(kt + 1) * NB],
                     start=(kt == 0), stop=(kt == KT - 1))
            h = HIST.get(n, 0)
            k.copy(raw[n], raw[n][0:r, h:h + NB], pb, pb[0:r, :])

    def seg_mix(blk):
        t0 = blk * NB
        if "gdn" in mixers:
            qc, kc, vc, qn, kn = ft[0], ft[1], fb[2], fb[0], fb[1]
            conv_silu(qc, raw["gq"], 128, "gcq")
            conv_silu(kc, raw["gk"], 128, "gck")
            conv_silu(ft[2], raw["gv"], 64, "gcv", out=vc)
            l2norm(qn, qc, 128, 128 ** -0.5)
            l2norm(kn, kc, 128, 1.0)
            beta, la, ba = rows_t[0], rows_t[1], rows_t[2]
            k.act(beta, beta[0:1, :], raw["gb"], raw["gb"][0:1, :], AF.Sigmoid)
            k.act(la, la[0:1, :], raw["ga"], raw["ga"][0:1, :], AF.Exp, bias=C("g_dtb", 1), sb=[cst])
            k.act(la, la[0:1, :], la, la[0:1, :], AF.Ln, bias=1.0)
            k.ts(la, la[0:1, :], la, la[0:1, :], negA[0:1, 0:1], None, ALU.mult, sb=[negA])
            k.act(ba, ba[0:1, :], la, la[0:1, :], AF.Exp)
            k.tt(ba, ba[0:1, :], ba, ba[0:1, :], beta, beta[0:1, :], ALU.mult)
            sd = scalar_decay(la, 128, True)
            Brep, BArep = rep[5], ft[5]
            replicate(Brep, beta)
            replicate(BArep, ba)
            Bs, K2s, rd, kkd, kTf, bTf = fb[3], fb[4], fb[5], fb[6], fb[7], fb[8]
            k.tt(Bs, Bs[:, :], kn, kn[:, :], BArep, BArep[:, :], ALU.mult)
            k.tt(K2s, K2s[:, :], kn, kn[:, :], Brep, Brep[:, :], ALU.mult)
            k.tt(rd, rd[:, :], qn, qn[:, :], sd["EG"], sd["EG"][:, :], ALU.mult)
            k.tt(kkd, kkd[:, :], kn, kn[:, :], sd["EGp"], sd["EGp"][:, :], ALU.mult)
            k.tt(kTf, kTf[:, :], K2s, K2s[:, :], sd["Etail"], sd["Etail"][:, :], ALU.mult)
            k.tt(bTf, bTf[:, :], Bs, Bs[:, :], sd["Etail"], sd["Etail"][:, :], ALU.mult)
            yb = ybank[0]
            sidx["gdn"] = core.block(128, True, qn, K2s, vc, rd, kTf, sd["EG"], sd["dec0T"], states["gdn"], sidx["gdn"], yb,
                                     KKs=kn, Bs=Bs, kkd=kkd, bTf=bTf, dec1T=sd["dec1T"], dec1=sd["dec1"])
            k.copy(osb, osb[0:64, :], yb, yb[0:64, :], eng="act")
            ob_, oa_ = omdst(0, blk)
            P.dma("sp", ob_, oa_, osb, osb[0:64, :], sem_buf=osb)


        if "ret" in mixers:
            cs_t, sn_t = ft[0], ft[1]
            P.dma("sp", cs_t, cs_t[0:64, :], rotd, rotd[:, t0:t0 + NB])
            P.dma("sp", sn_t, sn_t[0:64, :], rotd, rotd[:, T + t0:T + t0 + NB])
            rotJ = mview("rotJ")
            rotJ.lw = msk.lw
            qr, kr = fb[0], fb[1]
            for (dst, src, sc) in ((qr, raw["rq"], 1.0), (kr, raw["rk"], 0.125)):
                pj = P.ps()
                k.mm(pj, pj[0:64, :], rotJ, rotJ[0:64, 0:64], src, src[0:64, :])
                k.stt(ft[4], ft[4][0:64, :], pj, pj[0:64, :], sc, sn_t, sn_t[0:64, :], ALU.mult, ALU.mult)
                k.stt(ft[5], ft[5][0:64, :], src, src[0:64, :], sc, cs_t, cs_t[0:64, :], ALU.mult, ALU.mult)
                k.tt(dst, dst[0:64, :], ft[5], ft[5][0:64, :], ft[4], ft[4][0:64, :], ALU.add)
            la = rows_t[1]
            k.ts(la, la[0:1, :], reset, reset[0:1, :], 0.0, C("r_lg", 1), ALU.mult, ALU.add, sb=[cst])
            sd = scalar_decay(la, 64, False)
            rd, kTf, rvb = fb[3], fb[4], fb[2]
            k.tt(rd, rd[0:64, :], qr, qr[0:64, :], sd["EG"], sd["EG"][0:64, :], ALU.mult)
            k.tt(kTf, kTf[0:64, :], kr, kr[0:64, :], sd["Etail"], sd["Etail"][0:64, :], ALU.mult)
            k.copy(rvb, rvb[0:64, :], raw["rv"], raw["rv"][0:64, :], eng="pool")
            yb = ybank[1]
            sidx["ret"] = core.block(64, False, qr, kr, rvb, rd, kTf, sd["EG"], sd["dec0T"], states["ret"], sidx["ret"], yb,
                                     S32=st32["ret"])
            k.copy(osb, osb[0:64, :], yb, yb[0:64, :], eng="act")
            ob_, oa_ = omdst(1, blk)
            P.dma("sp", ob_, oa_, osb, osb[0:64, :], sem_buf=osb)

        if "ssd" in mixers:
            xc, Bc, Cc = fb[0], ft[1], fb[1]
            conv_silu(ft[0], raw["mx"], 64, "mcx", bias="mbx", out=xc)
            conv_silu(Bc, raw["mB"], 128, "mcB", bias="mbB")
            conv_silu(ft[2], raw["mC"], 128, "mcC", bias="mbC", out=Cc)
            dt, la = rows_t[0], rows_t[1]
            k.act(dt, dt[0:1, :], raw["mdt"], raw["mdt"][0:1, :], AF.Exp, bias=C("m_dtb", 1), sb=[cst])
            k.act(dt, dt[0:1, :], dt, dt[0:1, :], AF.Ln, bias=1.0)
            k.ts(la, la[0:1, :], dt, dt[0:1, :], negA[0:1, 1:2], None, ALU.mult, sb=[negA])
            sd = scalar_decay(la, 128, False)
            DTrep = rep[5]
            replicate(DTrep, dt)
            K2s, rd, kTf = fb[2], fb[3], fb[4]
            k.tt(K2s, K2s[:, :], Bc, Bc[:, :], DTrep, DTrep[:, :], ALU.mult)
            k.tt(rd, rd[:, :], Cc, Cc[:, :], sd["EG"], sd["EG"][:, :], ALU.mult)
            k.tt(kTf, kTf[:, :], K2s, K2s[:, :], sd["Etail"], sd["Etail"][:, :], ALU.mult)
            yb = ybank[0]
            sidx["ssd"] = core.block(128, False, Cc, K2s, xc, rd, kTf, sd["EG"], sd["dec0T"], states["ssd"], sidx["ssd"], yb,
                                     S32=st32["ssd"])
            k.stt(osb, osb[0:64, :], xc, xc[0:64, :], C("m_d", 64), yb, yb[0:64, :], ALU.mult, ALU.add, sb=[cst])
            ob_, oa_ = omdst(2, blk)
            P.dma("sp", ob_, oa_, osb, osb[0:64, :], sem_buf=osb)

        if "rwkv" in mixers:
            mixed = {}
            for (nm, mun, j, r_, dst) in (("wr", "mu_r", 0, 64, ft[0]), ("wk", "mu_k", 1, 64, ft[1]), ("wv", "mu_v", 2, 64, fb[0]),
                                          ("wl", "mu_wl", 3, 32, ft[3]), ("al", "mu_al", 4, 32, ft[4]), ("gl", "mu_gl", 5, 96, ft[5])):
                src = raw[nm]
                k.ts(ft[13], ft[13][0:r_, :], src, src[0:r_, 0:NB], C(mun, r_), None, ALU.mult, sb=[cst])
                k.stt(dst, dst[0:r_, :], src, src[0:r_, 1:1 + NB], omu[0:r_, j:j + 1], ft[13], ft[13][0:r_, :],
                      ALU.mult, ALU.add, sb=[omu])
                k.copy(src, src[0:r_, 0:1], src, src[0:r_, NB:NB + 1], eng="act")
            r, kx, v, tw, al, gl = ft[0], ft[1], fb[0], ft[3], ft[4], ft[5]
            k.act(tw, tw[0:32, :], tw, tw[0:32, :], AF.Tanh)
            pz = P.ps()
            k.mm(pz, pz[0:64, :], cst, C("w_up", 32), tw, tw[0:32, :])
            lw = ft[6]
            k.act(lw, lw[0:64, :], pz, pz[0:64, :], AF.Exp, bias=omu[0:64, 6:7], scale=-1.0, sb=[omu])
            k.act(lw, lw[0:64, :], lw, lw[0:64, :], AF.Ln, bias=1.0)
            k.act(lw, lw[0:64, :], lw, lw[0:64, :], AF.Exp, bias=omu[0:64, 8:9], scale=-1.0, sb=[omu])
            k.ts(lw, lw[0:64, :], lw, lw[0:64, :], -1.0, None, ALU.mult)
            gW, gWp = ft[7], ft[8]
            P.op("dve", lambda e: e.tensor_tensor_scan(gW[0:64, :], reset[0:64, :], lw[0:64, :], 0.0, ALU.mult, ALU.add),
                 reads=[reset, lw], writes=[gW])
            k.tt(gWp, gWp[0:64, :], gW, gW[0:64, :], lw, lw[0:64, :], ALU.subtract)
            k.act(gWp, gWp[0:64, :], gWp, gWp[0:64, :], AF.Exp)
            pa = P.ps()
            k.mm(pa, pa[0:64, :], cst, C("a_up", 32), al, al[0:32, :])
            a = ft[3]
            k.act(a, a[0:64, :], pa, pa[0:64, :], AF.Sigmoid, bias=C("a0", 64), sb=[cst])
            k.act(gl, gl[0:96, :], gl, gl[0:96, :], AF.Sigmoid)
            pg = P.ps()
            k.mm(pg, pg[0:64, :], cst, C("g_up", 96), gl, gl[0:96, :])
            gg = ft[4]
            k.copy(gg, gg[0:64, :], pg, pg[0:64, :], eng="act")
            kk = ft[5]
            k.ts(kk, kk[0:64, :], kx, kx[0:64, :], C("k_k", 64), None, ALU.mult, sb=[cst])
            l2norm(kk, kk, 64, 1.0)
            k2, b = ft[9], ft[10]
            k.ts(k2, k2[0:64, :], a, a[0:64, :], C("k_a", 64), omu[0:64, 7:8], ALU.mult, ALU.add, sb=[cst, omu])
            k.tt(k2, k2[0:64, :], k2, k2[0:64, :], kx, kx[0:64, :], ALU.mult)
            k.tt(b, b[0:64, :], a, a[0:64, :], kk, kk[0:64, :], ALU.mult)
            EGW, EI, Etail = ft[11], ft[12], rep[0]
            k.act(EGW, EGW[0:64, :], gW, gW[0:64, :], AF.Exp)
            k.act(EI, EI[0:64, :], gW, gW[0:64, :], AF.Exp, scale=-1.0)
            k.tt(Etail, c3(Etail[0:64, :]), gW, gW[0:64, 63:NB:64].unsqueeze(2).to_broadcast([64, NCH, CH]),
                 gW, c3(gW[0:64, :]), ALU.subtract)
            k.act(Etail, Etail[0:64, :], Etail, Etail[0:64, :], AF.Exp)
            rd, kkd, Bs, K2s, kTf, bTf = fb[1], fb[2], fb[3], fb[4], fb[5], fb[6]
            k.tt(rd, rd[0:64, :], r, r[0:64, :], EGW, EGW[0:64, :], ALU.mult)
            k.tt(kkd, kkd[0:64, :], kk, kk[0:64, :], gWp, gWp[0:64, :], ALU.mult)
            k.tt(Bs, Bs[0:64, :], b, b[0:64, :], EI, EI[0:64, :], ALU.mult)
            k.tt(K2s, K2s[0:64, :], k2, k2[0:64, :], EI, EI[0:64, :], ALU.mult)
            k.tt(kTf, kTf[0:64, :], k2, k2[0:64, :], Etail, Etail[0:64, :], ALU.mult)
            k.tt(bTf, bTf[0:64, :], b, b[0:64, :], Etail, Etail[0:64, :], ALU.mult)
            yb = ybank[1]
            sidx["rwkv"] = core.block(64, True, rd, K2s, v, rd, kTf, EGW, U8, states["rwkv"], sidx["rwkv"], yb,
                                      KKs=kkd, Bs=Bs, kkd=kkd, bTf=bTf, dec1T=SU8, dec1=SL8)
            y, yc, t1 = ft[6], ft[7], ft[8]
            k.copy(y, y[0:64, :], yb, yb[0:64, :], eng="act")
            pm = P.ps()
            k.mm(pm, pm[0:64, :], onesf, onesf[0:64, 0:64], y, y[0:64, :])
            k.stt(yc, yc[0:64, :], pm, pm[0:64, :], -1.0 / 64, y, y[0:64, :], ALU.mult, ALU.add)
            k.tt(t1, t1[0:64, :], yc, yc[0:64, :], yc, yc[0:64, :], ALU.mult)
            pv = P.ps()
            k.mm(pv, pv[0:64, :], onesf, onesf[0:64, 0:64], t1, t1[0:64, :])
            rsqrt_to(t1, t1[0:64, :], pv, pv[0:64, :], 64e-5, scale=1.0 / 64)
            k.tt(yc, yc[0:64, :], yc, yc[0:64, :], t1, t1[0:64, :], ALU.mult)
            k.ts(yc, yc[0:64, :], yc, yc[0:64, :], C("ln_w", 64), C("ln_b", 64), ALU.mult, ALU.add, sb=[cst])
            k.tt(t1, t1[0:64, :], r, r[0:64, :], k2, k2[0:64, :], ALU.mult)
            k.ts(t1, t1[0:64, :], t1, t1[0:64, :], C("r_k", 64), None, ALU.mult, sb=[cst])
            pb_ = P.ps()
            k.mm(pb_, pb_[0:64, :], onesf, onesf[0:64, 0:64], t1, t1[0:64, :])
            k.tt(t1, t1[0:64, :], pb_, pb_[0:64, :], v, v[0:64, :], ALU.mult)
            k.tt(yc, yc[0:64, :], yc, yc[0:64, :], t1, t1[0:64, :], ALU.add)
            k.tt(osb, osb[0:64, :], yc, yc[0:64, :], gg, gg[0:64, :], ALU.mult)
            ob_, oa_ = omdst(3, blk)
            P.dma("sp", ob_, oa_, osb, osb[0:64, :], sem_buf=osb)

    seg_rms(0)
    seg_inproj(0)
    for blk in range(nblk):
        if blk + 1 < nblk:
            seg_rms(blk + 1)
        seg_mix(blk)
        if blk + 1 < nblk:
            seg_inproj(blk + 1)
    P.rot = []


GDN_OFF, RET_OFF, M2_OFF, RW_OFF = 0, 2056, 3592, 5136


def core_cols(c):
    hg, half = c // 2, c % 2
    grp = c // 4
    cols = {}
    cols["gq"] = np.arange(GDN_OFF + hg * 128, GDN_OFF + hg * 128 + 128)
    cols["gk"] = np.arange(GDN_OFF + 512 + hg * 128, GDN_OFF + 512 + hg * 128 + 128)
    cols["gv"] = np.arange(GDN_OFF + 1024 + hg * 128 + half * 64, GDN_OFF + 1024 + hg * 128 + half * 64 + 64)
    cols["gb"] = np.array([GDN_OFF + 2048 + hg])
    cols["ga"] = np.array([GDN_OFF + 2052 + hg])
    cols["rq"] = np.arange(RET_OFF + hg * 64, RET_OFF + hg * 64 + 64)
    cols["rk"] = np.arange(RET_OFF + 256 + hg * 64, RET_OFF + 256 + hg * 64 + 64)
    cols["rv"] = np.arange(RET_OFF + 512 + hg * 128 + half * 64, RET_OFF + 512 + hg * 128 + half * 64 + 64)
    cols["mx"] = np.arange(M2_OFF + 512 + c * 64, M2_OFF + 512 + c * 64 + 64)
    cols["mB"] = np.arange(M2_OFF + 1024 + grp * 128, M2_OFF + 1024 + grp * 128 + 128)
    cols["mC"] = np.arange(M2_OFF + 1280 + grp * 128, M2_OFF + 1280 + grp * 128 + 128)
    cols["mdt"] = np.array([M2_OFF + 1536 + c])
    cols["wr"] = np.arange(RW_OFF + c * 64, RW_OFF + c * 64 + 64)
    cols["wk"] = np.arange(RW_OFF + 512 + c * 64, RW_OFF + 512 + c * 64 + 64)
    cols["wv"] = np.arange(RW_OFF + 1024 + c * 64, RW_OFF + 1024 + c * 64 + 64)
    cols["wl"] = np.arange(RW_OFF + 1536, RW_OFF + 1568)
    cols["al"] = np.arange(RW_OFF + 1568, RW_OFF + 1600)
    cols["gl"] = np.arange(RW_OFF + 1600, RW_OFF + 1696)
    return cols


def const_masks():
    m = np.zeros((128, NMSK), np.float32)
    o = MSK["ident"][0]
    m[:, o:o + 128] = np.eye(128, dtype=np.float32)
    i = np.arange(64)
    U = (i[:, None] <= i[None, :]).astype(np.float32)
    SU = (i[:, None] < i[None, :]).astype(np.float32)
    SL = (i[:, None] > i[None, :]).astype(np.float32)
    for nm, a in (("U8", U), ("SU8", SU), ("SL8", SL)):
        o = MSK[nm][0]
        m[0:64, o:o + NB] = np.tile(a, (1, NCH))
    o = MSK["reset"][0]
    r = np.ones(NB, np.float32)
    r[0::CH] = 0.0
    m[:, o:o + NB] = r[None, :]
    o = MSK["rotJ"][0]
    J = np.zeros((64, 64), np.float32)
    for p in range(32):
        J[2 * p + 1, 2 * p] = -1.0
        J[2 * p, 2 * p + 1] = 1.0
    m[0:64, o:o + 64] = J
    o = MSK["ones"][0]
    m[:, o:o + 128] = 1.0
    return m


def rot_table(T):
    theta = 1.0 / (10000.0 ** np.linspace(0.0, 1.0, 32, dtype=np.float32))
    ang = np.arange(T, dtype=np.float32)[None, :] * np.repeat(theta, 2)[:, None].astype(np.float32)
    return np.concatenate([np.cos(ang), np.sin(ang)], axis=1).astype(np.float32)


def prep_a(inp, l, c):
    hg, half, grp = c // 2, c % 2, c // 4
    cols = core_cols(c)
    allc = np.concatenate([cols[n] for n, _ in QROWS])
    w = inp["w_in"][l][:, allc]
    win = np.ascontiguousarray(w.reshape(KT, 128, NR).transpose(1, 0, 2).reshape(128, KT * NR))
    cst = np.zeros((128, NCST), np.float32)

    def put(name, arr, rows=None):
        o, wd = CST[name]
        arr = np.asarray(arr, np.float32)
        if arr.ndim == 0:
            cst[:, o] = arr
        elif arr.ndim == 1:
            cst[0:arr.shape[0], o] = arr
        else:
            cst[0:arr.shape[0], o:o + arr.shape[1]] = arr
    put("n1w", inp["norm1_w"][l].reshape(KT, 128).T)
    gcw = inp["gdn_conv_w"][l]
    put("gcq", gcw[:, cols["gq"] - GDN_OFF].T)
    put("gck", gcw[:, cols["gk"] - GDN_OFF].T)
    put("gcv", gcw[:, cols["gv"] - GDN_OFF].T)
    put("g_alog", inp["gdn_a_log"][l][hg])
    put("g_dtb", inp["gdn_dt_bias"][l][hg])
    put("r_lg", np.float32(np.log1p(-np.exp2(np.float32(-5.0 - hg)))))
    mcw = inp["m2_conv_w"][l]
    mcb = inp["m2_conv_b"][l]
    for nm, q in (("x", "mx"), ("B", "mB"), ("C", "mC")):
        ci = cols[q] - (M2_OFF + 512)
        put("mc" + nm, mcw[:, ci].T)
        put("mb" + nm, mcb[ci])
    put("m_alog", inp["m2_a_log"][l][c])
    put("m_dtb", inp["m2_dt_bias"][l][c])
    put("m_d", inp["m2_d"][l][c])
    mu = inp["rw_mu"][l]
    for nm, q in (("mu_r", "wr"), ("mu_k", "wk"), ("mu_v", "wv"), ("mu_wl", "wl"), ("mu_al", "al"), ("mu_gl", "gl")):
        put(nm, mu[cols[q] - RW_OFF])
    hs = slice(c * 64, (c + 1) * 64)
    put("w0", inp["rw_w0"][l][hs])
    put("a0", inp["rw_a0"][l][hs])
    put("k_k", inp["rw_k_k"][l][hs])
    put("k_a", inp["rw_k_a"][l][hs])
    put("r_k", inp["rw_r_k"][l][c])
    put("ln_w", inp["rw_ln_w"][l][hs])
    put("ln_b", inp["rw_ln_b"][l][hs])
    put("w_up", inp["rw_w_up"][l][:, hs])
    put("a_up", inp["rw_a_up"][l][:, hs])
    put("g_up", inp["rw_g_up"][l][:, hs])
    return win, cst


CSTB = {}
_o = 0
for _n, _w in [("n1w", 16), ("n2w", 16), ("fnw", 16), ("gnw", 1), ("rnw", 4), ("mnw", 4)]:
    CSTB[_n] = (_o, _w)
    _o += _w
NCSTB = _o
NGT = 12
NFF = D_FF // 128


def build_phase_b(TB, final):
    nc = bass.Bass("TRN2", target_bir_lowering=False)
    P = Prog(nc)
    k = K(P)
    hT = P.dram("hT", [D_MODEL, TB], F32, kind="ExternalInput")
    omx = P.dram("omx", [D_MODEL, TB], F32, kind="ExternalInput")
    wg = P.dram("wg", [NGT, 128, KT * 128], F32, kind="ExternalInput")
    wo = P.dram("wo", [KT, 128, KT * 128], F32, kind="ExternalInput")
    wu = P.dram("wu", [NFF, 128, KT * 128], F32, kind="ExternalInput")
    wd = P.dram("wd", [KT, 128, NFF * 128], F32, kind="ExternalInput")
    cstd = P.dram("cst", [128, NCSTB], F32, kind="ExternalInput")
    h2 = P.dram("dr_h2", [D_MODEL, TB], F32, kind="ExternalOutput")
    yo = P.dram("dr_y", [D_MODEL, TB], F32, kind="ExternalOutput") if final else None
    P.psum_banks(8)

    def omload(o, j, blk):
        P.dma("sp", o, o[:, :], omx, omx[j * 128:(j + 1) * 128, blk * NB:(blk + 1) * NB])
    emit_phase_b(P, k, TB, final, lambda kt, blk: (hT, hT[kt * 128:(kt + 1) * 128, blk * NB:(blk + 1) * NB]), omload,
                 wg, wo, wu, wd, cstd, h2, yo)
    evs = [h2.lw]
    if final:
        evs.append(yo.lw)
    P.wait_all("sp", evs)
    P.emit()
    return nc


def emit_phase_b(P, k, TB, final, hsrc, omload, wg, wo, wu, wd, cstd, h2, yo):
    nblk = TB // NB
    P.rot = []
    S = P.sbuf
    cst = S("cstsb", [128, NCSTB])
    P.dma("sp", cst, cst[:, :], cstd, cstd[:, :])

    def C(name, j0=0, j1=None):
        o, w = CSTB[name]
        if j1 is None:
            j1 = w
        return cst[:, o + j0:o + j1]
    ones_bf = S("ones_bf", [128, 128], BF16)
    k.memset(ones_bf, ones_bf[:, :], 1.0)
    onesf = S("onesf", [128, 128])
    k.memset(onesf, onesf[:, :], 1.0)
    hsb = S("hsb", [128, KT * NB])
    abf = S("abf", [128, KT * NB], BF16)
    obf = S("obf", [128, KT * NB], BF16)
    ubf = S("ubf", [128, NFF * NB], BF16)
    sqb = [S("sqb%d" % i, [128, NB], BF16) for i in range(2)]
    rstd = S("rstd", [128, NB])
    wsl = [S("wsl%d" % i, [128, KT * 128], BF16) for i in range(4)]
    wst = [S("wst%d" % i, [128, KT * 128]) for i in range(4)]
    omt = [S("omt%d" % i, [128, NB]) for i in range(3)]
    gt = [S("gt%d" % i, [128, NB]) for i in range(3)]
    t1 = S("t1", [128, NB])
    t2 = S("t2", [128, NB])
    osb = [S("osb%d" % i, [128, NB]) for i in range(2)]
    cnt = {"w": 0, "wd": 0, "om": 0, "g": 0, "o": 0}

    def rsqrt_to(ob, oap, ib, iap, eps, scale=1.0):
        k.act(ob, oap, ib, iap, AF.Ln, bias=eps, scale=scale)
        k.act(ob, oap, ob, oap, AF.Exp, scale=-0.5)

    def rmsnorm(dst, wname):
        pss = P.ps()
        for kt in range(KT):
            sq = sqb[kt % 2]
            k.act(sq, sq[:, :], hsb, hsb[:, kt * NB:(kt + 1) * NB], AF.Square)
            k.mm(pss, pss[:, :], ones_bf, ones_bf[:, :], sq, sq[:, :], start=(kt == 0), stop=(kt == KT - 1))
        rsqrt_to(rstd, rstd[:, :], pss, pss[:, :], EPS, scale=1.0 / D_MODEL)
        for kt in range(KT):
            k.stt(dst, dst[:, kt * NB:(kt + 1) * NB], hsb, hsb[:, kt * NB:(kt + 1) * NB],
                  C(wname, kt, kt + 1), rstd, rstd[:, :], ALU.mult, ALU.mult, sb=[cst])

    def load_w(src, j, piece=None):
        w = wsl[cnt["w"] % 4]
        st = wst[cnt["w"] % 4]
        cnt["w"] += 1
        if piece is None:
            P.dma("sp", st, st[:, :], src, src[j])
        else:
            P.dma("sp", st, st[:, :], src, src[j][:, piece * KT * 128:(piece + 1) * KT * 128])
        k.copy(w, w[:, :], st, st[:, :], eng=("pool", "dve", "act", "dve")[cnt["w"] % 4])
        return w

    def proj(w, rhs):
        pb = P.ps()
        for kt in range(KT):
            k.mm(pb, pb[:, :], w, w[:, kt * 128:(kt + 1) * 128], rhs, rhs[:, kt * NB:(kt + 1) * NB],
                 start=(kt == 0), stop=(kt == KT - 1))
        return pb

    def load_om(j, t0):
        o = omt[cnt["om"] % 3]
        cnt["om"] += 1
        omload(o, j, t0 // NB)
        return o

    def gate(j):
        w = load_w(wg, j)
        pb = proj(w, abf)
        g = gt[cnt["g"] % 3]
        cnt["g"] += 1
        k.act(g, g[:, :], pb, pb[:, :], AF.Silu)
        return g

    for blk in range(nblk):
        t0 = blk * NB
        for kt in range(KT):
            hb_, ha_ = hsrc(kt, blk)
            P.dma("sp", hsb, hsb[:, kt * NB:(kt + 1) * NB], hb_, ha_)
        rmsnorm(abf, "n1w")
        for j in range(4):
            o = load_om(j, t0)
            g = gate(j)
            k.tt(t1, t1[:, :], o, o[:, :], o, o[:, :], ALU.mult)
            pb = P.ps()
            k.mm(pb, pb[:, :], onesf, onesf[:, :], t1, t1[:, :])
            rsqrt_to(t1, t1[:, :], pb, pb[:, :], EPS, scale=1.0 / 128)
            k.stt(t2, t2[:, :], o, o[:, :], C("gnw"), t1, t1[:, :], ALU.mult, ALU.mult, sb=[cst])
            k.tt(obf, obf[:, j * NB:(j + 1) * NB], t2, t2[:, :], g, g[:, :], ALU.mult)
        for j in range(4):
            o = load_om(4 + j, t0)
            g = gate(4 + j)
            pb = P.ps()
            k.mm(pb, pb[:, :], onesf, onesf[:, :], o, o[:, :])
            k.stt(t2, t2[:, :], pb, pb[:, :], -1.0 / 128, o, o[:, :], ALU.mult, ALU.add)
            k.tt(t1, t1[:, :], t2, t2[:, :], t2, t2[:, :], ALU.mult)
            pb2 = P.ps()
            k.mm(pb2, pb2[:, :], onesf, onesf[:, :], t1, t1[:, :])
            rsqrt_to(t1, t1[:, :], pb2, pb2[:, :], EPS, scale=1.0 / 128)
            k.stt(t2, t2[:, :], t2, t2[:, :], C("rnw", j, j + 1), t1, t1[:, :], ALU.mult, ALU.mult, sb=[cst])
            k.tt(obf, obf[:, (4 + j) * NB:(5 + j) * NB], t2, t2[:, :], g, g[:, :], ALU.mult)
        for gi in range(2):
            ys = []
            pb = P.ps()
            for jj in range(2):
                j = gi * 2 + jj
                o = load_om(8 + j, t0)
                g = gate(8 + j)
                y = osb[jj]
                k.tt(y, y[:, :], o, o[:, :], g, g[:, :], ALU.mult)
                k.tt(t1, t1[:, :], y, y[:, :], y, y[:, :], ALU.mult)
                k.mm(pb, pb[:, :], onesf, onesf[:, :], t1, t1[:, :], start=(jj == 0), stop=(jj == 1))
                ys.append(y)
            rsqrt_to(t1, t1[:, :], pb, pb[:, :], EPS, scale=1.0 / 256)
            for jj in range(2):
                j = gi * 2 + jj
                k.stt(obf, obf[:, (8 + j) * NB:(9 + j) * NB], ys[jj], ys[jj][:, :], C("mnw", j, j + 1),
                      t1, t1[:, :], ALU.mult, ALU.mult, sb=[cst])
        for j in range(4):
            o = load_om(12 + j, t0)
            k.copy(obf, obf[:, (12 + j) * NB:(13 + j) * NB], o, o[:, :])
        for j in range(KT):
            w = load_w(wo, j)
            pb = proj(w, obf)
            k.tt(hsb, hsb[:, j * NB:(j + 1) * NB], hsb, hsb[:, j * NB:(j + 1) * NB], pb, pb[:, :], ALU.add)
        rmsnorm(abf, "n2w")
        for j in range(NFF):
            w = load_w(wu, j)
            pb = proj(w, abf)
            k.act(ubf, ubf[:, j * NB:(j + 1) * NB], pb, pb[:, :], AF.Relu)
            k.tt(ubf, ubf[:, j * NB:(j + 1) * NB], ubf, ubf[:, j * NB:(j + 1) * NB], ubf, ubf[:, j * NB:(j + 1) * NB],
                 ALU.mult, eng="dve")
        for j in range(KT):
            pb = P.ps()
            for piece in range(NFF // KT):
                w = load_w(wd, j, piece)
                for k2_ in range(KT):
                    kk_ = piece * KT + k2_
                    k.mm(pb, pb[:, :], w, w[:, k2_ * 128:(k2_ + 1) * 128], ubf, ubf[:, kk_ * NB:(kk_ + 1) * NB],
                         start=(kk_ == 0), stop=(kk_ == NFF - 1))
            k.tt(hsb, hsb[:, j * NB:(j + 1) * NB], hsb, hsb[:, j * NB:(j + 1) * NB], pb, pb[:, :], ALU.add)
        if h2 is not None:
            for kt in range(KT):
                P.dma("sp", h2, h2[kt * 128:(kt + 1) * 128, t0:t0 + NB], hsb, hsb[:, kt * NB:(kt + 1) * NB], sem_buf=osb[0])
        if final:
            pss = P.ps()
            for kt in range(KT):
                sq = sqb[kt % 2]
                k.act(sq, sq[:, :], hsb, hsb[:, kt * NB:(kt + 1) * NB], AF.Square)
                k.mm(pss, pss[:, :], ones_bf, ones_bf[:, :], sq, sq[:, :], start=(kt == 0), stop=(kt == KT - 1))
            rsqrt_to(rstd, rstd[:, :], pss, pss[:, :], EPS, scale=1.0 / D_MODEL)
            for kt in range(KT):
                y = osb[kt % 2]
                k.stt(y, y[:, :], hsb, hsb[:, kt * NB:(kt + 1) * NB], C("fnw", kt, kt + 1), rstd, rstd[:, :],
                      ALU.mult, ALU.mult, sb=[cst])
                P.dma("sp", yo, yo[kt * 128:(kt + 1) * 128, t0:t0 + NB], y, y[:, :], sem_buf=osb[1])
    return


def prep_b_weights(inp, l):
    def slabs(W, ntile):
        Kd, Md = W.shape
        return np.ascontiguousarray(W.reshape(Kd // 128, 128, Md // 128, 128).transpose(2, 1, 0, 3).reshape(Md // 128, 128, (Kd // 128) * 128))
    w_in = inp["w_in"][l]
    gcols = np.concatenate([np.arange(GDN_OFF + 1536, GDN_OFF + 2048), np.arange(RET_OFF + 1024, RET_OFF + 1536),
                            np.arange(M2_OFF, M2_OFF + 512)])
    wg = slabs(w_in[:, gcols], NGT)
    wo = slabs(inp["w_out"][l], KT)
    wu = slabs(inp["w_ffn_up"][l], NFF)
    wd = slabs(inp["w_ffn_down"][l], KT)
    cst = np.zeros((128, NCSTB), np.float32)
    cst[:, 0:16] = inp["norm1_w"][l].reshape(KT, 128).T
    cst[:, 16:32] = inp["norm2_w"][l].reshape(KT, 128).T
    cst[:, 32:48] = inp["final_norm_w"].reshape(KT, 128).T
    cst[:, 48] = inp["gdn_norm_w"][l]
    cst[:, 49:53] = inp["ret_norm_w"][l].reshape(4, 128).T
    cst[:, 53:57] = inp["m2_norm_w"][l].reshape(4, 128).T
    return {"wg": wg, "wo": wo, "wu": wu, "wd": wd, "cst": cst}


_CACHE = {}
TBK = SEQ // NCORE
OMROWS = (SEQ // NB) * 256


def build_fused(nl=2):
    T, TB = SEQ, TBK
    nc = bass.Bass("TRN2", target_bir_lowering=False)
    P = Prog(nc)
    k = K(P)
    xT = P.dram("xT", [D_MODEL, T], F32, kind="ExternalInput")
    xown = P.dram("xown", [D_MODEL, TB], F32, kind="ExternalInput")
    mskd = P.dram("msk", [128, NMSK], F32, kind="ExternalInput")
    rotd = P.dram("rot", [64, 2 * T], F32, kind="ExternalInput")
    idxd = P.dram("idx", [128, 32], mybir.dt.uint32, kind="ExternalInput")
    L = {}
    for l in range(2):
        L[l] = dict(
            win=P.dram("win%d" % l, [128, KT * NR], F32, kind="ExternalInput"),
            cst=P.dram("cst%d" % l, [128, NCST], F32, kind="ExternalInput"),
            wg=P.dram("wg%d" % l, [NGT, 128, KT * 128], F32, kind="ExternalInput"),
            wo=P.dram("wo%d" % l, [KT, 128, KT * 128], F32, kind="ExternalInput"),
            wu=P.dram("wu%d" % l, [NFF, 128, KT * 128], F32, kind="ExternalInput"),
            wd=P.dram("wd%d" % l, [KT, 128, NFF * 128], F32, kind="ExternalInput"),
            cstb=P.dram("cstb%d" % l, [128, NCSTB], F32, kind="ExternalInput"),
            om_loc=P.dram("om_loc%d" % l, [OMROWS, NB], F32),
            om_all=P.dram("om_all%d" % l, [NCORE * OMROWS, NB], F32),
        )
    h_loc = P.dram("h_loc", [D_MODEL, TB], F32)
    h_all = P.dram("h_all", [NCORE * D_MODEL, TB], F32)
    yo = P.dram("dr_y", [D_MODEL, TB], F32, kind="ExternalOutput")
    P.psum_banks(8)
    P.use_arena(53000)

    for l in range(nl):
        d = L[l]
        if l == 0:
            hsrc = lambda kt, blk: (xT, xT[kt * 128:(kt + 1) * 128, blk * NB:(blk + 1) * NB])
        else:
            def hsrc(kt, blk):
                r, b = blk // 2, blk % 2
                return (h_all, h_all[r * D_MODEL + kt * 128:r * D_MODEL + (kt + 1) * 128, b * NB:(b + 1) * NB])
        om_loc, om_all = d["om_loc"], d["om_all"]
        P.phase_reset()
        emit_phase_a(P, k, T, hsrc, d["win"], d["cst"], mskd, rotd,
                     lambda j, blk, om_loc=om_loc: (om_loc, om_loc[blk * 256 + j * 64:blk * 256 + (j + 1) * 64, :]))
        P.allgather(om_loc, om_all, NCORE)
        P.phase_reset()
        idx = P.sbuf("idxsb", [128, 32], mybir.dt.uint32)
        P.dma("sp", idx, idx[:, :], idxd, idxd[:, :])

        def omload(o, j, blk, om_all=om_all, idx=idx):
            P.gather(o, o[:, :], om_all, om_all[:, :], idx, idx[:, j * 2 + blk:j * 2 + blk + 1])
        if l == 0:
            hown = lambda kt, blk: (xown, xown[kt * 128:(kt + 1) * 128, blk * NB:(blk + 1) * NB])
        else:
            hown = lambda kt, blk: (h_loc, h_loc[kt * 128:(kt + 1) * 128, blk * NB:(blk + 1) * NB])
        final = (l == nl - 1)
        emit_phase_b(P, k, TB, final, hown, omload, d["wg"], d["wo"], d["wu"], d["wd"], d["cstb"],
                     None if final else h_loc, yo if final else None)
        if not final:
            P.allgather(h_loc, h_all, NCORE)
    P.wait_all("sp", [yo.lw])
    P.emit()
    return nc


def idx_table(c):
    idx = np.zeros((128, 32), np.uint32)
    p = np.arange(128)
    for j in range(16):
        slot, jj = j // 4, j % 4
        rank = np.where(p < 64, 2 * jj, 2 * jj + 1)
        for b in range(2):
            idx[:, j * 2 + b] = rank * OMROWS + (c * 2 + b) * 256 + slot * 64 + (p % 64)
    return idx


def kernel(**inp):
    inp = {k_: np.asarray(v) for k_, v in inp.items()}
    T, TB = SEQ, TBK
    msk = const_masks()
    rot = rot_table(T)
    xT = np.ascontiguousarray(inp["x"][0].T)
    cores = list(range(NCORE))
    if "fused" not in _CACHE:
        _CACHE["fused"] = build_fused()
    nc = _CACHE["fused"]
    wb = [prep_b_weights(inp, l) for l in range(2)]
    maps = []
    for c in cores:
        m = {"xT": xT, "xown": np.ascontiguousarray(xT[:, c * TB:(c + 1) * TB]), "msk": msk, "rot": rot,
             "idx": idx_table(c)}
        for l in range(2):
            win, cst = prep_a(inp, l, c)
            m["win%d" % l] = win
            m["cst%d" % l] = cst
            m["wg%d" % l] = wb[l]["wg"]
            m["wo%d" % l] = wb[l]["wo"]
            m["wu%d" % l] = wb[l]["wu"]
            m["wd%d" % l] = wb[l]["wd"]
            m["cstb%d" % l] = wb[l]["cst"]
        maps.append(m)
    res = run_bass_kernel_spmd(nc, maps, core_ids=cores)
    out = np.concatenate([res.results[c]["dr_y"] for c in cores], axis=1)
    return np.ascontiguousarray(out.T)[None].astype(np.float32)
```

```python
import math
import numpy as np
from contextlib import ExitStack
import concourse.bass as bass
import concourse.mybir as mybir
from concourse.bass_utils import run_bass_kernel_spmd

F32 = mybir.dt.float32
BF16 = mybir.dt.bfloat16
ALU = mybir.AluOpType
AF = mybir.ActivationFunctionType

ENGS = ("pe", "act", "dve", "pool", "sp")
SEM_ROT = 30000

D_MODEL = 2048
SEQ = 8192
NCORE = 8
EPS = 1e-6
NB = 512
CH = 64
NCH = NB // CH
KT = D_MODEL // 128
D_FF = 4 * D_MODEL


class Buf:
    __slots__ = ("name", "t", "lw", "rd", "dsem", "dcnt")

    def __init__(self, name, t):
        self.name = name
        self.t = t
        self.lw = None
        self.rd = []
        self.dsem = None
        self.dcnt = 0

    def __getitem__(self, idx):
        return self.t[idx]


class Prog:
    def __init__(self, nc):
        self.nc = nc
        self.es = ExitStack()
        self.ops = {e: [] for e in ENGS}
        self.cnt = {e: 0 for e in ENGS}
        self.gen = {e: 0 for e in ENGS}
        self.known = {e: {} for e in ENGS}
        self.sems = {}
        self.nbuf = 0
        self.npsum = 0
        self.psums = []
        self.rot = []
        self.arena = None
        self.aoff = 0
        self.barrier = []
        self.dsems = {}

    def sem(self, key):
        if key not in self.sems:
            nm = "s_" + "_".join(str(k) for k in key)
            self.sems[key] = self.es.enter_context(self.nc.semaphore(nm))
        return self.sems[key]

    def use_arena(self, nf32):
        self.arena = self.es.enter_context(self.nc.sbuf_tensor("arena", [128, nf32], F32))
        self.asize = nf32

    def sbuf(self, name, shape, dtype=F32):
        if self.arena is None:
            t = self.es.enter_context(self.nc.sbuf_tensor(name, list(shape), dtype))
            return Buf(name, t)
        rows, cols = shape
        esz = 2 if dtype == BF16 else 4
        n32 = (cols * esz + 3) // 4
        assert self.aoff + n32 <= self.asize, ("arena overflow", name, self.aoff, n32)
        ap = self.arena[0:rows, self.aoff:self.aoff + n32]
        if dtype != F32:
            ap = ap.bitcast(dtype)[:, 0:cols]
        self.aoff += n32
        b = Buf(name, ap)
        b.rd = list(self.barrier)
        return b

    def phase_reset(self):
        evs = [((e, self.gen[e]), self.cnt[e]) for e in ENGS if self.cnt[e] > 0]
        evs += list(self.dsems.items())
        self.barrier = evs
        self.aoff = 0

    def psum_banks(self, n=8):
        for i in range(n):
            t = self.es.enter_context(self.nc.psum_tensor("psb%d" % i, [128, 512], F32))
            self.psums.append(Buf("psb%d" % i, t))

    def ps(self):
        pool = self.rot if self.rot else self.psums
        b = pool[self.npsum % len(pool)]
        self.npsum += 1
        return b

    def dram(self, name, shape, dtype=F32, kind=None):
        if kind is None:
            t = self.nc.dram_tensor(name, list(shape), dtype)
        else:
            t = self.nc.dram_tensor(name, list(shape), dtype, kind=kind)
        return Buf(name, t.ap())

    def _deps(self, eng, reads, writes):
        evs = []
        for b in reads:
            if b.lw is not None:
                evs.append(b.lw)
            if b.name.startswith("psb"):
                evs.extend(b.rd)
        for b in writes:
            if b.lw is not None:
                evs.append(b.lw)
            evs.extend(b.rd)
        need = {}
        kn = self.known[eng]
        for (k, v) in evs:
            if kn.get(k, 0) >= v:
                continue
            if need.get(k, 0) < v:
                need[k] = v
        for k, v in need.items():
            kn[k] = v
        return list(need.items())

    def _record(self, ev, reads, writes):
        for b in writes:
            b.lw = ev
            b.rd = []
        for b in reads:
            if b in writes:
                continue
            b.rd.append(ev)
            if len(b.rd) > 24:
                m = {}
                for (k, v) in b.rd:
                    if m.get(k, 0) < v:
                        m[k] = v
                b.rd = list(m.items())

    def op(self, eng, fn, reads=(), writes=()):
        reads = [b for b in reads if b is not None]
        writes = [b for b in writes if b is not None]
        waits = self._deps(eng, reads, writes)
        if self.cnt[eng] >= SEM_ROT:
            self.gen[eng] += 1
            self.cnt[eng] = 0
        self.cnt[eng] += 1
        key = (eng, self.gen[eng])
        ev = (key, self.cnt[eng])
        self.ops[eng].append((waits, fn, key, 1))
        self._record(ev, reads, writes)
        return ev

    def dma(self, eng, out_buf, out_ap, in_buf, in_ap, sem_buf=None, **kw):
        sb = sem_buf
        if sb is None:
            sb = in_buf if (out_buf.name.startswith("dr_")) else out_buf
        if sb.dsem is None:
            self.nbuf += 1
            sb.dsem = ("d", self.nbuf)
        waits = self._deps(eng, [in_buf], [out_buf])
        sb.dcnt += 16
        ev = (sb.dsem, sb.dcnt)
        self.dsems[sb.dsem] = sb.dcnt

        def fn(e, out_ap=out_ap, in_ap=in_ap, kw=kw):
            return e.dma_start(out=out_ap, in_=in_ap, **kw)
        self.ops[eng].append((waits, fn, sb.dsem, 16))
        self._record(ev, [in_buf], [out_buf])
        return ev

    def gather(self, out_buf, out_ap, in_buf, in_ap, idx_buf, idx_ap):
        sb = out_buf
        if sb.dsem is None:
            self.nbuf += 1
            sb.dsem = ("d", self.nbuf)
        waits = self._deps("pool", [in_buf, idx_buf], [out_buf])
        sb.dcnt += 16
        ev = (sb.dsem, sb.dcnt)
        self.dsems[sb.dsem] = sb.dcnt

        def fn(e):
            return e.indirect_dma_start(out=out_ap, out_offset=None, in_=in_ap,
                                        in_offset=bass.IndirectOffsetOnAxis(ap=idx_ap, axis=0))
        self.ops["pool"].append((waits, fn, sb.dsem, 16))
        self._record(ev, [in_buf, idx_buf], [out_buf])
        return ev

    def allgather(self, in_buf, out_buf, ncores):
        self.ncc = getattr(self, "ncc", 0) + 1
        key = ("cc", self.ncc)
        waits = self._deps("pool", [in_buf], [out_buf])
        ev = (key, 1)
        self.dsems[key] = 1
        iap, oap = in_buf.t.opt(), out_buf.t.opt()

        def fn(e):
            return e.collective_compute("AllGather", ALU.bypass, replica_groups=[list(range(ncores))],
                                        ins=[iap], outs=[oap])
        self.ops["pool"].append((waits, fn, key, 1))
        self._record(ev, [in_buf], [out_buf])
        return ev

    def wait_all(self, eng, events):
        need = {}
        for (k, v) in events:
            if need.get(k, 0) < v:
                need[k] = v
        self.ops[eng].append((list(need.items()), None, None, 0))

    def emit(self):
        nc = self.nc
        for e in ENGS:
            for (waits, fn, key, inc) in self.ops[e]:
                for (k, v) in waits:
                    self.sem(k)
                if key is not None:
                    self.sem(key)
        with nc.Block() as block:
            def replay(ename, eng):
                for (waits, fn, key, inc) in self.ops[ename]:
                    if fn is None or inc != 1 or not waits:
                        for (k, v) in waits:
                            eng.wait_ge(self.sems[k], v)
                        if fn is not None:
                            fn(eng).then_inc(self.sems[key], inc)
                    else:
                        for (k, v) in waits[:-1]:
                            eng.wait_ge(self.sems[k], v)
                        ins = fn(eng)
                        ins._wait_ge(self.sems[waits[-1][0]], waits[-1][1])
                        ins.then_inc(self.sems[key], inc)

            @block.tensor
            def _(eng):
                replay("pe", eng)

            @block.scalar
            def _(eng):
                replay("act", eng)

            @block.vector
            def _(eng):
                replay("dve", eng)

            @block.gpsimd
            def _(eng):
                replay("pool", eng)

            @block.sync
            def _(eng):
                replay("sp", eng)
        self.es.close()


class K:
    def __init__(self, P):
        self.P = P
        self.flip = 0

    def mm(self, ob, oap, lb, lap, rb, rap, start=True, stop=True):
        self.P.op("pe", lambda e: e.matmul(oap, lhsT=lap, rhs=rap, start=start, stop=stop),
                  reads=[lb, rb], writes=[ob])

    def tt(self, ob, oap, ab, aap, bb, bap, op, eng="dve"):
        self.P.op(eng, lambda e: e.tensor_tensor(oap, aap, bap, op), reads=[ab, bb], writes=[ob])

    def ts(self, ob, oap, ab, aap, s1, s2, op0, op1=None, eng="dve", sb=()):
        if op1 is None:
            self.P.op(eng, lambda e: e.tensor_scalar(oap, aap, s1, None, op0), reads=[ab] + list(sb), writes=[ob])
        else:
            self.P.op(eng, lambda e: e.tensor_scalar(oap, aap, s1, s2, op0, op1), reads=[ab] + list(sb), writes=[ob])

    def stt(self, ob, oap, ab, aap, sc, bb, bap, op0, op1, eng="dve", sb=()):
        self.P.op(eng, lambda e: e.scalar_tensor_tensor(oap, aap, sc, bap, op0, op1),
                  reads=[ab, bb] + list(sb), writes=[ob])

    def act(self, ob, oap, ab, aap, func, bias=None, scale=None, sb=()):
        kw = {}
        if bias is not None:
            kw["bias"] = bias
        if scale is not None:
            kw["scale"] = scale
        self.P.op("act", lambda e: e.activation(oap, aap, func, **kw), reads=[ab] + list(sb), writes=[ob])

    def copy(self, ob, oap, ab, aap, eng=None):
        if eng is None:
            self.flip ^= 1
            eng = "act" if self.flip else "dve"
        if eng == "act":
            self.P.op("act", lambda e: e.copy(oap, aap), reads=[ab], writes=[ob])
        else:
            self.P.op(eng, lambda e: e.tensor_copy(oap, aap), reads=[ab], writes=[ob])

    def memset(self, ob, oap, val, eng="pool"):
        self.P.op(eng, lambda e: e.memset(oap, val), writes=[ob])


def c3(ap, j=CH):
    return ap.rearrange("p (c j) -> p c j", j=j)


class Core:
    def __init__(self, P, kk, cst):
        self.P = P
        self.k = kk
        self.cst = cst
        S = P.sbuf
        B = lambda n, shp: P.sbuf(n, shp, BF16)
        self.vtok = B("c_vtok", [64, NB])
        self.kT = B("c_kT", [64, NCH * 128])
        self.bT = B("c_bT", [64, NCH * 128])
        self.sc = [B("c_sc%d" % i, [64, NB]) for i in range(3)]
        self.xy = [B("c_xy%d" % i, [64, NB]) for i in range(4)]
        self.Ru = B("c_Ru", [64, NB])
        self.RU = B("c_RU", [64, NCH * 128])
        self.Ru32 = S("c_Ru32", [64, NB])
        self.RU32 = S("c_RU32", [64, NCH * 128])
        self.MT = B("c_MT", [128, NCH * 128])
        self.Qf = B("c_Qf", [128, NB])
        self.fill = lambda: None

    def transp(self, dst, src, rows, dst2=None):
        P, k = self.P, self.k
        ident = self.cst["identb"]
        per = 512 // rows
        for c0 in range(0, NCH, per):
            pb = P.ps()
            for c in range(c0, c0 + per):
                k.mm(pb, pb[0:64, (c - c0) * rows:(c - c0 + 1) * rows], src, src[0:rows, c * CH:(c + 1) * CH],
                     ident, ident[0:rows, 0:rows])
            if dst2 is not None:
                k.copy(dst2, dst2[0:64, c0 * rows:(c0 + per) * rows], pb, pb[0:64, 0:per * rows], eng="dve")
                k.copy(dst, dst[0:64, c0 * rows:(c0 + per) * rows], dst2, dst2[0:64, c0 * rows:(c0 + per) * rows], eng="act")
            else:
                k.copy(dst, dst[0:64, c0 * rows:(c0 + per) * rows], pb, pb[0:64, 0:per * rows])
            self.fill()

    def chunk_mm(self, width, terms):
        P, k = self.P, self.k
        per = 512 // width
        outs = []
        for c0 in range(0, NCH, per):
            pb = P.ps()
            for c in range(c0, c0 + per):
                o = pb[:, (c - c0) * width:(c - c0 + 1) * width]
                for i, (lb, lf, rb, rf) in enumerate(terms):
                    lap = lf(c)
                    m = lap.shape[1]
                    k.mm(pb, pb[0:m, (c - c0) * width:(c - c0 + 1) * width], lb, lap, rb, rf(c),
                         start=(i == 0), stop=(i == len(terms) - 1))
            outs.append((pb, c0, per))
            self.fill()
        return outs

    def block(self, dk, delta, Rs, K2s, vT, rd, kTf, gam, dec0T, S2, si, yb, S32=None,
              KKs=None, Bs=None, kkd=None, bTf=None, dec1T=None, dec1=None):
        P, k, cst = self.P, self.k, self.cst
        vtok, kT, bT, Ru, RU, MT, Qf = self.vtok, self.kT, self.bT, self.Ru, self.RU, self.MT, self.Qf
        Ru32, RU32 = self.Ru32, self.RU32
        ident = cst["ident"]
        cs = lambda c: slice(c * CH, (c + 1) * CH)
        ck = lambda c: slice(c * dk, (c + 1) * dk)
        self.transp(vtok, vT, 64)
        self.transp(kT, kTf, dk)
        RKT = self.sc[0]
        (pb, _, _), = self.chunk_mm(64, [(K2s, lambda c: K2s[0:dk, cs(c)], Rs, lambda c: Rs[0:dk, cs(c)])])
        k.tt(RKT, RKT[0:64, :], pb, pb[0:64, :], dec0T, dec0T[0:64, :], ALU.mult)
        if delta:
            self.transp(bT, bTf, dk)
            RBT, KKT = self.sc[1], self.sc[2]
            X, Y = self.xy[0], self.xy[1]
            (p1, _, _), = self.chunk_mm(64, [(Bs, lambda c: Bs[0:dk, cs(c)], Rs, lambda c: Rs[0:dk, cs(c)])])
            k.tt(RBT, RBT[0:64, :], p1, p1[0:64, :], dec0T, dec0T[0:64, :], ALU.mult)
            (p2, _, _), = self.chunk_mm(64, [(K2s, lambda c: K2s[0:dk, cs(c)], KKs, lambda c: KKs[0:dk, cs(c)])])
            k.tt(KKT, KKT[0:64, :], p2, p2[0:64, :], dec1T, dec1T[0:64, :], ALU.mult)
            (p3, _, _), = self.chunk_mm(64, [(Bs, lambda c: Bs[0:dk, cs(c)], KKs, lambda c: KKs[0:dk, cs(c)])])
            k.stt(Y, Y[0:64, :], p3, p3[0:64, :], -1.0, dec1T, dec1T[0:64, :], ALU.mult, ALU.mult)
            (p4, _, _), = self.chunk_mm(64, [(KKs, lambda c: KKs[0:dk, cs(c)], Bs, lambda c: Bs[0:dk, cs(c)])])
            k.stt(X, X[0:64, :], p4, p4[0:64, :], -1.0, dec1, dec1[0:64, :], ALU.mult, ALU.mult)
            (p5, _, _), = self.chunk_mm(64, [(KKT, lambda c: KKT[0:64, cs(c)], vtok, lambda c: vtok[0:64, cs(c)])])
            k.ts(Ru, Ru[0:64, :], p5, p5[0:64, :], -1.0, None, ALU.mult)
            self.transp(RU, kkd, dk)
            xi = 0
            for lvl in range(6):
                X, Y = self.xy[xi], self.xy[xi + 1]
                last = (lvl == 5)
                if not last:
                    Xn, Yn = self.xy[2 - xi], self.xy[3 - xi]
                    (py, _, _), = self.chunk_mm(64, [(X, lambda c: X[0:64, cs(c)], Y, lambda c: Y[0:64, cs(c)])])
                (pu, _, _), = self.chunk_mm(64, [(Y, lambda c: Y[0:64, cs(c)], Ru, lambda c: Ru[0:64, cs(c)])])
                if not last:
                    k.copy(Yn, Yn[0:64, :], py, py[0:64, :], eng="act")
                outsU = self.chunk_mm(dk, [(Y, lambda c: Y[0:64, cs(c)], RU, lambda c: RU[0:64, ck(c)])])
                k.tt(Ru, Ru[0:64, :], Ru, Ru[0:64, :], pu, pu[0:64, :], ALU.add)
                if not last:
                    (px, _, _), = self.chunk_mm(64, [(Y, lambda c: Y[0:64, cs(c)], X, lambda c: X[0:64, cs(c)])])
                for (pbU, c0, n) in outsU:
                    k.tt(RU, RU[0:64, c0 * dk:(c0 + n) * dk], RU, RU[0:64, c0 * dk:(c0 + n) * dk],
                         pbU, pbU[0:64, 0:n * dk], ALU.add, eng="dve")
                if not last:
                    k.copy(Xn, Xn[0:64, :], px, px[0:64, :], eng="act")
                    xi = 2 - xi
            outsM = self.chunk_mm(dk, [(RU, lambda c: RU[0:64, ck(c)], bT, lambda c: bT[0:64, ck(c)])])
            for (pbM, c0, n) in outsM:
                for c in range(c0, c0 + n):
                    k.stt(MT, MT[0:dk, ck(c)], ident, ident[0:dk, 0:dk], gam[0:dk, c * CH + 63:c * CH + 64],
                          pbM, pbM[0:dk, (c - c0) * dk:(c - c0 + 1) * dk], ALU.mult, ALU.subtract, sb=[gam])
            (pq, _, _), = self.chunk_mm(64, [(RU, lambda c: RU[0:64, ck(c)], RBT, lambda c: RBT[0:64, cs(c)])])
            k.tt(Qf, Qf[0:dk, :], rd, rd[0:dk, :], pq, pq[0:dk, :], ALU.subtract)
            Q = Qf
        else:
            Q = rd
        for c in range(NCH):
            S = S2[si]
            Sn = S2[1 - si]
            sl = cs(c)
            pbs = P.ps()
            if delta:
                k.mm(pbs, pbs[0:dk, 0:64], MT, MT[0:dk, ck(c)], S, S[0:dk, 0:64], start=True, stop=False)
                k.mm(pbs, pbs[0:dk, 0:64], kT, kT[0:64, ck(c)], vtok, vtok[0:64, sl], start=False, stop=False)
                k.mm(pbs, pbs[0:dk, 0:64], bT, bT[0:64, ck(c)], Ru, Ru[0:64, sl], start=False, stop=True)
                k.copy(Sn, Sn[0:dk, 0:64], pbs, pbs[0:dk, 0:64], eng="dve")
            else:
                k.mm(pbs, pbs[0:dk, 0:64], kT, kT[0:64, ck(c)], vtok, vtok[0:64, sl], start=True, stop=True)
                k.stt(S32, S32[0:dk, 0:64], S32, S32[0:dk, 0:64], gam[0:dk, c * CH + 63:c * CH + 64],
                      pbs, pbs[0:dk, 0:64], ALU.mult, ALU.add, sb=[gam])
                k.copy(Sn, Sn[0:dk, 0:64], S32, S32[0:dk, 0:64], eng="act")
            k.mm(yb, yb[0:64, sl], S, S[0:dk, 0:64], Q, Q[0:dk, sl], start=True, stop=False)
            k.mm(yb, yb[0:64, sl], vtok, vtok[0:64, sl], RKT, RKT[0:64, sl], start=False, stop=not delta)
            if delta:
                k.mm(yb, yb[0:64, sl], Ru, Ru[0:64, sl], RBT, RBT[0:64, sl], start=False, stop=True)
            self.fill()
            si = 1 - si
        return si


QROWS = [("gq", 128), ("gk", 128), ("gv", 64), ("gb", 1), ("ga", 1),
         ("rq", 64), ("rk", 64), ("rv", 64),
         ("mx", 64), ("mB", 128), ("mC", 128), ("mdt", 1),
         ("wr", 64), ("wk", 64), ("wv", 64), ("wl", 32), ("al", 32), ("gl", 96)]
QOFF = {}
_o = 0
for _n, _r in QROWS:
    QOFF[_n] = (_o, _r)
    _o += _r
NR = _o
HIST = {"gq": 3, "gk": 3, "gv": 3, "mx": 3, "mB": 3, "mC": 3,
        "wr": 1, "wk": 1, "wv": 1, "wl": 1, "al": 1, "gl": 1}
CST = {}
_o = 0
for _n, _w in [("n1w", 16), ("gcq", 4), ("gck", 4), ("gcv", 4), ("g_alog", 1), ("g_dtb", 1),
               ("r_lg", 1), ("mcx", 4), ("mcB", 4), ("mcC", 4), ("mbx", 1), ("mbB", 1), ("mbC", 1),
               ("m_alog", 1), ("m_dtb", 1), ("m_d", 1),
               ("mu_r", 1), ("mu_k", 1), ("mu_v", 1), ("mu_wl", 1), ("mu_al", 1), ("mu_gl", 1),
               ("w0", 1), ("a0", 1), ("k_k", 1), ("k_a", 1), ("r_k", 1), ("ln_w", 1), ("ln_b", 1),
               ("w_up", 64), ("a_up", 64), ("g_up", 64)]:
    CST[_n] = (_o, _w)
    _o += _w
NCST = _o
MSK = {}
_o = 0
for _n, _w in [("ident", 128), ("U8", NB), ("SU8", NB), ("SL8", NB), ("reset", NB), ("rotJ", 64), ("ones", 128)]:
    MSK[_n] = (_o, _w)
    _o += _w
NMSK = _o


def build_phase_a(T, mixers=("gdn", "ret", "ssd", "rwkv")):
    nc = bass.Bass("TRN2", target_bir_lowering=False)
    P = Prog(nc)
    k = K(P)
    hT = P.dram("hT", [D_MODEL, T], F32, kind="ExternalInput")
    win = P.dram("win", [128, KT * NR], F32, kind="ExternalInput")
    cstd = P.dram("cst", [128, NCST], F32, kind="ExternalInput")
    mskd = P.dram("msk", [128, NMSK], F32, kind="ExternalInput")
    rotd = P.dram("rot", [64, 2 * T], F32, kind="ExternalInput")
    om = P.dram("dr_om", [4 * 64, T], F32, kind="ExternalOutput")
    P.psum_banks(8)
    emit_phase_a(P, k, T, lambda kt, blk: (hT, hT[kt * 128:(kt + 1) * 128, blk * NB:(blk + 1) * NB]),
                 win, cstd, mskd, rotd, lambda j, blk: (om, om[j * 64:(j + 1) * 64, blk * NB:(blk + 1) * NB]), mixers)
    P.wait_all("sp", [om.lw] if om.lw else [])
    P.emit()
    return nc


def emit_phase_a(P, k, T, hsrc, win, cstd, mskd, rotd, omdst, mixers=("gdn", "ret", "ssd", "rwkv")):
    nblk = T // NB
    P.rot = P.psums[0:5]
    fillb = P.psums[5]
    ybank = P.psums[6:8]
    S = P.sbuf
    wsb = S("wsb", [128, KT * NR], BF16)
    cst = S("cstsb", [128, NCST])
    msk = S("msksb", [128, NMSK])
    P.dma("sp", cst, cst[:, :], cstd, cstd[:, :])
    P.dma("sp", msk, msk[:, :], mskd, mskd[:, :])

    def C(name, rows=128, j0=0, j1=None):
        o, w = CST[name]
        if j1 is None:
            j1 = w
        return cst[0:rows, o + j0:o + j1]

    def M(name, rows=64):
        o, w = MSK[name]
        return msk[0:rows, o:o + w]

    class MV:
        def __init__(self, name):
            self.o, self.w = MSK[name]

        def __getitem__(self, idx):
            r, c = idx
            c0 = 0 if c.start is None else c.start
            c1 = self.w if c.stop is None else c.stop
            return msk[r, self.o + c0:self.o + c1]
    class View(Buf):
        pass

    def mview(name):
        v = MV(name)
        b = Buf.__new__(Buf)
        b.name = "msk_" + name
        b.t = v
        b.lw = None
        b.rd = []
        b.dsem = None
        b.dcnt = 0
        return b
    ident, U8, SU8, SL8 = mview("ident"), mview("U8"), mview("SU8"), mview("SL8")
    identb = S("identb", [128, 128], BF16)
    k.copy(identb, identb[:, :], msk, msk[:, MSK["ident"][0]:MSK["ident"][0] + 128], eng="act")
    core = Core(P, k, {"ident": ident, "identb": identb})
    ones_bf = S("ones_bf", [128, 128], BF16)
    k.memset(ones_bf, ones_bf[:, :], 1.0)
    fb = [S("fb%d" % i, [128, NB], BF16) for i in range(9)]
    fillstate = {"quota": 0, "pending": None}

    def fill(n=1):
        pend = fillstate["pending"]
        for _ in range(n):
            if pend and fillstate["quota"] > 0:
                fillstate["quota"] -= 1
                pend.popleft()()
    core.fill = fill

    def section(i):
        pend = fillstate["pending"]
        fillstate["quota"] = ((len(pend) + (3 - i)) // (4 - i)) if pend else 0
    rotJb = S("rotJb", [64, 64], BF16)
    k.copy(rotJb, rotJb[:, :], msk, msk[0:64, MSK["rotJ"][0]:MSK["rotJ"][0] + 64], eng="act")
    zero_ev = None

    for b in (ident, U8, SU8, SL8):
        b.lw = msk.lw

    hbuf = [S("hbuf%d" % i, [128, NB]) for i in range(2)]
    hcnt = [0]
    abf = S("abf", [128, KT * NB], BF16)
    sqb = [S("sqb%d" % i, [128, NB], BF16) for i in range(2)]
    rstd = S("rstd", [128, NB])
    raws = [{n: S("raw%d_%s" % (s_, n), [max(r, 1), HIST.get(n, 0) + NB], F32 if r == 1 else BF16) for n, r in QROWS}
            for s_ in range(2)]
    for raw_ in raws:
        for n, r in QROWS:
            if HIST.get(n, 0):
                k.memset(raw_[n], raw_[n][0:r, 0:HIST[n]], 0.0)
    ft = [S("ft%d" % i, [128, NB]) for i in range(14)]
    _si = 0
    for kt in range(KT):
        for c0 in range(0, NR, NB):
            c1 = min(NR, c0 + NB)
            stg = ft[_si % 14]
            _si += 1
            P.dma("sp", stg, stg[:, 0:c1 - c0], win, win[:, kt * NR + c0:kt * NR + c1])
            k.copy(wsb, wsb[:, kt * NR + c0:kt * NR + c1], stg, stg[:, 0:c1 - c0], eng=("pool" if _si % 2 else "act"))
    rows_t = [S("row%d" % i, [1, NB]) for i in range(5)]
    rep = [S("rep%d" % i, [128, NB]) for i in range(6)]
    dect = [S("dec%d" % i, [64, NB]) for i in range(3)]
    cols = S("cols", [64, 16])
    osb = S("osb", [64, NB])
    states = {m: [S("st_%s%d" % (m, i), [128, 64], BF16) for i in range(2)] for m in mixers}
    st32 = {m: S("st32_%s" % m, [128, 64]) for m in mixers if m in ("ret", "ssd")}
    sidx = {m: 0 for m in mixers}
    for m in mixers:
        k.memset(states[m][0], states[m][0][:, :], 0.0)
    for m in st32:
        k.memset(st32[m], st32[m][:, :], 0.0)
    negA = S("negA", [128, 4])
    k.act(negA, negA[:, 0:1], cst, C("g_alog"), AF.Exp)
    k.act(negA, negA[:, 1:2], cst, C("m_alog"), AF.Exp)
    k.ts(negA, negA[:, 0:2], negA, negA[:, 0:2], -1.0, None, ALU.mult)
    omu = S("omu", [128, 16])
    for j, mun in enumerate(("mu_r", "mu_k", "mu_v", "mu_wl", "mu_al", "mu_gl")):
        k.ts(omu, omu[:, j:j + 1], cst, C(mun), -1.0, 1.0, ALU.mult, ALU.add)
    k.ts(omu, omu[:, 6:7], cst, C("w0"), -1.0, None, ALU.mult)
    k.ts(omu, omu[:, 7:8], cst, C("k_a"), -1.0, 1.0, ALU.mult, ALU.add)
    k.memset(omu, omu[:, 8:9], -0.5, eng="dve")
    onesf = mview("ones")
    onesf.lw = msk.lw
    reset = mview("reset")
    reset.lw = msk.lw

    def rsqrt_to(ob, oap, ib, iap, eps, scale=1.0):
        k.act(ob, oap, ib, iap, AF.Ln, bias=eps, scale=scale)
        k.act(ob, oap, ob, oap, AF.Exp, scale=-0.5)

    def conv_silu(dst, src, rows, wname, bias=None, out=None, nxt=None):
        wo = CST[wname][0]
        k.ts(dst, dst[0:rows, :], src, src[0:rows, 0:NB], cst[0:rows, wo:wo + 1], None, ALU.mult, sb=[cst])
        for i in range(1, 4):
            k.stt(dst, dst[0:rows, :], src, src[0:rows, i:i + NB], cst[0:rows, wo + i:wo + i + 1],
                  dst, dst[0:rows, :], ALU.mult, ALU.add, sb=[cst])
        out = out or dst
        if bias is None:
            k.act(out, out[0:rows, :], dst, dst[0:rows, :], AF.Silu)
        else:
            k.act(out, out[0:rows, :], dst, dst[0:rows, :], AF.Silu, bias=C(bias, rows), sb=[cst])
        nxt = nxt or src
        k.copy(nxt, nxt[0:rows, 0:3], src, src[0:rows, NB:NB + 3], eng="act")

    def replicate(dst, row, rows=128):
        pb = P.ps()
        k.mm(pb, pb[0:rows, :], onesf, onesf[0:1, 0:rows], row, row[0:1, :])
        fill(2)
        k.copy(dst, dst[0:rows, :], pb, pb[0:rows, :])

    def to_cols(j, row):
        pb = P.ps()
        for c in range(NCH):
            k.mm(pb, pb[0:64, c:c + 1], row, row[0:1, c * CH:(c + 1) * CH], onesf, onesf[0:1, 0:1])
        fill(2)
        k.copy(cols, cols[0:64, j * 8:(j + 1) * 8], pb, pb[0:64, 0:8])

    def l2norm(dst, src, rows, scale):
        sq = ft[13]
        sqh = sqb[0]
        k.tt(sqh, sqh[0:rows, :], src, src[0:rows, :], src, src[0:rows, :], ALU.mult)
        pb = P.ps()
        k.mm(pb, pb[0:rows, :], ones_bf, ones_bf[0:rows, 0:rows], sqh, sqh[0:rows, :])
        fill(2)
        rsqrt_to(sq, sq[0:rows, :], pb, pb[0:rows, :], EPS)
        k.stt(dst, dst[0:rows, :], src, src[0:rows, :], scale, sq, sq[0:rows, :], ALU.mult, ALU.mult)

    def scalar_decay(la, dk, delta):
        g, gp = rows_t[3], rows_t[4]
        k.P.op("dve", lambda e: e.tensor_tensor_scan(g[0:1, :], reset[0:1, :], la[0:1, :], 0.0, ALU.mult, ALU.add),
               reads=[reset, la], writes=[g])
        Grep, EG, Etail = rep[0], rep[1], rep[2]
        replicate(Grep, g)
        to_cols(0, g)
        k.act(EG, EG[:, :], Grep, Grep[:, :], AF.Exp)
        k.tt(Etail, c3(Etail[:, :]), Grep, Grep[:, 63:NB:64].unsqueeze(2).to_broadcast([128, NCH, CH]),
             Grep, c3(Grep[:, :]), ALU.subtract)
        k.act(Etail, Etail[:, :], Etail, Etail[:, :], AF.Exp)
        d0 = dect[0]
        gcolB = cols[0:64, 0:8].unsqueeze(2).to_broadcast([64, NCH, CH])
        k.tt(d0, c3(d0[0:64, :]), Grep, c3(Grep[0:64, :]), cols, gcolB, ALU.subtract)
        k.ts(d0, d0[0:64, :], d0, d0[0:64, :], 0.0, None, ALU.min)
        k.act(d0, d0[0:64, :], d0, d0[0:64, :], AF.Exp)
        k.tt(d0, d0[0:64, :], d0, d0[0:64, :], U8, U8[0:64, :], ALU.mult)
        res = {"Grep": Grep, "EG": EG, "Etail": Etail, "dec0T": d0}
        if delta:
            k.tt(gp, gp[0:1, :], g, g[0:1, :], la, la[0:1, :], ALU.subtract)
            Gprep, EGp = rep[3], rep[4]
            replicate(Gprep, gp)
            to_cols(1, gp)
            k.act(EGp, EGp[:, :], Gprep, Gprep[:, :], AF.Exp)
            d1T, d1 = dect[1], dect[2]
            k.tt(d1T, c3(d1T[0:64, :]), Gprep, c3(Gprep[0:64, :]), cols, gcolB, ALU.subtract)
            k.ts(d1T, d1T[0:64, :], d1T, d1T[0:64, :], 0.0, None, ALU.min)
            k.act(d1T, d1T[0:64, :], d1T, d1T[0:64, :], AF.Exp)
            k.tt(d1T, d1T[0:64, :], d1T, d1T[0:64, :], SU8, SU8[0:64, :], ALU.mult)
            gpcolB = cols[0:64, 8:16].unsqueeze(2).to_broadcast([64, NCH, CH])
            k.tt(d1, c3(d1[0:64, :]), cols, gpcolB, Grep, c3(Grep[0:64, :]), ALU.subtract)
            k.ts(d1, d1[0:64, :], d1, d1[0:64, :], 0.0, None, ALU.min)
            k.act(d1, d1[0:64, :], d1, d1[0:64, :], AF.Exp)
            k.tt(d1, d1[0:64, :], d1, d1[0:64, :], SL8, SL8[0:64, :], ALU.mult)
            res.update({"EGp": EGp, "dec1T": d1T, "dec1": d1})
        return res

    def seg_rms(blk):
        t0 = blk * NB
        pss = P.ps()
        for kt in range(KT):
            hb = hbuf[hcnt[0] % 2]
            hcnt[0] += 1
            hsb_, hap_ = hsrc(kt, blk)
            P.dma("sp", hb, hb[:, :], hsb_, hap_)
            sq = sqb[kt % 2]
            k.act(sq, sq[:, :], hb, hb[:, :], AF.Square)
            k.mm(pss, pss[:, :], ones_bf, ones_bf[:, :], sq, sq[:, :], start=(kt == 0), stop=(kt == KT - 1))
        rsqrt_to(rstd, rstd[:, :], pss, pss[:, :], EPS, scale=1.0 / D_MODEL)
        for kt in range(KT):
            hb = hbuf[hcnt[0] % 2]
            hcnt[0] += 1
            hsb_, hap_ = hsrc(kt, blk)
            P.dma("sp", hb, hb[:, :], hsb_, hap_)
            k.stt(abf, abf[:, kt * NB:(kt + 1) * NB], hb, hb[:, :],
                  C("n1w", 128, kt, kt + 1), rstd, rstd[:, :], ALU.mult, ALU.mult, sb=[cst])
    need = []
    if "gdn" in mixers:
        need += ["gq", "gk", "gv", "gb", "ga"]
    if "ret" in mixers:
        need += ["rq", "rk", "rv"]
    if "ssd" in mixers:
        need += ["mx", "mB", "mC", "mdt"]
    if "rwkv" in mixers:
        need += ["wr", "wk", "wv", "wl", "al", "gl"]

    def group_steps(blk, n):
        raw = raws[blk % 2]
        off, r = QOFF[n]
        h = HIST.get(n, 0)

        def mk(q):
            def f():
                pb = fillb
                for kt in range(q * 4, q * 4 + 4):
                    k.mm(pb, pb[0:r, :], wsb, wsb[:, kt * NR + off:kt * NR + off + r], abf, abf[:, kt * NB:(kt + 1) * NB],
                         start=(kt == 0), stop=(kt == KT - 1))
                if q == 3:
                    k.copy(raw[n], raw[n][0:r, h:h + NB], pb, pb[0:r, :])
            return f
        return [mk(q) for q in range(4)]

    def seg_mix(blk):
        t0 = blk * NB
        raw = raws[blk % 2]
        rawn = raws[(blk + 1) % 2]
        if "gdn" in mixers:
            section(0)
            qc, kc, vc, qn, kn = ft[0], ft[1], fb[2], fb[0], fb[1]
            conv_silu(qc, raw["gq"], 128, "gcq", nxt=rawn["gq"])
            conv_silu(kc, raw["gk"], 128, "gck", nxt=rawn["gk"])
            conv_silu(ft[2], raw["gv"], 64, "gcv", out=vc, nxt=rawn["gv"])
            l2norm(qn, qc, 128, 128 ** -0.5)
            l2norm(kn, kc, 128, 1.0)
            beta, la, ba = rows_t[0], rows_t[1], rows_t[2]
            k.act(beta, beta[0:1, :], raw["gb"], raw["gb"][0:1, :], AF.Sigmoid)
            k.act(la, la[0:1, :], raw["ga"], raw["ga"][0:1, :], AF.Exp, bias=C("g_dtb", 1), sb=[cst])
            k.act(la, la[0:1, :], la, la[0:1, :], AF.Ln, bias=1.0)
            k.ts(la, la[0:1, :], la, la[0:1, :], negA[0:1, 0:1], None, ALU.mult, sb=[negA])
            k.act(ba, ba[0:1, :], la, la[0:1, :], AF.Exp)
            k.tt(ba, ba[0:1, :], ba, ba[0:1, :], beta, beta[0:1, :], ALU.mult)
            sd = scalar_decay(la, 128, True)
            Brep, BArep = rep[5], ft[5]
            replicate(Brep, beta)
            replicate(BArep, ba)
            Bs, K2s, rd, kkd, kTf, bTf = fb[3], fb[4], fb[5], fb[6], fb[7], fb[8]
            k.tt(Bs, Bs[:, :], kn, kn[:, :], BArep, BArep[:, :], ALU.mult)
            k.tt(K2s, K2s[:, :], kn, kn[:, :], Brep, Brep[:, :], ALU.mult)
            k.tt(rd, rd[:, :], qn, qn[:, :], sd["EG"], sd["EG"][:, :], ALU.mult)
            k.tt(kkd, kkd[:, :], kn, kn[:, :], sd["EGp"], sd["EGp"][:, :], ALU.mult)
            k.tt(kTf, kTf[:, :], K2s, K2s[:, :], sd["Etail"], sd["Etail"][:, :], ALU.mult)
            k.tt(bTf, bTf[:, :], Bs, Bs[:, :], sd["Etail"], sd["Etail"][:, :], ALU.mult)
            yb = ybank[0]
            sidx["gdn"] = core.block(128, True, qn, K2s, vc, rd, kTf, sd["EG"], sd["dec0T"], states["gdn"], sidx["gdn"], yb,
                                     KKs=kn, Bs=Bs, kkd=kkd, bTf=bTf, dec1T=sd["dec1T"], dec1=sd["dec1"])
            k.copy(osb, osb[0:64, :], yb, yb[0:64, :], eng="act")
            ob_, oa_ = omdst(0, blk)
            P.dma("sp", ob_, oa_, osb, osb[0:64, :], sem_buf=osb)


        if "ret" in mixers:
            section(1)
            cs_t, sn_t = ft[0], ft[1]
            P.dma("sp", cs_t, cs_t[0:64, :], rotd, rotd[:, t0:t0 + NB])
            P.dma("sp", sn_t, sn_t[0:64, :], rotd, rotd[:, T + t0:T + t0 + NB])
            rotJ = rotJb
            qr, kr = fb[0], fb[1]
            for (dst, src, sc) in ((qr, raw["rq"], 1.0), (kr, raw["rk"], 0.125)):
                pj = P.ps()
                k.mm(pj, pj[0:64, :], rotJ, rotJ[0:64, 0:64], src, src[0:64, :])
                fill(2)
                k.stt(ft[4], ft[4][0:64, :], pj, pj[0:64, :], sc, sn_t, sn_t[0:64, :], ALU.mult, ALU.mult)
                k.stt(ft[5], ft[5][0:64, :], src, src[0:64, :], sc, cs_t, cs_t[0:64, :], ALU.mult, ALU.mult)
                k.tt(dst, dst[0:64, :], ft[5], ft[5][0:64, :], ft[4], ft[4][0:64, :], ALU.add)
            la = rows_t[1]
            k.ts(la, la[0:1, :], reset, reset[0:1, :], 0.0, C("r_lg", 1), ALU.mult, ALU.add, sb=[cst])
            sd = scalar_decay(la, 64, False)
            rd, kTf, rvb = fb[3], fb[4], fb[2]
            k.tt(rd, rd[0:64, :], qr, qr[0:64, :], sd["EG"], sd["EG"][0:64, :], ALU.mult)
            k.tt(kTf, kTf[0:64, :], kr, kr[0:64, :], sd["Etail"], sd["Etail"][0:64, :], ALU.mult)
            k.copy(rvb, rvb[0:64, :], raw["rv"], raw["rv"][0:64, :], eng="pool")
            yb = ybank[1]
            sidx["ret"] = core.block(64, False, qr, kr, rvb, rd, kTf, sd["EG"], sd["dec0T"], states["ret"], sidx["ret"], yb,
                                     S32=st32["ret"])
            k.copy(osb, osb[0:64, :], yb, yb[0:64, :], eng="act")
            ob_, oa_ = omdst(1, blk)
            P.dma("sp", ob_, oa_, osb, osb[0:64, :], sem_buf=osb)

        if "ssd" in mixers:
            section(2)
            xc, Bc, Cc = fb[0], ft[1], fb[1]
            conv_silu(ft[0], raw["mx"], 64, "mcx", bias="mbx", out=xc, nxt=rawn["mx"])
            conv_silu(Bc, raw["mB"], 128, "mcB", bias="mbB", nxt=rawn["mB"])
            conv_silu(ft[2], raw["mC"], 128, "mcC", bias="mbC", out=Cc, nxt=rawn["mC"])
            dt, la = rows_t[0], rows_t[1]
            k.act(dt, dt[0:1, :], raw["mdt"], raw["mdt"][0:1, :], AF.Exp, bias=C("m_dtb", 1), sb=[cst])
            k.act(dt, dt[0:1, :], dt, dt[0:1, :], AF.Ln, bias=1.0)
            k.ts(la, la[0:1, :], dt, dt[0:1, :], negA[0:1, 1:2], None, ALU.mult, sb=[negA])
            sd = scalar_decay(la, 128, False)
            DTrep = rep[5]
            replicate(DTrep, dt)
            K2s, rd, kTf = fb[2], fb[3], fb[4]
            k.tt(K2s, K2s[:, :], Bc, Bc[:, :], DTrep, DTrep[:, :], ALU.mult)
            k.tt(rd, rd[:, :], Cc, Cc[:, :], sd["EG"], sd["EG"][:, :], ALU.mult)
            k.tt(kTf, kTf[:, :], K2s, K2s[:, :], sd["Etail"], sd["Etail"][:, :], ALU.mult)
            yb = ybank[0]
            sidx["ssd"] = core.block(128, False, Cc, K2s, xc, rd, kTf, sd["EG"], sd["dec0T"], states["ssd"], sidx["ssd"], yb,
                                     S32=st32["ssd"])
            k.stt(osb, osb[0:64, :], xc, xc[0:64, :], C("m_d", 64), yb, yb[0:64, :], ALU.mult, ALU.add, sb=[cst])
            ob_, oa_ = omdst(2, blk)
            P.dma("sp", ob_, oa_, osb, osb[0:64, :], sem_buf=osb)

        if "rwkv" in mixers:
            section(3)
            mixed = {}
            for (nm, mun, j, r_, dst) in (("wr", "mu_r", 0, 64, ft[0]), ("wk", "mu_k", 1, 64, ft[1]), ("wv", "mu_v", 2, 64, fb[0]),
                                          ("wl", "mu_wl", 3, 32, ft[3]), ("al", "mu_al", 4, 32, ft[4]), ("gl", "mu_gl", 5, 96, ft[5])):
                src = raw[nm]
                k.ts(ft[13], ft[13][0:r_, :], src, src[0:r_, 0:NB], C(mun, r_), None, ALU.mult, sb=[cst])
                k.stt(dst, dst[0:r_, :], src, src[0:r_, 1:1 + NB], omu[0:r_, j:j + 1], ft[13], ft[13][0:r_, :],
                      ALU.mult, ALU.add, sb=[omu])
                k.copy(rawn[nm], rawn[nm][0:r_, 0:1], src, src[0:r_, NB:NB + 1], eng="act")
            r, kx, v, tw, al, gl = ft[0], ft[1], fb[0], ft[3], ft[4], ft[5]
            k.act(tw, tw[0:32, :], tw, tw[0:32, :], AF.Tanh)
            pz = P.ps()
            k.mm(pz, pz[0:64, :], cst, C("w_up", 32), tw, tw[0:32, :])
            fill(2)
            lw = ft[6]
            k.act(lw, lw[0:64, :], pz, pz[0:64, :], AF.Exp, bias=omu[0:64, 6:7], scale=-1.0, sb=[omu])
            k.act(lw, lw[0:64, :], lw, lw[0:64, :], AF.Ln, bias=1.0)
            k.act(lw, lw[0:64, :], lw, lw[0:64, :], AF.Exp, bias=omu[0:64, 8:9], scale=-1.0, sb=[omu])
            k.ts(lw, lw[0:64, :], lw, lw[0:64, :], -1.0, None, ALU.mult)
            gW, gWp = ft[7], ft[8]
            P.op("dve", lambda e: e.tensor_tensor_scan(gW[0:64, :], reset[0:64, :], lw[0:64, :], 0.0, ALU.mult, ALU.add),
                 reads=[reset, lw], writes=[gW])
            k.tt(gWp, gWp[0:64, :], gW, gW[0:64, :], lw, lw[0:64, :], ALU.subtract)
            k.act(gWp, gWp[0:64, :], gWp, gWp[0:64, :], AF.Exp)
            pa = P.ps()
            k.mm(pa, pa[0:64, :], cst, C("a_up", 32), al, al[0:32, :])
            fill(2)
            a = ft[3]
            k.act(a, a[0:64, :], pa, pa[0:64, :], AF.Sigmoid, bias=C("a0", 64), sb=[cst])
            k.act(gl, gl[0:96, :], gl, gl[0:96, :], AF.Sigmoid)
            pg = P.ps()
            k.mm(pg, pg[0:64, :], cst, C("g_up", 96), gl, gl[0:96, :])
            fill(2)
            gg = ft[4]
            k.copy(gg, gg[0:64, :], pg, pg[0:64, :], eng="act")
            kk = ft[5]
            k.ts(kk, kk[0:64, :], kx, kx[0:64, :], C("k_k", 64), None, ALU.mult, sb=[cst])
            l2norm(kk, kk, 64, 1.0)
            k2, b = ft[9], ft[10]
            k.ts(k2, k2[0:64, :], a, a[0:64, :], C("k_a", 64), omu[0:64, 7:8], ALU.mult, ALU.add, sb=[cst, omu])
            k.tt(k2, k2[0:64, :], k2, k2[0:64, :], kx, kx[0:64, :], ALU.mult)
            k.tt(b, b[0:64, :], a, a[0:64, :], kk, kk[0:64, :], ALU.mult)
            EGW, EI, Etail = ft[11], ft[12], rep[0]
            k.act(EGW, EGW[0:64, :], gW, gW[0:64, :], AF.Exp)
            k.act(EI, EI[0:64, :], gW, gW[0:64, :], AF.Exp, scale=-1.0)
            k.tt(Etail, c3(Etail[0:64, :]), gW, gW[0:64, 63:NB:64].unsqueeze(2).to_broadcast([64, NCH, CH]),
                 gW, c3(gW[0:64, :]), ALU.subtract)
            k.act(Etail, Etail[0:64, :], Etail, Etail[0:64, :], AF.Exp)
            rd, kkd, Bs, K2s, kTf, bTf = fb[1], fb[2], fb[3], fb[4], fb[5], fb[6]
            k.tt(rd, rd[0:64, :], r, r[0:64, :], EGW, EGW[0:64, :], ALU.mult)
            k.tt(kkd, kkd[0:64, :], kk, kk[0:64, :], gWp, gWp[0:64, :], ALU.mult)
            k.tt(Bs, Bs[0:64, :], b, b[0:64, :], EI, EI[0:64, :], ALU.mult)
            k.tt(K2s, K2s[0:64, :], k2, k2[0:64, :], EI, EI[0:64, :], ALU.mult)
            k.tt(kTf, kTf[0:64, :], k2, k2[0:64, :], Etail, Etail[0:64, :], ALU.mult)
            k.tt(bTf, bTf[0:64, :], b, b[0:64, :], Etail, Etail[0:64, :], ALU.mult)
            yb = ybank[1]
            sidx["rwkv"] = core.block(64, True, rd, K2s, v, rd, kTf, EGW, U8, states["rwkv"], sidx["rwkv"], yb,
                                      KKs=kkd, Bs=Bs, kkd=kkd, bTf=bTf, dec1T=SU8, dec1=SL8)
            y, yc, t1 = ft[6], ft[7], ft[8]
            k.copy(y, y[0:64, :], yb, yb[0:64, :], eng="act")
            pm = P.ps()
            k.mm(pm, pm[0:64, :], onesf, onesf[0:64, 0:64], y, y[0:64, :])
            k.stt(yc, yc[0:64, :], pm, pm[0:64, :], -1.0 / 64, y, y[0:64, :], ALU.mult, ALU.add)
            k.tt(t1, t1[0:64, :], yc, yc[0:64, :], yc, yc[0:64, :], ALU.mult)
            pv = P.ps()
            k.mm(pv, pv[0:64, :], onesf, onesf[0:64, 0:64], t1, t1[0:64, :])
            rsqrt_to(t1, t1[0:64, :], pv, pv[0:64, :], 64e-5, scale=1.0 / 64)
            k.tt(yc, yc[0:64, :], yc, yc[0:64, :], t1, t1[0:64, :], ALU.mult)
            k.ts(yc, yc[0:64, :], yc, yc[0:64, :], C("ln_w", 64), C("ln_b", 64), ALU.mult, ALU.add, sb=[cst])
            k.tt(t1, t1[0:64, :], r, r[0:64, :], k2, k2[0:64, :], ALU.mult)
            k.ts(t1, t1[0:64, :], t1, t1[0:64, :], C("r_k", 64), None, ALU.mult, sb=[cst])
            pb_ = P.ps()
            k.mm(pb_, pb_[0:64, :], onesf, onesf[0:64, 0:64], t1, t1[0:64, :])
            k.tt(t1, t1[0:64, :], pb_, pb_[0:64, :], v, v[0:64, :], ALU.mult)
            k.tt(yc, yc[0:64, :], yc, yc[0:64, :], t1, t1[0:64, :], ALU.add)
            k.tt(osb, osb[0:64, :], yc, yc[0:64, :], gg, gg[0:64, :], ALU.mult)
            ob_, oa_ = omdst(3, blk)
            P.dma("sp", ob_, oa_, osb, osb[0:64, :], sem_buf=osb)

    from collections import deque
    pending = deque()
    fillstate["pending"] = pending
    seg_rms(0)
    for n in need:
        for f in group_steps(0, n):
            f()
    for blk in range(nblk):
        if blk + 1 < nblk:
            seg_rms(blk + 1)
            for n in need:
                pending.extend(group_steps(blk + 1, n))
        seg_mix(blk)
        while pending:
            pending.popleft()()
    P.rot = []


GDN_OFF, RET_OFF, M2_OFF, RW_OFF = 0, 2056, 3592, 5136


def core_cols(c):
    hg, half = c // 2, c % 2
    grp = c // 4
    cols = {}
    cols["gq"] = np.arange(GDN_OFF + hg * 128, GDN_OFF + hg * 128 + 128)
    cols["gk"] = np.arange(GDN_OFF + 512 + hg * 128, GDN_OFF + 512 + hg * 128 + 128)
    cols["gv"] = np.arange(GDN_OFF + 1024 + hg * 128 + half * 64, GDN_OFF + 1024 + hg * 128 + half * 64 + 64)
    cols["gb"] = np.array([GDN_OFF + 2048 + hg])
    cols["ga"] = np.array([GDN_OFF + 2052 + hg])
    cols["rq"] = np.arange(RET_OFF + hg * 64, RET_OFF + hg * 64 + 64)
    cols["rk"] = np.arange(RET_OFF + 256 + hg * 64, RET_OFF + 256 + hg * 64 + 64)
    cols["rv"] = np.arange(RET_OFF + 512 + hg * 128 + half * 64, RET_OFF + 512 + hg * 128 + half * 64 + 64)
    cols["mx"] = np.arange(M2_OFF + 512 + c * 64, M2_OFF + 512 + c * 64 + 64)
    cols["mB"] = np.arange(M2_OFF + 1024 + grp * 128, M2_OFF + 1024 + grp * 128 + 128)
    cols["mC"] = np.arange(M2_OFF + 1280 + grp * 128, M2_OFF + 1280 + grp * 128 + 128)
    cols["mdt"] = np.array([M2_OFF + 1536 + c])
    cols["wr"] = np.arange(RW_OFF + c * 64, RW_OFF + c * 64 + 64)
    cols["wk"] = np.arange(RW_OFF + 512 + c * 64, RW_OFF + 512 + c * 64 + 64)
    cols["wv"] = np.arange(RW_OFF + 1024 + c * 64, RW_OFF + 1024 + c * 64 + 64)
    cols["wl"] = np.arange(RW_OFF + 1536, RW_OFF + 1568)
    cols["al"] = np.arange(RW_OFF + 1568, RW_OFF + 1600)
    cols["gl"] = np.arange(RW_OFF + 1600, RW_OFF + 1696)
    return cols


def const_masks():
    m = np.zeros((128, NMSK), np.float32)
    o = MSK["ident"][0]
    m[:, o:o + 128] = np.eye(128, dtype=np.float32)
    i = np.arange(64)
    U = (i[:, None] <= i[None, :]).astype(np.float32)
    SU = (i[:, None] < i[None, :]).astype(np.float32)
    SL = (i[:, None] > i[None, :]).astype(np.float32)
    for nm, a in (("U8", U), ("SU8", SU), ("SL8", SL)):
        o = MSK[nm][0]
        m[0:64, o:o + NB] = np.tile(a, (1, NCH))
    o = MSK["reset"][0]
    r = np.ones(NB, np.float32)
    r[0::CH] = 0.0
    m[:, o:o + NB] = r[None, :]
    o = MSK["rotJ"][0]
    J = np.zeros((64, 64), np.float32)
    for p in range(32):
        J[2 * p + 1, 2 * p] = -1.0
        J[2 * p, 2 * p + 1] = 1.0
    m[0:64, o:o + 64] = J
    o = MSK["ones"][0]
    m[:, o:o + 128] = 1.0
    return m


def rot_table(T):
    theta = 1.0 / (10000.0 ** np.linspace(0.0, 1.0, 32, dtype=np.float32))
    ang = np.arange(T, dtype=np.float32)[None, :] * np.repeat(theta, 2)[:, None].astype(np.float32)
    return np.concatenate([np.cos(ang), np.sin(ang)], axis=1).astype(np.float32)


def prep_a(inp, l, c):
    hg, half, grp = c // 2, c % 2, c // 4
    cols = core_cols(c)
    allc = np.concatenate([cols[n] for n, _ in QROWS])
    w = inp["w_in"][l][:, allc]
    win = np.ascontiguousarray(w.reshape(KT, 128, NR).transpose(1, 0, 2).reshape(128, KT * NR))
    cst = np.zeros((128, NCST), np.float32)

    def put(name, arr, rows=None):
        o, wd = CST[name]
        arr = np.asarray(arr, np.float32)
        if arr.ndim == 0:
            cst[:, o] = arr
        elif arr.ndim == 1:
            cst[0:arr.shape[0], o] = arr
        else:
            cst[0:arr.shape[0], o:o + arr.shape[1]] = arr
    put("n1w", inp["norm1_w"][l].reshape(KT, 128).T)
    gcw = inp["gdn_conv_w"][l]
    put("gcq", gcw[:, cols["gq"] - GDN_OFF].T)
    put("gck", gcw[:, cols["gk"] - GDN_OFF].T)
    put("gcv", gcw[:, cols["gv"] - GDN_OFF].T)
    put("g_alog", inp["gdn_a_log"][l][hg])
    put("g_dtb", inp["gdn_dt_bias"][l][hg])
    put("r_lg", np.float32(np.log1p(-np.exp2(np.float32(-5.0 - hg)))))
    mcw = inp["m2_conv_w"][l]
    mcb = inp["m2_conv_b"][l]
    for nm, q in (("x", "mx"), ("B", "mB"), ("C", "mC")):
        ci = cols[q] - (M2_OFF + 512)
        put("mc" + nm, mcw[:, ci].T)
        put("mb" + nm, mcb[ci])
    put("m_alog", inp["m2_a_log"][l][c])
    put("m_dtb", inp["m2_dt_bias"][l][c])
    put("m_d", inp["m2_d"][l][c])
    mu = inp["rw_mu"][l]
    for nm, q in (("mu_r", "wr"), ("mu_k", "wk"), ("mu_v", "wv"), ("mu_wl", "wl"), ("mu_al", "al"), ("mu_gl", "gl")):
        put(nm, mu[cols[q] - RW_OFF])
    hs = slice(c * 64, (c + 1) * 64)
    put("w0", inp["rw_w0"][l][hs])
    put("a0", inp["rw_a0"][l][hs])
    put("k_k", inp["rw_k_k"][l][hs])
    put("k_a", inp["rw_k_a"][l][hs])
    put("r_k", inp["rw_r_k"][l][c])
    put("ln_w", inp["rw_ln_w"][l][hs])
    put("ln_b", inp["rw_ln_b"][l][hs])
    put("w_up", inp["rw_w_up"][l][:, hs])
    put("a_up", inp["rw_a_up"][l][:, hs])
    put("g_up", inp["rw_g_up"][l][:, hs])
    return win, cst


CSTB = {}
_o = 0
for _n, _w in [("n1w", 16), ("n2w", 16), ("fnw", 16), ("gnw", 1), ("rnw", 4), ("mnw", 4)]:
    CSTB[_n] = (_o, _w)
    _o += _w
NCSTB = _o
NGT = 12
NFF = D_FF // 128


def build_phase_b(TB, final):
    nc = bass.Bass("TRN2", target_bir_lowering=False)
    P = Prog(nc)
    k = K(P)
    hT = P.dram("hT", [D_MODEL, TB], F32, kind="ExternalInput")
    omx = P.dram("omx", [D_MODEL, TB], F32, kind="ExternalInput")
    wg = P.dram("wg", [NGT, 128, KT * 128], F32, kind="ExternalInput")
    wo = P.dram("wo", [KT, 128, KT * 128], F32, kind="ExternalInput")
    wu = P.dram("wu", [NFF, 128, KT * 128], F32, kind="ExternalInput")
    wd = P.dram("wd", [KT, 128, NFF * 128], F32, kind="ExternalInput")
    cstd = P.dram("cst", [128, NCSTB], F32, kind="ExternalInput")
    h2 = P.dram("dr_h2", [D_MODEL, TB], F32, kind="ExternalOutput")
    yo = P.dram("dr_y", [D_MODEL, TB], F32, kind="ExternalOutput") if final else None
    P.psum_banks(8)
    P.use_arena(53000)

    def omload(o, j, blk):
        P.dma("sp", o, o[:, :], omx, omx[j * 128:(j + 1) * 128, blk * NB:(blk + 1) * NB])
    emit_phase_b(P, k, TB, final, lambda kt, blk: (hT, hT[kt * 128:(kt + 1) * 128, blk * NB:(blk + 1) * NB]), omload,
                 wg, wo, wu, wd, cstd, h2, yo)
    evs = [h2.lw]
    if final:
        evs.append(yo.lw)
    P.wait_all("sp", evs)
    P.emit()
    return nc


def emit_phase_b(P, k, TB, final, hsrc, omload, wg, wo, wu, wd, cstd, h2, yo):
    nblk = TB // NB
    P.rot = []
    S = P.sbuf
    cst = S("cstsb", [128, NCSTB])
    P.dma("sp", cst, cst[:, :], cstd, cstd[:, :])

    def C(name, j0=0, j1=None):
        o, w = CSTB[name]
        if j1 is None:
            j1 = w
        return cst[:, o + j0:o + j1]
    ones_bf = S("ones_bf", [128, 128], BF16)
    k.memset(ones_bf, ones_bf[:, :], 1.0)
    onesf = S("onesf", [128, 128])
    k.memset(onesf, onesf[:, :], 1.0)
    assert nblk == 2
    hsbs = [S("hsb%d" % i, [128, KT * NB]) for i in range(2)]
    mbfs = [S("mbf%d" % i, [128, KT * NB], BF16) for i in range(2)]
    sqb = [S("sqb%d" % i, [128, NB], BF16) for i in range(2)]
    rstd = S("rstd", [128, NB])
    wsl = [S("wsl%d" % i, [128, KT * 128], BF16) for i in range(3)]
    wst = [S("wst%d" % i, [128, KT * 128]) for i in range(2)]
    osb = [S("osb%d" % i, [128, NB]) for i in range(2)]
    mark = P.aoff
    abf = S("abf", [128, KT * NB], BF16)
    obf = S("obf", [128, KT * NB], BF16)
    omt = [S("omt%d" % i, [128, NB]) for i in range(3)]
    gt = [S("gt%d" % i, [128, NB]) for i in range(3)]
    t1 = S("t1", [128, NB])
    t2 = S("t2", [128, NB])
    HF = NFF // 2
    cnt = {"w": 0, "wd": 0, "om": 0, "g": 0, "o": 0}

    def rsqrt_to(ob, oap, ib, iap, eps, scale=1.0):
        k.act(ob, oap, ib, iap, AF.Ln, bias=eps, scale=scale)
        k.act(ob, oap, ob, oap, AF.Exp, scale=-0.5)

    def rmsnorm(dst, wname, hsb):
        pss = P.ps()
        for kt in range(KT):
            sq = sqb[kt % 2]
            k.act(sq, sq[:, :], hsb, hsb[:, kt * NB:(kt + 1) * NB], AF.Square)
            k.mm(pss, pss[:, :], ones_bf, ones_bf[:, :], sq, sq[:, :], start=(kt == 0), stop=(kt == KT - 1))
        rsqrt_to(rstd, rstd[:, :], pss, pss[:, :], EPS, scale=1.0 / D_MODEL)
        for kt in range(KT):
            k.stt(dst, dst[:, kt * NB:(kt + 1) * NB], hsb, hsb[:, kt * NB:(kt + 1) * NB],
                  C(wname, kt, kt + 1), rstd, rstd[:, :], ALU.mult, ALU.mult, sb=[cst])

    def load_w(src, j, piece=None):
        w = wsl[cnt["w"] % 3]
        st = wst[cnt["w"] % 2]
        cnt["w"] += 1
        if piece is None:
            P.dma("sp", st, st[:, :], src, src[j])
        else:
            P.dma("sp", st, st[:, :], src, src[j][:, piece * KT * 128:(piece + 1) * KT * 128])
        k.copy(w, w[:, :], st, st[:, :], eng=("pool", "dve", "act", "dve")[cnt["w"] % 4])
        return w

    def proj(w, rhs):
        pb = P.ps()
        for kt in range(KT):
            k.mm(pb, pb[:, :], w, w[:, kt * 128:(kt + 1) * 128], rhs, rhs[:, kt * NB:(kt + 1) * NB],
                 start=(kt == 0), stop=(kt == KT - 1))
        return pb

    def load_om(j, t0):
        o = omt[cnt["om"] % 3]
        cnt["om"] += 1
        omload(o, j, t0 // NB)
        return o

    def gate(j):
        w = load_w(wg, j)
        pb = proj(w, abf)
        g = gt[cnt["g"] % 3]
        cnt["g"] += 1
        k.act(g, g[:, :], pb, pb[:, :], AF.Silu)
        return g

    for blk in range(nblk):
        t0 = blk * NB
        hsb = hsbs[blk]
        for kt in range(KT):
            hb_, ha_ = hsrc(kt, blk)
            P.dma("sp", hsb, hsb[:, kt * NB:(kt + 1) * NB], hb_, ha_)
        rmsnorm(abf, "n1w", hsb)
        for j in range(4):
            o = load_om(j, t0)
            g = gate(j)
            k.tt(t1, t1[:, :], o, o[:, :], o, o[:, :], ALU.mult)
            pb = P.ps()
            k.mm(pb, pb[:, :], onesf, onesf[:, :], t1, t1[:, :])
            rsqrt_to(t1, t1[:, :], pb, pb[:, :], EPS, scale=1.0 / 128)
            k.stt(t2, t2[:, :], o, o[:, :], C("gnw"), t1, t1[:, :], ALU.mult, ALU.mult, sb=[cst])
            k.tt(obf, obf[:, j * NB:(j + 1) * NB], t2, t2[:, :], g, g[:, :], ALU.mult)
        for j in range(4):
            o = load_om(4 + j, t0)
            g = gate(4 + j)
            pb = P.ps()
            k.mm(pb, pb[:, :], onesf, onesf[:, :], o, o[:, :])
            k.stt(t2, t2[:, :], pb, pb[:, :], -1.0 / 128, o, o[:, :], ALU.mult, ALU.add)
            k.tt(t1, t1[:, :], t2, t2[:, :], t2, t2[:, :], ALU.mult)
            pb2 = P.ps()
            k.mm(pb2, pb2[:, :], onesf, onesf[:, :], t1, t1[:, :])
            rsqrt_to(t1, t1[:, :], pb2, pb2[:, :], EPS, scale=1.0 / 128)
            k.stt(t2, t2[:, :], t2, t2[:, :], C("rnw", j, j + 1), t1, t1[:, :], ALU.mult, ALU.mult, sb=[cst])
            k.tt(obf, obf[:, (4 + j) * NB:(5 + j) * NB], t2, t2[:, :], g, g[:, :], ALU.mult)
        for gi in range(2):
            ys = []
            pb = P.ps()
            for jj in range(2):
                j = gi * 2 + jj
                o = load_om(8 + j, t0)
                g = gate(8 + j)
                y = osb[jj]
                k.tt(y, y[:, :], o, o[:, :], g, g[:, :], ALU.mult)
                k.tt(t1, t1[:, :], y, y[:, :], y, y[:, :], ALU.mult)
                k.mm(pb, pb[:, :], onesf, onesf[:, :], t1, t1[:, :], start=(jj == 0), stop=(jj == 1))
                ys.append(y)
            rsqrt_to(t1, t1[:, :], pb, pb[:, :], EPS, scale=1.0 / 256)
            for jj in range(2):
                j = gi * 2 + jj
                k.stt(obf, obf[:, (8 + j) * NB:(9 + j) * NB], ys[jj], ys[jj][:, :], C("mnw", j, j + 1),
                      t1, t1[:, :], ALU.mult, ALU.mult, sb=[cst])
        for j in range(4):
            o = load_om(12 + j, t0)
            k.copy(obf, obf[:, (12 + j) * NB:(13 + j) * NB], o, o[:, :])
        for j in range(KT):
            w = load_w(wo, j)
            pb = proj(w, obf)
            k.tt(hsb, hsb[:, j * NB:(j + 1) * NB], hsb, hsb[:, j * NB:(j + 1) * NB], pb, pb[:, :], ALU.add)
        rmsnorm(mbfs[blk], "n2w", hsb)
    P.barrier = [((e, P.gen[e]), P.cnt[e]) for e in ENGS if P.cnt[e] > 0] + list(P.dsems.items())
    P.aoff = mark
    ubf = S("ubf", [128, HF * 2 * NB], BF16)
    for half in range(2):
        for jj in range(HF):
            j = half * HF + jj
            w = load_w(wu, j)
            for bb in range(2):
                pb = proj(w, mbfs[bb])
                usl = ubf[:, (jj * 2 + bb) * NB:(jj * 2 + bb + 1) * NB]
                k.act(ubf, usl, pb, pb[:, :], AF.Relu)
                k.tt(ubf, usl, ubf, usl, ubf, usl, ALU.mult, eng="dve")
        for j in range(KT):
            pbs = [P.ps(), P.ps()]
            for pc in range(HF // KT):
                piece = half * (HF // KT) + pc
                w = load_w(wd, j, piece)
                for bb in range(2):
                    for k2_ in range(KT):
                        kl = pc * KT + k2_
                        k.mm(pbs[bb], pbs[bb][:, :], w, w[:, k2_ * 128:(k2_ + 1) * 128],
                             ubf, ubf[:, (kl * 2 + bb) * NB:(kl * 2 + bb + 1) * NB],
                             start=(kl == 0), stop=(kl == HF - 1))
            for bb in range(2):
                hb_ = hsbs[bb]
                k.tt(hb_, hb_[:, j * NB:(j + 1) * NB], hb_, hb_[:, j * NB:(j + 1) * NB], pbs[bb], pbs[bb][:, :], ALU.add)
    for blk in range(nblk):
        t0 = blk * NB
        hsb = hsbs[blk]
        if h2 is not None:
            for kt in range(KT):
                P.dma("sp", h2, h2[kt * 128:(kt + 1) * 128, t0:t0 + NB], hsb, hsb[:, kt * NB:(kt + 1) * NB], sem_buf=osb[0])
        if final:
            pss = P.ps()
            for kt in range(KT):
                sq = sqb[kt % 2]
                k.act(sq, sq[:, :], hsb, hsb[:, kt * NB:(kt + 1) * NB], AF.Square)
                k.mm(pss, pss[:, :], ones_bf, ones_bf[:, :], sq, sq[:, :], start=(kt == 0), stop=(kt == KT - 1))
            rsqrt_to(rstd, rstd[:, :], pss, pss[:, :], EPS, scale=1.0 / D_MODEL)
            for kt in range(KT):
                y = osb[kt % 2]
                k.stt(y, y[:, :], hsb, hsb[:, kt * NB:(kt + 1) * NB], C("fnw", kt, kt + 1), rstd, rstd[:, :],
                      ALU.mult, ALU.mult, sb=[cst])
                P.dma("sp", yo, yo[kt * 128:(kt + 1) * 128, t0:t0 + NB], y, y[:, :], sem_buf=osb[1])
    return


def prep_b_weights(inp, l):
    def slabs(W, ntile):
        Kd, Md = W.shape
        return np.ascontiguousarray(W.reshape(Kd // 128, 128, Md // 128, 128).transpose(2, 1, 0, 3).reshape(Md // 128, 128, (Kd // 128) * 128))
    w_in = inp["w_in"][l]
    gcols = np.concatenate([np.arange(GDN_OFF + 1536, GDN_OFF + 2048), np.arange(RET_OFF + 1024, RET_OFF + 1536),
                            np.arange(M2_OFF, M2_OFF + 512)])
    wg = slabs(w_in[:, gcols], NGT)
    wo = slabs(inp["w_out"][l], KT)
    wu = slabs(inp["w_ffn_up"][l], NFF)
    wd = slabs(inp["w_ffn_down"][l], KT)
    cst = np.zeros((128, NCSTB), np.float32)
    cst[:, 0:16] = inp["norm1_w"][l].reshape(KT, 128).T
    cst[:, 16:32] = inp["norm2_w"][l].reshape(KT, 128).T
    cst[:, 32:48] = inp["final_norm_w"].reshape(KT, 128).T
    cst[:, 48] = inp["gdn_norm_w"][l]
    cst[:, 49:53] = inp["ret_norm_w"][l].reshape(4, 128).T
    cst[:, 53:57] = inp["m2_norm_w"][l].reshape(4, 128).T
    return {"wg": wg, "wo": wo, "wu": wu, "wd": wd, "cst": cst}


_CACHE = {}
TBK = SEQ // NCORE
OMROWS = (SEQ // NB) * 256


def build_fused(nl=2):
    T, TB = SEQ, TBK
    nc = bass.Bass("TRN2", target_bir_lowering=False)
    P = Prog(nc)
    k = K(P)
    xT = P.dram("xT", [D_MODEL, T], F32, kind="ExternalInput")
    xown = P.dram("xown", [D_MODEL, TB], F32, kind="ExternalInput")
    mskd = P.dram("msk", [128, NMSK], F32, kind="ExternalInput")
    rotd = P.dram("rot", [64, 2 * T], F32, kind="ExternalInput")
    idxd = P.dram("idx", [128, 32], mybir.dt.uint32, kind="ExternalInput")
    L = {}
    for l in range(2):
        L[l] = dict(
            win=P.dram("win%d" % l, [128, KT * NR], F32, kind="ExternalInput"),
            cst=P.dram("cst%d" % l, [128, NCST], F32, kind="ExternalInput"),
            wg=P.dram("wg%d" % l, [NGT, 128, KT * 128], F32, kind="ExternalInput"),
            wo=P.dram("wo%d" % l, [KT, 128, KT * 128], F32, kind="ExternalInput"),
            wu=P.dram("wu%d" % l, [NFF, 128, KT * 128], F32, kind="ExternalInput"),
            wd=P.dram("wd%d" % l, [KT, 128, NFF * 128], F32, kind="ExternalInput"),
            cstb=P.dram("cstb%d" % l, [128, NCSTB], F32, kind="ExternalInput"),
            om_loc=P.dram("om_loc%d" % l, [OMROWS, NB], F32),
            om_all=P.dram("om_all%d" % l, [NCORE * OMROWS, NB], F32),
        )
    h_loc = P.dram("h_loc", [D_MODEL, TB], F32)
    h_all = P.dram("h_all", [NCORE * D_MODEL, TB], F32)
    yo = P.dram("dr_y", [D_MODEL, TB], F32, kind="ExternalOutput")
    P.psum_banks(8)
    idx = Buf("idxsb", P.es.enter_context(nc.sbuf_tensor("idxsb", [128, 32], mybir.dt.uint32)))
    P.use_arena(53000)
    P.dma("sp", idx, idx[:, :], idxd, idxd[:, :])

    for l in range(nl):
        d = L[l]
        if l == 0:
            hsrc = lambda kt, blk: (xT, xT[kt * 128:(kt + 1) * 128, blk * NB:(blk + 1) * NB])
        else:
            def hsrc(kt, blk):
                r, b = blk // 2, blk % 2
                return (h_all, h_all[r * D_MODEL + kt * 128:r * D_MODEL + (kt + 1) * 128, b * NB:(b + 1) * NB])
        om_loc, om_all = d["om_loc"], d["om_all"]
        P.phase_reset()
        emit_phase_a(P, k, T, hsrc, d["win"], d["cst"], mskd, rotd,
                     lambda j, blk, om_loc=om_loc: (om_loc, om_loc[blk * 256 + j * 64:blk * 256 + (j + 1) * 64, :]))
        P.allgather(om_loc, om_all, NCORE)
        P.phase_reset()

        def omload(o, j, blk, om_all=om_all, idx=idx):
            P.gather(o, o[:, :], om_all, om_all[:, :], idx, idx[:, j * 2 + blk:j * 2 + blk + 1])
        if l == 0:
            hown = lambda kt, blk: (xown, xown[kt * 128:(kt + 1) * 128, blk * NB:(blk + 1) * NB])
        else:
            hown = lambda kt, blk: (h_loc, h_loc[kt * 128:(kt + 1) * 128, blk * NB:(blk + 1) * NB])
        final = (l == nl - 1)
        emit_phase_b(P, k, TB, final, hown, omload, d["wg"], d["wo"], d["wu"], d["wd"], d["cstb"],
                     None if final else h_loc, yo if final else None)
        if not final:
            P.allgather(h_loc, h_all, NCORE)
    P.wait_all("sp", [yo.lw])
    P.emit()
    return nc


def idx_table(c):
    idx = np.zeros((128, 32), np.uint32)
    p = np.arange(128)
    for j in range(16):
        slot, jj = j // 4, j % 4
        rank = np.where(p < 64, 2 * jj, 2 * jj + 1)
        for b in range(2):
            idx[:, j * 2 + b] = rank * OMROWS + (c * 2 + b) * 256 + slot * 64 + (p % 64)
    return idx


def kernel(**inp):
    inp = {k_: np.asarray(v) for k_, v in inp.items()}
    T, TB = SEQ, TBK
    msk = const_masks()
    rot = rot_table(T)
    xT = np.ascontiguousarray(inp["x"][0].T)
    cores = list(range(NCORE))
    if "fused" not in _CACHE:
        _CACHE["fused"] = build_fused()
    nc = _CACHE["fused"]
    wb = [prep_b_weights(inp, l) for l in range(2)]
    maps = []
    for c in cores:
        m = {"xT": xT, "xown": np.ascontiguousarray(xT[:, c * TB:(c + 1) * TB]), "msk": msk, "rot": rot,
             "idx": idx_table(c)}
        for l in range(2):
            win, cst = prep_a(inp, l, c)
            m["win%d" % l] = win
            m["cst%d" % l] = cst
            m["wg%d" % l] = wb[l]["wg"]
            m["wo%d" % l] = wb[l]["wo"]
            m["wu%d" % l] = wb[l]["wu"]
            m["wd%d" % l] = wb[l]["wd"]
            m["cstb%d" % l] = wb[l]["cst"]
        maps.append(m)
    res = run_bass_kernel_spmd(nc, maps, core_ids=cores)
    out = np.concatenate([res.results[c]["dr_y"] for c in cores], axis=1)
    return np.ascontiguousarray(out.T)[None].astype(np.float32)
```
